# Optimizing a Trainium2 kernel written in Bass

```python
import math
import jax, jax.numpy as jnp
from jax import lax
import numpy as np

D_MODEL = 1024
BATCH = 8
SEQ = 2048
DEPTH = 2

N_MIXERS = 2
N_A = (DEPTH + 1) // 2
N_B = DEPTH // 2
N_SUB = 3
D_FF = 2816
FFN_RES = 0.5
EPS = 1e-6

MLA_HEADS = 16
Q_LORA = 384
KV_LORA = 256
QK_NOPE = 64
QK_ROPE = 32
V_HEAD = 64
ROPE_THETA = 10000.0
Q_BLOCK = 128

DIL_GROUPS = ((128, 1), (512, 4), (2048, 16))
N_GROUPS = 3
DIL_HEADS = 16
DIL_HEAD_DIM = 64
DIL_BLOCK = 128
DIL_WIDTH = DIL_HEADS * DIL_HEAD_DIM

N_BUCKETS = 32
MAX_DISTANCE = 2048

kernel_name = "hybrid_mla_dilated_macaron"


def rmsnorm(x, g):
    xf = x.astype(jnp.float32)
    y = xf * lax.rsqrt(jnp.mean(xf * xf, axis=-1, keepdims=True) + EPS)
    return (y * g.astype(jnp.float32)).astype(x.dtype)


def swiglu(h, w_gate, w_up, w_down):
    return (jax.nn.silu(h @ w_gate) * (h @ w_up)) @ w_down


def rope(x, pos):
    half = x.shape[-1] // 2
    freqs = ROPE_THETA ** (-jnp.arange(half, dtype=jnp.float32) / half)
    ang = pos[:, None] * freqs[None, :]
    cos = jnp.cos(ang)[None, :, None, :]
    sin = jnp.sin(ang)[None, :, None, :]
    x1 = x[..., :half].astype(jnp.float32)
    x2 = x[..., half:].astype(jnp.float32)
    return jnp.concatenate([x1 * cos - x2 * sin, x1 * sin + x2 * cos], axis=-1).astype(x.dtype)


def causal_block_attention(q, k, v, scale):
    B, S, H, dq = q.shape
    nb = S // Q_BLOCK
    qb = q.reshape(B, nb, Q_BLOCK, H, dq).transpose(1, 0, 3, 2, 4)
    kt = k.transpose(0, 2, 1, 3)
    vt = v.transpose(0, 2, 1, 3)
    kpos = jnp.arange(S)

    def one_block(args):
        qi, n = args
        s = jnp.einsum('bhqd,bhkd->bhqk', qi, kt).astype(jnp.float32) * scale
        qpos = n * Q_BLOCK + jnp.arange(Q_BLOCK)
        s = jnp.where(kpos[None, :] <= qpos[:, None], s, -jnp.inf)
        p = jax.nn.softmax(s, axis=-1).astype(vt.dtype)
        return jnp.einsum('bhqk,bhkd->bhqd', p, vt)

    out = lax.map(one_block, (qb, jnp.arange(nb)))
    return out.transpose(1, 0, 3, 2, 4).reshape(B, S, H, v.shape[-1])


def mla(h, w_in, q_norm, w_q_up, kv_norm, w_kv_up, w_o):
    B, S, _ = h.shape
    lat = h @ w_in
    cq = lat[..., :Q_LORA]
    ckv = lat[..., Q_LORA:Q_LORA + KV_LORA]
    k_rope = lat[..., Q_LORA + KV_LORA:][:, :, None, :]
    q = (rmsnorm(cq, q_norm) @ w_q_up).reshape(B, S, MLA_HEADS, QK_NOPE + QK_ROPE)
    kv = (rmsnorm(ckv, kv_norm) @ w_kv_up).reshape(B, S, MLA_HEADS, QK_NOPE + V_HEAD)
    pos = jnp.arange(S, dtype=jnp.float32)
    q = jnp.concatenate([q[..., :QK_NOPE], rope(q[..., QK_NOPE:], pos)], axis=-1)
    k_rope = jnp.broadcast_to(rope(k_rope, pos), (B, S, MLA_HEADS, QK_ROPE))
    k = jnp.concatenate([kv[..., :QK_NOPE], k_rope.astype(kv.dtype)], axis=-1)
    v = kv[..., QK_NOPE:]
    o = causal_block_attention(q, k, v, (QK_NOPE + QK_ROPE) ** -0.5)
    return o.reshape(B, S, MLA_HEADS * V_HEAD) @ w_o


def t5_bucket(dist):
    max_exact = N_BUCKETS // 2
    d = jnp.maximum(dist, 1).astype(jnp.float32)
    large = max_exact + (jnp.log(d / max_exact) / math.log(MAX_DISTANCE / max_exact)
                         * (N_BUCKETS - max_exact)).astype(jnp.int32)
    large = jnp.minimum(large, N_BUCKETS - 1)
    return jnp.where(dist < max_exact, dist, large)


def strided_window_attention(q, k, v, dilation, span, bias_table):
    B, S, H, E = q.shape
    L = S // dilation
    nb = -(-L // DIL_BLOCK)
    Lp = nb * DIL_BLOCK
    qs = q.reshape(B, L, dilation, H, E)
    qb = jnp.pad(qs, ((0, 0), (0, Lp - L), (0, 0), (0, 0), (0, 0))).reshape(B, nb, DIL_BLOCK, dilation, H, E)

    def windows(t):
        tp = jnp.pad(t.reshape(B, L, dilation, H, E),
                     ((0, 0), (DIL_BLOCK, Lp - L), (0, 0), (0, 0), (0, 0)))
        tp = tp.reshape(B, nb + 1, DIL_BLOCK, dilation, H, E)
        return jnp.concatenate([tp[:, :-1], tp[:, 1:]], axis=2)

    kw, vw = windows(k), windows(v)
    s = jnp.einsum('bnqrhe,bnkrhe->bnrhqk', qb, kw).astype(jnp.float32) * (E ** -0.5)
    iq = jnp.arange(DIL_BLOCK)[:, None]
    ik = jnp.arange(2 * DIL_BLOCK)[None, :]
    rel = DIL_BLOCK + iq - ik
    in_window = (rel >= 0) & (rel <= span)
    bucket = t5_bucket(jnp.maximum(rel, 0) * dilation)
    bias = jnp.transpose(bias_table[bucket], (2, 0, 1)).astype(jnp.float32)
    key_m = (jnp.arange(nb)[:, None] - 1) * DIL_BLOCK + jnp.arange(2 * DIL_BLOCK)[None, :]
    valid = in_window[None] & (key_m >= 0)[:, None, :]
    logits = jnp.where(valid[None, :, None, None], s + bias, -jnp.inf)
    lse = jax.nn.logsumexp(logits, axis=-1)
    p = jnp.exp(logits - lse[..., None]).astype(v.dtype)
    o = jnp.einsum('bnrhqk,bnkrhe->bnqrhe', p, vw)
    o = o.reshape(B, Lp, dilation, H, E)[:, :L].reshape(B, S, H, E)
    lse = jnp.transpose(lse, (0, 1, 4, 2, 3)).reshape(B, Lp, dilation, H)[:, :L].reshape(B, S, H)
    return o, lse


def dilated_attention(h, w_in, w_o, rel_bias):
    B, S, _ = h.shape
    proj = (h @ w_in).reshape(B, S, N_GROUPS, 3, DIL_HEADS, DIL_HEAD_DIM)
    outs, lses = [], []
    for g, (window, dilation) in enumerate(DIL_GROUPS):
        o, lse = strided_window_attention(
            proj[:, :, g, 0], proj[:, :, g, 1], proj[:, :, g, 2],
            dilation, window // dilation, rel_bias[:, g * DIL_HEADS:(g + 1) * DIL_HEADS])
        outs.append(o)
        lses.append(lse)
    alpha = jax.nn.softmax(jnp.stack(lses, axis=0), axis=0)
    o = jnp.sum(alpha[..., None] * jnp.stack(outs, axis=0).astype(jnp.float32), axis=0)
    return o.astype(h.dtype).reshape(B, S, DIL_WIDTH) @ w_o


def sandwich(x, fn, pre_g, post_g, shift, scale, gate, res_w):
    hn = rmsnorm(x, pre_g) * (1 + scale[:, None, :]) + shift[:, None, :]
    y = rmsnorm(fn(hn), post_g)
    return x + res_w * gate[:, None, :] * y


def setup_inputs(seed: int = 0) -> dict:
    key = jax.random.key(seed)
    ks = jax.random.split(key, 20)
    D = D_MODEL
    nrm = lambda k, shape, fan: jax.random.normal(k, shape, jnp.float32) * fan ** -0.5
    return {
        "x": jax.random.normal(ks[0], (BATCH, SEQ, D), jnp.float32),
        "c": jax.random.normal(ks[1], (BATCH, D), jnp.float32),
        "norm_pre": 1.0 + 0.05 * jax.random.normal(ks[2], (DEPTH, N_SUB, D), jnp.float32),
        "norm_post": 1.0 + 0.05 * jax.random.normal(ks[3], (DEPTH, N_SUB, D), jnp.float32),
        "w_mod": nrm(ks[4], (DEPTH, D, N_SUB * 3 * D), D) * 0.5,
        "b_mod": 0.02 * jax.random.normal(ks[5], (DEPTH, N_SUB * 3 * D), jnp.float32),
        "ffn_w_gate": nrm(ks[6], (DEPTH, 2, D, D_FF), D),
        "ffn_w_up": nrm(ks[7], (DEPTH, 2, D, D_FF), D),
        "ffn_w_down": nrm(ks[8], (DEPTH, 2, D_FF, D), D_FF),
        "mla_w_in": nrm(ks[9], (N_A, D, Q_LORA + KV_LORA + QK_ROPE), D),
        "mla_q_norm": 1.0 + 0.05 * jax.random.normal(ks[10], (N_A, Q_LORA), jnp.float32),
        "mla_w_q_up": nrm(ks[11], (N_A, Q_LORA, MLA_HEADS * (QK_NOPE + QK_ROPE)), Q_LORA),
        "mla_kv_norm": 1.0 + 0.05 * jax.random.normal(ks[12], (N_A, KV_LORA), jnp.float32),
        "mla_w_kv_up": nrm(ks[13], (N_A, KV_LORA, MLA_HEADS * (QK_NOPE + V_HEAD)), KV_LORA),
        "mla_w_o": nrm(ks[14], (N_A, MLA_HEADS * V_HEAD, D), MLA_HEADS * V_HEAD),
        "dil_w_in": nrm(ks[15], (N_B, D, N_GROUPS * 3 * DIL_WIDTH), D),
        "dil_w_o": nrm(ks[16], (N_B, DIL_WIDTH, D), DIL_WIDTH),
        "rel_bias": 0.5 * jax.random.normal(ks[17], (N_BUCKETS, N_GROUPS * DIL_HEADS), jnp.float32),
    }


def reference(x, c, norm_pre, norm_post, w_mod, b_mod, ffn_w_gate, ffn_w_up, ffn_w_down,
              mla_w_in, mla_q_norm, mla_w_q_up, mla_kv_norm, mla_w_kv_up, mla_w_o,
              dil_w_in, dil_w_o, rel_bias):
    B = x.shape[0]
    for i in range(DEPTH):
        mod = (jax.nn.silu(c) @ w_mod[i] + b_mod[i]).reshape(B, N_SUB, 3, D_MODEL)

        def ffn_first(h, i=i):
            return swiglu(h, ffn_w_gate[i, 0], ffn_w_up[i, 0], ffn_w_down[i, 0])

        def ffn_second(h, i=i):
            return swiglu(h, ffn_w_gate[i, 1], ffn_w_up[i, 1], ffn_w_down[i, 1])

        if i % N_MIXERS == 0:
            a = i // N_MIXERS
            def mixer(h, a=a):
                return mla(h, mla_w_in[a], mla_q_norm[a], mla_w_q_up[a],
                           mla_kv_norm[a], mla_w_kv_up[a], mla_w_o[a])
        else:
            b = i // N_MIXERS
            def mixer(h, b=b):
                return dilated_attention(h, dil_w_in[b], dil_w_o[b], rel_bias)

        x = sandwich(x, ffn_first, norm_pre[i, 0], norm_post[i, 0],
                     mod[:, 0, 0], mod[:, 0, 1], mod[:, 0, 2], FFN_RES)
        x = sandwich(x, mixer, norm_pre[i, 1], norm_post[i, 1],
                     mod[:, 1, 0], mod[:, 1, 1], mod[:, 1, 2], 1.0)
        x = sandwich(x, ffn_second, norm_pre[i, 2], norm_post[i, 2],
                     mod[:, 2, 0], mod[:, 2, 1], mod[:, 2, 2], FFN_RES)
    return x
```

```python
import math
from contextlib import ExitStack

import numpy as np
import ml_dtypes

import concourse.bass as bass
import concourse.mybir as mybir
from concourse.bass_utils import run_bass_kernel_spmd

F32 = mybir.dt.float32
BF16 = mybir.dt.bfloat16
AF = mybir.ActivationFunctionType
ALU = mybir.AluOpType

ENGS = ("pe", "act", "dve", "pool", "sp")

D = 1024
SQ = 2048
NT = 16
DFF = 2816
NFC = 22
EPS = 1e-6
N_CORES = 8


class DSem:
    def __init__(self, handle, name):
        self.h = handle
        self.name = name
        self.count = 0


class Op:
    __slots__ = ("eng", "fn", "waits", "inc", "cs", "pos", "dsem")


class Sched:
    def __init__(self):
        self.streams = {e: [] for e in ENGS}
        self.ncomp = {e: 0 for e in ENGS}
        self.last_w = {}
        self.readers = {}
        self.waited = {e: {} for e in ENGS}
        self.comp_ops = {e: [] for e in ENGS}
        self.dsems = []
        self.pending = {e: None for e in ENGS}

    def _deps(self, eng, reads, writes):
        need = {}

        def add(d):
            if d is None:
                return
            cs, pos = d
            if need.get(cs, 0) < pos:
                need[cs] = pos

        if self.pending[eng] is not None:
            for cs, pos in self.pending[eng].items():
                add((cs, pos))
            self.pending[eng] = None
        for t in reads:
            add(self.last_w.get(t))
        for t in writes:
            add(self.last_w.get(t))
            for r in self.readers.get(t, ()):
                add(r)
        waits = []
        wd = self.waited[eng]
        for cs, pos in need.items():
            if cs == "pe" and eng == "pe":
                continue
            if wd.get(cs, 0) >= pos:
                continue
            wd[cs] = pos
            waits.append((cs, pos))
            if isinstance(cs, str):
                self.comp_ops[cs][pos - 1].inc = True
        return waits

    def _commit(self, me, reads, writes):
        for t in reads:
            self.readers.setdefault(t, []).append(me)
        for t in writes:
            self.last_w[t] = me
            self.readers[t] = []

    def op(self, eng, fn, reads=(), writes=()):
        o = Op()
        o.eng = eng
        o.fn = fn
        o.waits = self._deps(eng, reads, writes)
        o.inc = False
        o.dsem = None
        self.ncomp[eng] += 1
        o.cs = eng
        o.pos = self.ncomp[eng]
        self.comp_ops[eng].append(o)
        self.streams[eng].append(o)
        self._commit((o.cs, o.pos), reads, writes)
        return o

    def dma(self, eng, dsem, fn, reads=(), writes=()):
        o = Op()
        o.eng = eng
        o.fn = fn
        o.waits = self._deps(eng, reads, writes)
        if dsem.count > 0 and self.waited[eng].get(dsem, 0) < dsem.count:
            self.waited[eng][dsem] = dsem.count
            o.waits.append((dsem, dsem.count))
        o.inc = True
        o.dsem = dsem
        dsem.count += 1
        o.cs = dsem
        o.pos = dsem.count
        self.streams[eng].append(o)
        self._commit((o.cs, o.pos), reads, writes)
        return o

    def barrier(self):
        front = {}
        for e in ENGS:
            if self.ncomp[e] > 0:
                front[e] = self.ncomp[e]
        for d in self.dsems:
            if d.count > 0:
                front[d] = d.count
        for e in ENGS:
            p = dict(front)
            if self.pending[e] is not None:
                for k, v in self.pending[e].items():
                    if p.get(k, 0) < v:
                        p[k] = v
            self.pending[e] = p

    def emit(self, block, sems, final_dsems):
        val = {}
        for e in ENGS:
            c = 0
            for o in self.comp_ops[e]:
                if o.inc:
                    c += 1
                    val[(e, o.pos)] = c

        def wait(engobj, cs, pos):
            if isinstance(cs, str):
                engobj.wait_ge(sems[cs], val[(cs, pos)])
            else:
                engobj.wait_ge(cs.h, 16 * pos)

        def run(e, engobj):
            for o in self.streams[e]:
                for cs, pos in o.waits:
                    wait(engobj, cs, pos)
                ins = o.fn(engobj)
                if o.dsem is not None:
                    ins.then_inc(o.dsem.h, 16)
                elif o.inc:
                    ins.then_inc(sems[e], 1)
            if e == "sp":
                for d in final_dsems:
                    engobj.wait_ge(d.h, 16 * d.count)

        @block.tensor
        def _(eng):
            run("pe", eng)

        @block.scalar
        def _(eng):
            run("act", eng)

        @block.vector
        def _(eng):
            run("dve", eng)

        @block.gpsimd
        def _(eng):
            run("pool", eng)

        @block.sync
        def _(eng):
            run("sp", eng)


class Prog:
    def __init__(self):
        self.nc = bass.Bass("TRN2", target_bir_lowering=False)
        self.S = Sched()
        self.inputs = {}
        self.es = ExitStack()
        self.nds = 0
        self.ds_pool = {}
        self.ds_taken = None

    def din(self, name, shape, dtype=F32):
        if name not in self.inputs:
            self.inputs[name] = self.nc.dram_tensor(name, list(shape), dtype, kind="ExternalInput").ap()
        return self.inputs[name]

    def sb(self, es, name, shape, dtype):
        self.nsb = getattr(self, "nsb", 0) + 1
        return es.enter_context(self.nc.sbuf_tensor(f"{name}_{self.nsb}", list(shape), dtype))

    def dsem(self, name, q="pool"):
        pool = self.ds_pool.setdefault(q, [])
        if pool:
            d = pool.pop()
        else:
            self.nds += 1
            d = DSem(self.es.enter_context(self.nc.semaphore(f"dsem{self.nds}")), name)
            d.q = q
            self.S.dsems.append(d)
        if self.ds_taken is not None:
            self.ds_taken.append(d)
        return d

    def phase_begin(self):
        self.ds_taken = []

    def phase_end(self):
        self.S.barrier()
        for d in self.ds_taken:
            self.ds_pool[d.q].append(d)
        self.ds_taken = None

    def mm(self, out, lhsT, rhs, start, stop, r, w):
        self.S.op("pe", lambda e: e.matmul(out=out, lhsT=lhsT, rhs=rhs, start=start, stop=stop), r, w)

    def tr(self, out, in_, r, w):
        ident = self.ident[:]
        self.S.op("pe", lambda e: e.transpose(out=out, in_=in_, identity=ident), r, w)

    def act(self, out, in_, func, r, w, bias=None, scale=None, accum=None, eng="act"):
        kw = {}
        if bias is not None:
            kw["bias"] = bias
        if scale is not None:
            kw["scale"] = scale
        if accum is not None:
            kw["accum_out"] = accum
        self.S.op("act", lambda e: e.activation(out=out, in_=in_, func=func, **kw), r, w)

    def ts(self, eng, out, in0, s1, s2, op0, op1, r, w):
        if s2 is None:
            self.S.op(eng, lambda e: e.tensor_scalar(out=out, in0=in0, scalar1=s1, scalar2=None, op0=op0), r, w)
        else:
            self.S.op(eng, lambda e: e.tensor_scalar(out=out, in0=in0, scalar1=s1, scalar2=s2, op0=op0, op1=op1),
                      r, w)

    def tt(self, eng, out, in0, in1, op, r, w):
        self.S.op(eng, lambda e: e.tensor_tensor(out=out, in0=in0, in1=in1, op=op), r, w)

    def stt(self, eng, out, in0, scalar, in1, op0, op1, r, w):
        self.S.op(eng, lambda e: e.scalar_tensor_tensor(out=out, in0=in0, scalar=scalar, in1=in1, op0=op0, op1=op1),
                  r, w)

    def cp(self, eng, out, in_, r, w):
        self.S.op(eng, lambda e: e.tensor_copy(out=out, in_=in_), r, w)

    def recip(self, out, in_, r, w):
        self.S.op("dve", lambda e: e.reciprocal(out=out, in_=in_), r, w)

    def memset(self, eng, ap, v, w):
        self.S.op(eng, lambda e: e.memset(ap, v), (), w)

    def dma(self, eng, dsem, out, in_, r, w):
        assert dsem.q == eng, (dsem.name, dsem.q, eng)
        return self.S.dma(eng, dsem, lambda e: e.dma_start(out=out, in_=in_), r, w)

    def build(self, sublayers, final_store=True, dbg=None):
        nc = self.nc
        es = self.es
        P = self
        self.dbg = dbg
        with es:
            x_d = self.din("x", [SQ, D])
            cT_d = self.din("cT", [128, 8])
            id_d = self.din("ident", [128, 128], BF16)
            self.y_d = nc.dram_tensor("y", [SQ, D], F32, kind="ExternalOutput").ap()

            self.x_sb = self.sb(es, "x_sb", [128, NT, D], F32)
            self.ident = self.sb(es, "ident_sb", [128, 128], BF16)
            self.cs = self.sb(es, "cs", [128, 8], F32)
            self.cs_bf = self.sb(es, "cs_bf", [128, 8], BF16)
            self.cs_bc = self.sb(es, "cs_bc", [128, 8, 128], BF16)
            self.A_fm = self.sb(es, "A_fm", [128, 8], F32)
            self.B_fm = self.sb(es, "B_fm", [128, 8], F32)
            self.C = self.sb(es, "C_bc", [128, D], F32)
            self.epsb = self.sb(es, "epsb", [128, 1], F32)
            self.junk = self.sb(es, "junk", [128, 2, D], BF16)
            self.junk_i = 0
            self.xn = self.sb(es, "xn", [128, 2, D], BF16)
            self.st = self.sb(es, "st", [128, 64], F32)
            self.ps = es.enter_context(nc.psum_tensor("ps", [128, 8, 512], F32))
            self.sems = {e: es.enter_context(nc.semaphore(f"s_{e}")) for e in ENGS}
            self.out_ds = [self.dsem(f"out{i}", "sp") for i in range(4)]
            self.modc = self.sb(es, "modc", [128, 6, 48], F32)
            self.modres = self.sb(es, "modres", [128, 6, 24], F32)
            self.wmr = [self.sb(es, f"wmr{i}", [128, 8, 128], BF16) for i in range(2)]
            self.dwm = [self.dsem(f"wmr{i}") for i in range(2)]
            self.ones_f = self.sb(es, "ones_f", [128, 128], F32)
            self.ident_f = self.sb(es, "ident_f", [128, 128], F32)
            self.wm_i = 0
            self.mod_pending = []
            self.mod_loaded = []
            self.mod_step = 0
            self.mod_emitted = {}
            self.psM = None
            self.xn_i = 0
            self.pst_i = 0

            ld = [self.dsem(f"ldx{i}", "sp") for i in range(4)]
            for t in range(NT):
                self.dma("sp", ld[t % 4], self.x_sb[:, t, :], x_d[t * 128:(t + 1) * 128, :], (), [("x", t)])
            dmisc = self.dsem("misc", "sp")
            self.dma("sp", dmisc, self.ident[:], id_d, (), ["ident"])
            self.dma("sp", dmisc, self.cs[:], cT_d, (), ["cs"])
            self.memset("dve", self.epsb[:], EPS, ["eps"])
            self.memset("dve", self.ones_f[:], 1.0, ["ones_f"])
            self.cp("dve", self.ident_f[:], self.ident[:], ["ident"], ["ident_f"])
            self.dma("sp", dmisc, self.modc[:], self.din("mod_consts", [128, 6, 48]), (), ["modc"])
            self.act(self.cs[:], self.cs[:], AF.Silu, ["cs"], ["cs"])
            self.cp("dve", self.cs_bf[:], self.cs[:], ["cs"], ["cs_bf"])
            self.cp("dve", self.cs_bc[:], self.cs[:].unsqueeze(2).to_broadcast([128, 8, 128]), ["cs"], ["cs_bc"])

            for si, (layer, j) in enumerate(sublayers):
                last = final_store and si == len(sublayers) - 1
                if dbg == "load":
                    break
                if si == 0:
                    self.sub_idx = {sl_: i_ for i_, sl_ in enumerate(sublayers)}
                    self.sublist = list(sublayers)
                    self.mod_pending = [(i_, cg) for i_ in range(len(sublayers)) for cg in range(24)]
                self.phase_begin()
                self.mod_apply(si)
                self.phase_end()
                if dbg == "mod":
                    break
                self.phase_begin()
                if j in (0, 2):
                    self.ffn_phase(layer, j // 2, last)
                elif layer % 2 == 0:
                    self.mla_phase(layer // 2, last)
                else:
                    self.dil_phase(layer // 2, last)
                self.phase_end()

            if dbg is not None:
                for t in range(NT):
                    self.dma("sp", self.out_ds[t % 4], self.y_d[t * 128:(t + 1) * 128, :], self.x_sb[:, t, :],
                             [("x", t)], [])
            block = es.enter_context(nc.Block())
            self.S.emit(block, self.sems, self.out_ds)
        return nc

    def mod_unit(self, flush=False):
        w_mod = self.din("w_mod", [2, D, 9 * D])
        ps = self.ps
        loaded = self.mod_loaded
        if self.mod_pending and len(loaded) < 2 and not flush:
            si, cg = self.mod_pending.pop(0)
            layer, j = self.sublist[si]
            wv = w_mod[layer].rearrange("(kc p) f -> p kc f", p=128)
            sl = self.wm_i % 2
            self.wm_i += 1
            c0 = j * 3 * D + cg * 128
            self.dma("pool", self.dwm[sl], self.wmr[sl][:], wv[:, :, c0:c0 + 128], (), [("wmr", sl)])
            loaded.append((si, cg, sl, self.mod_step))
        self.mod_step += 1
        if loaded and (flush or loaded[0][3] < self.mod_step - 1 or not self.mod_pending):
            si, cg, sl, _ = loaded.pop(0)
            layer, j = self.sublist[si]
            bank, col0 = self.psM
            for kc in range(8):
                self.mm(ps[:, bank, col0 + cg:col0 + cg + 1], self.wmr[sl][:, kc, :], self.cs_bf[:, kc:kc + 1],
                        kc == 0, kc == 7, [("wmr", sl), "cs_bf"], [("ps", bank)])
            if cg == 23:
                gsi = layer * 3 + j
                self.cp("dve", self.modres[:, gsi, :], ps[:, bank, col0:col0 + 24], [("ps", bank)],
                        [("modres", gsi)])
                self.mod_emitted[si] = True
        return bool(self.mod_pending or loaded)

    def mod_apply(self, si):
        layer, j = self.sublist[si]
        res_w = 0.5 if j in (0, 2) else 1.0
        ps = self.ps
        if not self.mod_emitted.get(si):
            while self.mod_loaded:
                self.mod_unit(flush=True)
            self.psM = (0, 0)
            while not self.mod_emitted.get(si):
                if self.mod_pending and self.mod_pending[0][0] != si:
                    self.mod_unit(flush=True)
                else:
                    self.mod_unit()
        mr = self.modres
        mc = self.modc
        cf = self.st[:, 56:64]
        si = layer * 3 + j
        self.tt("dve", self.B_fm[:], mr[:, si, 0:8], mc[:, si, 0:8], ALU.add, [("modres", si), "modc"], ["B"])
        self.tt("dve", self.A_fm[:], mr[:, si, 8:16], mc[:, si, 8:16], ALU.add, [("modres", si), "modc"], ["A"])
        self.stt("dve", self.A_fm[:], self.A_fm[:], 1.0, mc[:, si, 24:32], ALU.add, ALU.mult, ["A", "modc"], ["A"])
        self.tt("dve", cf, mr[:, si, 16:24], mc[:, si, 16:24], ALU.add, [("modres", si), "modc"], ["cf"])
        self.stt("dve", cf, cf, float(res_w), mc[:, si, 32:40], ALU.mult, ALU.mult, ["cf", "modc"], ["cf"])
        with ExitStack() as es:
            dg = self.sb(es, "dg", [128, 8, 128], F32)
            for kc in range(8):
                self.ts("dve", dg[:, kc, :], self.ident_f[:], cf[:, kc:kc + 1], None, ALU.mult, None,
                        ["cf", "ident_f"], [("dg", kc)])
                bank = 1 + kc // 4
                self.mm(ps[:, bank, (kc % 4) * 128:(kc % 4 + 1) * 128], self.ones_f[:], dg[:, kc, :], True, True,
                        [("dg", kc), "ones_f"], [("ps", bank)])
            for h in range(2):
                self.cp("dve", self.C[:, h * 512:(h + 1) * 512], ps[:, 1 + h, :], [("ps", 1 + h)], ["C"])
            self.S.barrier()

    def mod_phase(self, layer, j):
        P = self
        w_mod = self.din("w_mod", [2, D, 9 * D])
        bm_fm = self.din("bm_fm", [2, 3, 128, 16])
        bm_gate = self.din("bm_gate", [2, 3, D])
        pre_fm = self.din("pre_fm", [2, 3, 128, 8])
        post_g = self.din("post_g", [2, 3, D])
        res_w = 0.5 if j in (0, 2) else 1.0
        wv = w_mod[layer].rearrange("(kc p) f -> p kc f", p=128)
        ps = self.ps
        with ExitStack() as es:
            wm = [self.sb(es, f"wm{i}", [128, 8, 256], BF16) for i in range(4)]
            pg = self.sb(es, "pg", [128, D], F32)
            bmf = self.sb(es, "bmf", [128, 16], F32)
            pref = self.sb(es, "pref", [128, 8], F32)
            dw = [self.dsem(f"wm{i}") for i in range(4)]
            dv = self.dsem("modv", "sp")
            self.dma("sp", dv, bmf[:], bm_fm[layer, j], (), ["bmf"])
            self.dma("sp", dv, pref[:], pre_fm[layer, j], (), ["pref"])
            self.dma("sp", dv, self.C[:], bm_gate[layer, j].partition_broadcast(128), (), ["C"])
            self.dma("sp", dv, pg[:], post_g[layer, j].partition_broadcast(128), (), ["pg"])
            def ld_wm(ch):
                c0 = j * 3 * D + ch * 256
                self.dma("pool", dw[ch % 4], wm[ch % 4][:], wv[:, :, c0:c0 + 256], (), [("wm", ch % 4)])

            for ch in range(3):
                ld_wm(ch)
            for ch in range(12):
                sl = ch % 4
                if ch + 3 < 12:
                    ld_wm(ch + 3)
                if ch < 8:
                    for sub in range(2):
                        cg = ch * 2 + sub
                        for kc in range(8):
                            self.mm(ps[:, 0, cg:cg + 1], wm[sl][:, kc, sub * 128:(sub + 1) * 128],
                                    self.cs_bf[:, kc:kc + 1], kc == 0, kc == 7,
                                    [("wm", sl), "cs_bf"], [("ps", 0)])
                else:
                    gc = ch - 8
                    bank = 1 + gc // 2
                    for kc in range(8):
                        self.mm(ps[:, bank, (gc % 2) * 256:(gc % 2) * 256 + 256], self.cs_bc[:, kc, :],
                                wm[sl][:, kc, :], kc == 0, kc == 7,
                                [("wm", sl), "cs_bc"], [("ps", bank)])
            self.tt("dve", self.B_fm[:], ps[:, 0, 0:8], bmf[:, 0:8], ALU.add, [("ps", 0), "bmf"], ["B"])
            self.tt("dve", self.A_fm[:], ps[:, 0, 8:16], bmf[:, 8:16], ALU.add, [("ps", 0), "bmf"], ["A"])
            self.stt("dve", self.A_fm[:], self.A_fm[:], 1.0, pref[:], ALU.add, ALU.mult, ["A", "pref"], ["A"])
            for h in range(2):
                self.tt("dve", self.C[:, h * 512:(h + 1) * 512], ps[:, 1 + h, :], self.C[:, h * 512:(h + 1) * 512],
                        ALU.add, [("ps", 1 + h), "C"], ["C"])
            self.stt("dve", self.C[:], self.C[:], float(res_w), pg[:], ALU.mult, ALU.mult, ["C", "pg"], ["C"])
            self.S.barrier()

    def pre_stats(self, tiles, col0=0):
        n = len(tiles)
        st = self.st
        for i, t in enumerate(tiles):
            js = self.junk_i % 2
            self.junk_i += 1
            self.act(self.junk[:, js, :], self.x_sb[:, t, :], AF.Square, [("x", t)],
                     [("st", col0 + i), ("junk", js)], scale=1.0 / 32.0, accum=st[:, col0 + i:col0 + i + 1])
        rd = [("st", col0 + i) for i in range(n)]
        wr = [("st", 32 + col0 + i) for i in range(n)]
        self.act(st[:, col0:col0 + n], st[:, col0:col0 + n], AF.Sqrt, rd + ["eps"], rd, bias=self.epsb[:, 0:1])
        self.recip(st[:, 32 + col0:32 + col0 + n], st[:, col0:col0 + n], rd, wr)

    def pre_tile(self, t, rcol, dst, dst_tok, banks=(6, 7)):
        if self.dbg == "pre_a":
            return
        sl = self.xn_i % 2
        self.xn_i += 1
        xn = self.xn[:, sl, :]
        self.ts("dve", xn, self.x_sb[:, t, :], self.st[:, 32 + rcol:33 + rcol], None, ALU.mult, None,
                [("x", t), ("st", 32 + rcol)], [("xn", sl)])
        if self.dbg == "pre_b":
            return
        bank = banks[self.pst_i % 2]
        self.pst_i += 1
        psb = self.ps[:, bank, :].bitcast(BF16).rearrange("p (k t) -> p k t", k=8)
        for kc in range(8):
            self.tr(psb[:, kc, :], xn[:, kc * 128:(kc + 1) * 128], [("xn", sl), "ident"], [("ps", bank)])
        if self.dbg == "pre_c":
            return
        for kc in range(8):
            if self.dbg == "pre_d" and kc % 2 == 1:
                continue
            if self.dbg == "pre_e" and kc % 2 == 0:
                continue
            if bank % 2 == 0:
                self.act(dst(kc), psb[:, kc, :], AF.Identity, [("ps", bank), "A", "B"], [dst_tok(kc)],
                         scale=self.A_fm[:, kc:kc + 1], bias=self.B_fm[:, kc:kc + 1])
            else:
                self.ts("dve", dst(kc), psb[:, kc, :], self.A_fm[:, kc:kc + 1], self.B_fm[:, kc:kc + 1],
                        ALU.mult, ALU.add, [("ps", bank), "A", "B"], [dst_tok(kc)])

    def epilogue(self, t, b0, scr, scr_tok, par, last):
        ps = self.ps
        st = self.st
        c = 16 + par * 2
        yv = ps[:, b0:b0 + 2, :]
        js = self.junk_i % 2
        self.junk_i += 1
        self.act(self.junk[:, js, :].rearrange("p (a b) -> p a b", a=2), yv, AF.Square,
                 [("ps", b0), ("ps", b0 + 1)], [("st", c), ("junk", js)], scale=1.0 / 32.0, accum=st[:, c:c + 1])
        self.act(st[:, c:c + 1], st[:, c:c + 1], AF.Sqrt, [("st", c), "eps"], [("st", c)], bias=self.epsb[:, 0:1])
        self.recip(st[:, c + 1:c + 2], st[:, c:c + 1], [("st", c)], [("st", c + 1)])
        for h in range(2):
            self.stt("dve", scr(h), ps[:, b0 + h, :], st[:, c + 1:c + 2], self.C[:, h * 512:(h + 1) * 512],
                     ALU.mult, ALU.mult, [("ps", b0 + h), ("st", c + 1), "C"], [scr_tok(h)])
            xs = self.x_sb[:, t, h * 512:(h + 1) * 512]
            self.tt("pool", xs, xs, scr(h), ALU.add, [("x", t), scr_tok(h)], [("x", t)])
        if last:
            self.dma("sp", self.out_ds[t % 4], self.y_d[t * 128:(t + 1) * 128, :], self.x_sb[:, t, :],
                     [("x", t)], [])

    def ffn_phase(self, layer, f, last):
        wg_d = self.din("ffn_w_gate", [2, 2, D, DFF])
        wu_d = self.din("ffn_w_up", [2, 2, D, DFF])
        wd_d = self.din("ffn_w_down", [2, 2, DFF, D])
        wgv = wg_d[layer, f].rearrange("(kc p) f -> p kc f", p=128)
        wuv = wu_d[layer, f].rearrange("(kc p) f -> p kc f", p=128)
        wdv = wd_d[layer, f].rearrange("(fc p) d -> p fc d", p=128)
        ps = self.ps
        NR = 3
        with ExitStack() as es:
            hnT = self.sb(es, "hnT", [128, 8, 1024], BF16)
            hT = self.sb(es, "hT", [128, NFC, 1024], BF16)
            wd = self.sb(es, "wd", [128, NFC, D], BF16)
            wg = [self.sb(es, f"wg{i}", [128, 8, 128], BF16) for i in range(NR)]
            wu = [self.sb(es, f"wu{i}", [128, 8, 128], BF16) for i in range(NR)]
            sg = self.sb(es, "sg", [128, 2, 512], F32)
            dg = [self.dsem(f"wg{i}") for i in range(NR)]
            du = [self.dsem(f"wu{i}") for i in range(NR)]
            dd = [self.dsem(f"wd{i}") for i in range(4)]
            gi = [0]

            def load_chunk(fc):
                sl = fc % NR
                self.dma("pool", dg[sl], wg[sl][:], wgv[:, :, fc * 128:(fc + 1) * 128], (), [("wg", sl)])
                self.dma("pool", du[sl], wu[sl][:], wuv[:, :, fc * 128:(fc + 1) * 128], (), [("wu", sl)])

            def prologue(tb, banks):
                tiles = [tb * 8 + i for i in range(8)]
                self.pre_stats(tiles)
                for tl, t in enumerate(tiles):
                    self.pre_tile(t, tl, lambda kc, tl=tl: hnT[:, kc, tl * 128:(tl + 1) * 128],
                                  lambda kc, tl=tl: ("hnT", tl, kc), banks)

            def gateup(tb):
                for fc in range(NFC):
                    if fc + NR - 1 < NFC:
                        load_chunk(fc + NR - 1)
                    self.dma("pool", dd[fc % 4], wd[:, fc, :], wdv[:, fc, :], (), [("wd", fc)])
                    sl = fc % NR
                    for nt in range(2):
                        par = gi[0] % 2
                        gi[0] += 1
                        for kc in range(8):
                            rd_h = [("hnT", nt * 4 + q, kc) for q in range(4)]
                            self.mm(ps[:, par, :], wg[sl][:, kc, :], hnT[:, kc, nt * 512:(nt + 1) * 512],
                                    kc == 0, kc == 7, [("wg", sl)] + rd_h, [("ps", par)])
                        for kc in range(8):
                            rd_h = [("hnT", nt * 4 + q, kc) for q in range(4)]
                            self.mm(ps[:, 2 + par, :], wu[sl][:, kc, :], hnT[:, kc, nt * 512:(nt + 1) * 512],
                                    kc == 0, kc == 7, [("wu", sl)] + rd_h, [("ps", 2 + par)])
                        self.act(sg[:, par, :], ps[:, par, :], AF.Silu, [("ps", par)], [("sg", par)])
                        self.tt("dve", hT[:, fc, nt * 512:(nt + 1) * 512], sg[:, par, :], ps[:, 2 + par, :],
                                ALU.mult, [("sg", par), ("ps", 2 + par)], [("hT", fc, nt)])

            def down(tb):
                tiles = [tb * 8 + i for i in range(8)]
                for tl, t in enumerate(tiles):
                    par = tl % 2
                    b0 = 4 + 2 * par
                    for h in range(2):
                        for fc in range(NFC):
                            self.mm(ps[:, b0 + h, :], hT[:, fc, tl * 128:(tl + 1) * 128],
                                    wd[:, fc, h * 512:(h + 1) * 512], fc == 0, fc == NFC - 1,
                                    [("hT", fc, tl // 4), ("wd", fc)], [("ps", b0 + h)])
                    self.epilogue(t, b0, lambda h: sg[:, h, :], lambda h: ("sg", h), par, last)

            for fc in range(NR - 1):
                load_chunk(fc)
            prologue(0, (6, 7))
            gateup(0)
            for fc in range(NR - 1):
                load_chunk(fc)
            prologue(1, (0, 1))
            down(0)
            gateup(1)
            down(1)

    def post_tile(self, t, o_tile, wo, oT, scr, par, last):
        ps = self.ps
        bank = 6 + (self.pst_i % 2)
        self.pst_i += 1
        sl = self.pst_i % 2
        psb = ps[:, bank, :].bitcast(BF16).rearrange("p (k t) -> p k t", k=8)
        for kc in range(8):
            self.tr(psb[:, kc, :], o_tile(kc), ["o_all"], [("ps", bank)])
        if bank == 6:
            self.act(oT[:, sl, :, :], psb, AF.Identity, [("ps", bank)], [("oT", sl)])
        else:
            self.cp("dve", oT[:, sl, :, :], psb, [("ps", bank)], [("oT", sl)])
        b0 = 2 * par
        for h in range(2):
            for kc in range(8):
                self.mm(ps[:, b0 + h, :], oT[:, sl, kc, :], wo[:, kc, h * 512:(h + 1) * 512], kc == 0, kc == 7,
                        [("oT", sl), "wo"], [("ps", b0 + h)])
        self.epilogue(t, b0, lambda h: scr[:, h, :], lambda h: ("scr", h), par, last)

    def mla_phase(self, a, last):
        H = 16
        w_in_d = self.din("mla_w_in", [1, D, 672])
        qn_d = self.din("mla_q_norm", [1, 384])
        wq_d = self.din("mla_w_q_up", [1, 384, 1536])
        kvn_d = self.din("mla_kv_norm", [1, 256])
        wkv_d = self.din("mla_w_kv_up", [1, 256, 2048])
        wo_d = self.din("mla_w_o", [1, D, D])
        cc_d = self.din("rope_cc", [128, NT, 32])
        ss_d = self.din("rope_ss", [128, NT, 32])
        mask_d = self.din("tri_mask", [128, 128], BF16)
        ps = self.ps
        st = self.st
        scale = 96.0 ** -0.5
        with ExitStack() as es0:
            cqnT = self.sb(es0, "cqnT", [128, 3, SQ], BF16)
            ckvnT = self.sb(es0, "ckvnT", [128, 2, SQ], BF16)
            krT = self.sb(es0, "krT", [128, SQ], BF16)
            o_sb = self.sb(es0, "o_sb", [128, NT, D], BF16)
            cc = self.sb(es0, "cc", [128, NT, 32], F32)
            ssn = self.sb(es0, "ssn", [128, NT, 32], F32)
            mask = self.sb(es0, "mask", [128, 128], BF16)
            dm = self.dsem("mla_c", "sp")
            self.dma("sp", dm, cc[:], cc_d, (), ["cc"])
            self.dma("sp", dm, ssn[:], ss_d, (), ["ssn"])
            self.dma("sp", dm, mask[:], mask_d, (), ["mask"])
            with ExitStack() as es:
                win = self.sb(es, "win", [128, 8, 672], BF16)
                hnt = self.sb(es, "hnt", [128, 2, 8, 128], BF16)
                latn = self.sb(es, "latn", [128, 2, 768], BF16)
                qg = self.sb(es, "qg", [128, 384], F32)
                kvg = self.sb(es, "kvg", [128, 256], F32)
                rt = self.sb(es, "rt", [128, 2, 64], F32)
                dwi = [self.dsem("win0"), self.dsem("win1")]
                self.dma("pool", dwi[0], win[:, 0:4, :], w_in_d[a].rearrange("(kc p) f -> p kc f", p=128)[:, 0:4, :],
                         (), [("win", 0)])
                self.dma("pool", dwi[1], win[:, 4:8, :], w_in_d[a].rearrange("(kc p) f -> p kc f", p=128)[:, 4:8, :],
                         (), [("win", 1)])
                self.dma("sp", dm, qg[:], qn_d[a].partition_broadcast(128), (), ["qg"])
                self.dma("sp", dm, kvg[:], kvn_d[a].partition_broadcast(128), (), ["kvg"])
                self.memset("dve", latn[:], 0.0, [("latn", 0), ("latn", 1)])
                self.pre_stats(list(range(NT)))
                def stA(t):
                    sl = t % 2
                    self.pre_tile(t, t, lambda kc, sl=sl: hnt[:, sl, kc, :], lambda kc, sl=sl: ("hnt", sl, kc))

                def stB(t):
                    sl = t % 2
                    bA, bB = (0, 1) if sl == 0 else (2, 3)
                    for kc in range(8):
                        self.mm(ps[:, bA, 0:384], hnt[:, sl, kc, :], win[:, kc, 0:384], kc == 0, kc == 7,
                                [("hnt", sl, kc), ("win", kc // 4)], [("ps", bA)])
                    for kc in range(8):
                        self.mm(ps[:, bB, 0:288], hnt[:, sl, kc, :], win[:, kc, 384:672], kc == 0, kc == 7,
                                [("hnt", sl, kc), ("win", kc // 4)], [("ps", bB)])

                def stC(t):
                    sl = t % 2
                    bA, bB = (0, 1) if sl == 0 else (2, 3)
                    c = 20 + sl * 4
                    js = self.junk_i % 2
                    self.junk_i += 1
                    self.act(self.junk[:, js, 0:384], ps[:, bA, 0:384], AF.Square, [("ps", bA)],
                             [("st", c), ("junk", js)], scale=384.0 ** -0.5, accum=st[:, c:c + 1])
                    js = self.junk_i % 2
                    self.junk_i += 1
                    self.act(self.junk[:, js, 0:256], ps[:, bB, 0:256], AF.Square, [("ps", bB)],
                             [("st", c + 1), ("junk", js)], scale=1.0 / 16.0, accum=st[:, c + 1:c + 2])
                    self.act(st[:, c:c + 2], st[:, c:c + 2], AF.Sqrt, [("st", c), ("st", c + 1), "eps"],
                             [("st", c), ("st", c + 1)], bias=self.epsb[:, 0:1])
                    self.recip(st[:, c + 2:c + 4], st[:, c:c + 2], [("st", c), ("st", c + 1)], [("st", c + 2)])
                    self.stt("dve", latn[:, sl, 0:384], ps[:, bA, 0:384], st[:, c + 2:c + 3], qg[:],
                             ALU.mult, ALU.mult, [("ps", bA), ("st", c + 2), "qg"], [("latn", sl)])
                    self.stt("dve", latn[:, sl, 384:640], ps[:, bB, 0:256], st[:, c + 3:c + 4], kvg[:],
                             ALU.mult, ALU.mult, [("ps", bB), ("st", c + 2), "kvg"], [("latn", sl)])
                    kr = ps[:, bB, 256:288]
                    self.tt("dve", rt[:, sl, 0:32], kr, cc[:, t, :], ALU.mult,
                            [("ps", bB), ("st", c + 2), "cc"], [("rt", sl)])
                    self.tt("dve", rt[:, sl, 32:48], ps[:, bB, 272:288], ssn[:, t, 0:16], ALU.mult,
                            [("ps", bB), "ssn"], [("rt", sl)])
                    self.tt("dve", rt[:, sl, 48:64], ps[:, bB, 256:272], ssn[:, t, 16:32], ALU.mult,
                            [("ps", bB), "ssn"], [("rt", sl)])
                    self.tt("dve", latn[:, sl, 704:736], rt[:, sl, 0:32], rt[:, sl, 32:64], ALU.add,
                            [("rt", sl)], [("latn", sl)])
                    bank = 4 + sl
                    psb = ps[:, bank, :].bitcast(BF16).rearrange("p (k t) -> p k t", k=8)
                    for k6 in range(6):
                        self.tr(psb[:, k6, :], latn[:, sl, k6 * 128:(k6 + 1) * 128], [("latn", sl), "ident"],
                                [("ps", bank)])
                    tc = slice(t * 128, (t + 1) * 128)
                    self.act(cqnT[:, :, tc], psb[:, 0:3, :], AF.Identity, [("ps", bank)], [("cqnT", t)])
                    self.act(ckvnT[:, :, tc], psb[:, 3:5, :], AF.Identity, [("ps", bank)], [("ckvnT", t)])
                    self.act(krT[64:96, tc], psb[64:96, 5, :], AF.Identity, [("ps", bank)], [("krT", t)])

                for step in range(NT + 2):
                    if step < NT:
                        stA(step)
                    if 0 <= step - 1 < NT:
                        stB(step - 1)
                    if 0 <= step - 2 < NT:
                        stC(step - 2)
                self.S.barrier()
            with ExitStack() as es:
                qT = self.sb(es, "qT", [128, 4, SQ], BF16)
                kT = self.sb(es, "kT", [128, 4, SQ], BF16)
                V = self.sb(es, "V", [128, NT, 4, 65], BF16)
                wq = self.sb(es, "wq", [128, 3, 384], BF16)
                wkk = self.sb(es, "wkk", [128, 2, 4, 64], BF16)
                wkv = self.sb(es, "wkv", [128, 2, 4, 64], BF16)
                qb = self.sb(es, "qb", [128, 2, 4, 96], BF16)
                rq = self.sb(es, "rq", [128, 2, 4, 64], F32)
                pT = self.sb(es, "pT", [128, 5, 512], BF16)
                rz = self.sb(es, "rz", [128, 2, 4], F32)
                dq = self.dsem("wq")
                dk = self.dsem("wkk")
                dvv = self.dsem("wkv")
                self.memset("pool", V[:], 1.0, [("V", kt) for kt in range(NT)])
                wqv = wq_d[a].rearrange("(kc p) f -> p kc f", p=128)
                wkvv = wkv_d[a].rearrange("(kc p) (h c) -> p kc h c", p=128, c=128)
                for h4 in range(4):
                    self.cp("pool" if h4 % 2 == 0 else "dve", kT[64:96, h4, :], krT[64:96, :], [], [("kTr", h4)])
                for hg in range(4):
                    self.dma("pool", dq, wq[:], wqv[:, :, hg * 384:(hg + 1) * 384], (), ["wq"])
                    for kc in range(2):
                        self.dma("pool", dk, wkk[:, kc], wkvv[:, kc, hg * 4:(hg + 1) * 4, 0:64], (), ["wkk"])
                        self.dma("pool", dvv, wkv[:, kc], wkvv[:, kc, hg * 4:(hg + 1) * 4, 64:128], (), ["wkv"])
                    for h4 in range(4):
                        for cch in range(4):
                            bank = 2 + (cch % 2)
                            for kc in range(2):
                                self.mm(ps[0:64, bank, :], wkk[:, kc, h4, :], ckvnT[:, kc, cch * 512:(cch + 1) * 512],
                                        kc == 0, kc == 1, ["wkk"], [("ps", bank)])
                            self.act(kT[0:64, h4, cch * 512:(cch + 1) * 512], ps[0:64, bank, :], AF.Identity,
                                     [("ps", bank)], [("kTn", h4, cch)])
                    for kt in range(NT):
                        bank = 6 + (kt % 2)
                        for kc in range(2):
                            self.mm(ps[:, bank, 0:256], ckvnT[:, kc, kt * 128:(kt + 1) * 128],
                                    wkv[:, kc, :, :].rearrange("p h c -> p (h c)"), kc == 0, kc == 1,
                                    ["wkv"], [("ps", bank)])
                        self.act(V[:, kt, :, 0:64], ps[:, bank, 0:256].rearrange("p (h c) -> p h c", c=64),
                                 AF.Identity, [("ps", bank)], [("V", kt)])
                    def q1(t):
                        sl = t % 2
                        bank = sl
                        for kc in range(3):
                            self.mm(ps[:, bank, 0:384], cqnT[:, kc, t * 128:(t + 1) * 128], wq[:, kc, :],
                                    kc == 0, kc == 2, ["wq"], [("ps", bank)])

                    def q2(t):
                        sl = t % 2
                        bank = sl
                        pq = ps[:, bank, 0:384].rearrange("p (h c) -> p h c", c=96)
                        ccb = cc[:, t, :].unsqueeze(1).to_broadcast([128, 4, 32])
                        s0 = ssn[:, t, 0:16].unsqueeze(1).to_broadcast([128, 4, 16])
                        s1 = ssn[:, t, 16:32].unsqueeze(1).to_broadcast([128, 4, 16])
                        self.cp("dve", qb[:, sl, :, 0:64], pq[:, :, 0:64], [("ps", bank)], [("qb", sl)])
                        self.tt("dve", rq[:, sl, :, 0:32], pq[:, :, 64:96], ccb, ALU.mult, [("ps", bank), "cc"],
                                [("rq", sl)])
                        self.tt("dve", rq[:, sl, :, 32:48], pq[:, :, 80:96], s0, ALU.mult, [("ps", bank), "ssn"],
                                [("rq", sl)])
                        self.tt("dve", rq[:, sl, :, 48:64], pq[:, :, 64:80], s1, ALU.mult, [("ps", bank), "ssn"],
                                [("rq", sl)])
                        self.tt("dve", qb[:, sl, :, 64:96], rq[:, sl, :, 0:32], rq[:, sl, :, 32:64], ALU.add,
                                [("rq", sl)], [("qb", sl)])
                        bk = 4 + sl
                        psb = ps[:, bk, :].bitcast(BF16).rearrange("p (k t) -> p k t", k=8)
                        for h4 in range(4):
                            self.tr(psb[0:96, h4, :], qb[:, sl, h4, :], [("qb", sl), "ident"], [("ps", bk)])
                        if sl == 0:
                            self.act(qT[0:96, :, t * 128:(t + 1) * 128], psb[0:96, 0:4, :], AF.Identity,
                                     [("ps", bk)], [("qT", t)])
                        else:
                            self.cp("dve", qT[0:96, :, t * 128:(t + 1) * 128], psb[0:96, 0:4, :],
                                    [("ps", bk)], [("qT", t)])

                    for step in range(NT + 1):
                        if step < NT:
                            q1(step)
                        if step >= 1:
                            q2(step - 1)
                    items = [(qc, h4, kt) for qc in range(4) for h4 in range(4) for kt in range(qc * 4 + 4)]
                    LA = 4
                    SBK = (0, 1, 2, 6, 7)
                    slots = {}

                    def score_item(i):
                        qc, h4, kt = items[i]
                        jl0 = max(0, kt - qc * 4)
                        n0 = jl0 * 128
                        sl = i % 5
                        slots[i] = sl
                        bk = SBK[sl]
                        self.mm(ps[:, bk, n0:512], kT[0:96, h4, kt * 128:(kt + 1) * 128],
                                qT[0:96, h4, qc * 512 + n0:(qc + 1) * 512], True, True,
                                [("kTn", h4, kt // 4), ("kTr", h4)] + [("qT", qc * 4 + j) for j in range(jl0, 4)],
                                [("ps", bk)])
                        self.act(pT[:, sl, n0:512], ps[:, bk, n0:512], AF.Exp, [("ps", bk)], [("pT", sl)],
                                 scale=scale)
                        if kt >= qc * 4:
                            self.tt("dve", pT[:, sl, n0:n0 + 128], pT[:, sl, n0:n0 + 128], mask[:],
                                    ALU.mult, [("pT", sl), "mask"], [("pT", sl)])

                    def pv_item(i):
                        qc, h4, kt = items[i]
                        h = hg * 4 + h4
                        nkt = qc * 4 + 4
                        jl0 = max(0, kt - qc * 4)
                        sl = slots[i]
                        gidx = qc * 4 + h4
                        ob = 3 + (gidx % 2)
                        orz = gidx % 2
                        for jl in range(jl0, 4):
                            self.mm(ps[:, ob, jl * 65:(jl + 1) * 65], pT[:, sl, jl * 128:(jl + 1) * 128],
                                    V[:, kt, h4, :], kt == 0 and jl == 0, kt == nkt - 1 and jl == 3,
                                    [("pT", sl), ("V", kt)], [("ps", ob)])
                        if kt == nkt - 1:
                            po = ps[:, ob, 0:260].rearrange("p (j c) -> p j c", c=65)
                            self.recip(rz[:, orz, :].unsqueeze(2), po[:, :, 64:65], [("ps", ob)], [("rz", orz)])
                            self.tt("dve", o_sb[:, qc * 4:(qc + 1) * 4, h * 64:(h + 1) * 64], po[:, :, 0:64],
                                    rz[:, orz, :].unsqueeze(2).to_broadcast([128, 4, 64]), ALU.mult,
                                    [("ps", ob), ("rz", orz)], [("o", qc, h)])

                    for i in range(min(LA, len(items))):
                        score_item(i)
                    self.psM = (5, 256)
                    for i in range(len(items)):
                        if i + LA < len(items):
                            score_item(i + LA)
                        pv_item(i)
                        if i % 6 == 5:
                            self.mod_unit()
                    if hg == 3:
                        while self.mod_loaded:
                            self.mod_unit(flush=True)
                self.S.barrier()
            with ExitStack() as es:
                wo = self.sb(es, "wo", [128, 8, D], BF16)
                oT = self.sb(es, "oT", [128, 2, 8, 128], BF16)
                scr = self.sb(es, "scr", [128, 2, 512], F32)
                dwo = [self.dsem("wo0"), self.dsem("wo1")]
                wov = wo_d[a].rearrange("(kc p) f -> p kc f", p=128)
                for i in range(2):
                    self.dma("pool", dwo[i], wo[:, i * 4:(i + 1) * 4, :], wov[:, i * 4:(i + 1) * 4, :], (), ["wo"])
                for t in range(NT):
                    self.post_tile(t, lambda kc, t=t: o_sb[:, t, kc * 128:(kc + 1) * 128], wo, oT, scr, t % 2, last)
                self.S.barrier()

    def dil_phase(self, b, last):
        DIL = (1, 4, 16)
        w_in_d = self.din("dil_w_in", [1, D, 9 * D])
        wo_d = self.din("dil_w_o", [1, D, D])
        eb_d = self.din("dil_eb", [3, 8, 128, 512])
        ps = self.ps
        wv_all = w_in_d[b].rearrange("(kc p) f -> p kc f", p=128)

        def tok(T, rows, g, r, n):
            dl = DIL[g]
            s0 = r + dl * n * 128
            return T[rows, s0:s0 + dl * 127 + 1:dl]

        with ExitStack() as es0:
            oT = self.sb(es0, "oT", [128, 8, SQ], BF16)
            with ExitStack() as es:
                hnT = self.sb(es, "hnT", [128, 8, SQ], BF16)
                Ut = self.sb(es, "Ut", [128, SQ], F32)
                Zt = self.sb(es, "Zt", [128, SQ], F32)
                qz = [self.sb(es, f"qz{e}", [128, SQ], BF16) for e in range(2)]
                kT = self.sb(es, "kT", [128, SQ], BF16)
                V0 = self.sb(es, "V0", [128, 16, 128], BF16)
                V1 = self.sb(es, "V1", [128, 16, 128], BF16)
                PA = self.sb(es, "PA", [128, 128], F32)
                PB = self.sb(es, "PB", [128, 128], F32)
                wr = [[self.sb(es, f"dw{i}_{s}", [128, 8, 128], BF16) for s in range(3)] for i in range(2)]
                eb = self.sb(es, "eb", [128, 2, 512], BF16)
                ebr = self.sb(es, "ebr", [128, 1, 512], F32)
                rzl = ebr
                pT = self.sb(es, "pT", [128, 4, 512], BF16)
                dwq = [[self.dsem(f"dw{i}{s}") for s in range(3)] for i in range(2)]
                deb = [self.dsem("eb0", "sp"), self.dsem("eb1", "sp")]
                self.memset("pool", qz[0][:], 0.0, ["qz"])
                self.memset("pool", qz[1][:], 0.0, ["qz"])
                self.memset("dve", V0[:], 1.0, ["V0"])
                self.memset("dve", V1[:], 1.0, ["V1"])
                dpp = self.dsem("pp", "sp")
                self.dma("sp", dpp, PA[:], self.din("swap_a", [128, 128]), (), ["PA"])
                self.dma("sp", dpp, PB[:], self.din("swap_b", [128, 128]), (), ["PB"])
                self.pre_stats(list(range(NT)))
                for t in range(NT):
                    self.pre_tile(t, t, lambda kc, t=t: hnT[:, kc, t * 128:(t + 1) * 128],
                                  lambda kc, t=t: ("hnT", t, kc))
                it = 0
                sbi = 0
                ubi = 0

                def ld_w(hp_, g_, wi_):
                    for s_ in range(3):
                        c0 = (g_ * 3 + s_) * D + hp_ * 128
                        self.dma("pool", dwq[wi_][s_], wr[wi_][s_][:], wv_all[:, :, c0:c0 + 128], (),
                                 [("dw", wi_, s_)])
                    self.dma("sp", deb[wi_], ebr[:, 0, :], eb_d[g_, hp_], (), [("ebr", 0)])
                    self.act(eb[:, wi_, :], ebr[:, 0, :], AF.Exp, [("ebr", 0)], [("eb", wi_)])

                n_it = 24
                dbg = ""
                for hp in range(8):
                    for g in range(3):
                        dl = DIL[g]
                        nb = 16 // dl
                        wi = it % 2
                        it += 1
                        if it == 1:
                            ld_w(hp, g, wi)
                        nxt = hp * 3 + g + 1
                        if nxt < n_it:
                            ld_w(nxt // 3, nxt % 3, 1 - wi)
                        for s in range(2):
                            for cch in range(4):
                                bank = cch % 2
                                for kc in range(8):
                                    self.mm(ps[:, bank, :], wr[wi][s][:, kc, :], hnT[:, kc, cch * 512:(cch + 1) * 512],
                                            kc == 0, kc == 7, [("dw", wi, s)] + [("hnT", cch * 4 + q, kc) for q in range(4)],
                                            [("ps", bank)])
                                cs_ = slice(cch * 512, (cch + 1) * 512)
                                if s == 0:
                                    self.act(qz[0][0:64, cs_], ps[0:64, bank, :], AF.Identity,
                                             [("ps", bank), "qz"], [("qk", 0, cch)], scale=0.125)
                                    self.act(qz[1][64:128, cs_], ps[64:128, bank, :], AF.Identity,
                                             [("ps", bank), "qz"], [("qk", 0, cch)], scale=0.125)
                                else:
                                    self.act(kT[:, cs_], ps[:, bank, :], AF.Identity,
                                             [("ps", bank)], [("qk", 1, cch)])
                        for b4 in range(0 if dbg.startswith("dil_a") else 4):
                            bank = 2 + (b4 % 2)
                            for bb in range(4):
                                bi = b4 * 4 + bb
                                r, n = bi // nb, bi % nb
                                for kc in range(8):
                                    self.mm(ps[:, bank, bb * 128:(bb + 1) * 128], tok(hnT[:, kc, :], slice(0, 128), g, r, n),
                                            wr[wi][2][:, kc, :], kc == 0, kc == 7,
                                            [("dw", wi, 2)] + [("hnT", q, kc) for q in range(NT)], [("ps", bank)])
                            pv = ps[:, bank, :].rearrange("p (b c) -> p b c", c=128)
                            self.act(V0[:, b4 * 4:(b4 + 1) * 4, 0:64], pv[:, :, 0:64], AF.Identity, [("ps", bank)],
                                     [("V", b4)])
                            self.act(V1[:, b4 * 4:(b4 + 1) * 4, 64:128], pv[:, :, 64:128], AF.Identity, [("ps", bank)],
                                     [("V", b4)])

                        SB = (4, 5, 0, 1)

                        def scores(bi):
                            nonlocal sbi
                            r, n = bi // nb, bi % nb
                            sl = sbi % 4
                            sbi += 1
                            bank = SB[sl]
                            kb0 = 0 if n > 0 else 1
                            allr = slice(0, 128)
                            for e in range(2):
                                for kb in range(kb0, 2):
                                    c0 = (e * 2 + kb) * 128
                                    self.mm(ps[:, bank, c0:c0 + 128], tok(kT, allr, g, r, n - 1 + kb),
                                            tok(qz[e], allr, g, r, n), True, True,
                                            [("qk", 0, q) for q in range(4)] + [("qk", 1, q) for q in range(4)],
                                            [("ps", bank)])
                            if kb0 == 0:
                                self.act(pT[:, sl, :], ps[:, bank, :], AF.Exp, [("ps", bank)], [("pT", sl)])
                                self.tt("dve", pT[:, sl, :], pT[:, sl, :], eb[:, wi, :], ALU.mult,
                                        [("pT", sl), ("eb", wi)], [("pT", sl)])
                            else:
                                v4 = lambda a: a.rearrange("p (e k q) -> p e k q", e=2, k=2)[:, :, 1, :]
                                self.act(v4(pT[:, sl, :]), v4(ps[:, bank, :]), AF.Exp, [("ps", bank)], [("pT", sl)])
                                self.tt("dve", v4(pT[:, sl, :]), v4(pT[:, sl, :]), v4(eb[:, wi, :]), ALU.mult,
                                        [("pT", sl), ("eb", wi)], [("pT", sl)])
                            return sl, kb0

                        def pv_block(bi, sl, kb0, ub, zb):
                            bb = bi % 4
                            for e, bank, Vs in ((0, ub, V0), (1, zb, V1)):
                                kbs = list(range(kb0, 2))
                                for i, kb in enumerate(kbs):
                                    c0 = (e * 2 + kb) * 128
                                    self.mm(ps[:, bank, bb * 128:(bb + 1) * 128], Vs[:, bi - 1 + kb, :],
                                            pT[:, sl, c0:c0 + 128], i == 0, i == len(kbs) - 1,
                                            [("pT", sl), ("V", (bi - 1 + kb) // 4)], [("ps", bank)])

                        def dst_view(T, b4):
                            if g == 0:
                                return T[:, b4 * 512:(b4 + 1) * 512]
                            if g == 1:
                                return T[:, b4:b4 + 4 * 511 + 1:4]
                            return T[:, :].rearrange("p (m r) -> p r m", r=16)[:, b4 * 4:(b4 + 1) * 4, :]

                        if dbg.startswith("dil_a") or dbg.startswith("dil_b"):
                            continue
                        LA = 3
                        pend = [scores(i) for i in range(LA)]
                        for bi in range(16):
                            b4 = bi // 4
                            if bi % 4 == 0:
                                ub = 6 + (ubi % 2)
                                zb = 2 + (ubi % 2)
                                ubi += 1
                            cur = pend.pop(0)
                            if bi + LA < 16:
                                pend.append(scores(bi + LA))
                            if dbg.startswith("dil_c"):
                                continue
                            pv_block(bi, cur[0], cur[1], ub, zb)
                            if dbg.startswith("dil_d"):
                                continue
                            if bi % 4 == 3:
                                pu = ps[:, ub, :]
                                pz = ps[:, zb, :]
                                if g == 2:
                                    pu = pu.rearrange("p (r m) -> p r m", r=4)
                                    pz = pz.rearrange("p (r m) -> p r m", r=4)
                                du = dst_view(Ut, b4)
                                dz = dst_view(Zt, b4)
                                if g == 0:
                                    self.cp("dve", du, pu, [("ps", ub)], [("Ut", b4)])
                                    self.cp("dve", dz, pz, [("ps", zb)], [("Zt", b4)])
                                else:
                                    self.tt("dve", du, pu, du, ALU.add, [("ps", ub)] + [("Ut", q) for q in range(4)],
                                            [("Ut", q) for q in range(4)])
                                    self.tt("dve", dz, pz, dz, ALU.add, [("ps", zb)] + [("Zt", q) for q in range(4)],
                                            [("Zt", q) for q in range(4)])
                    for q in range(4):
                        cs_ = slice(q * 512, (q + 1) * 512)
                        swb = 4 + (q % 2)
                        rd = [("Ut", k) for k in range(4)] + [("Zt", k) for k in range(4)]
                        self.mm(ps[:, swb, :], PA[:], Ut[:, cs_], True, False, rd + ["PA"], [("ps", swb)])
                        self.mm(ps[:, swb, :], PB[:], Zt[:, cs_], False, True, rd + ["PB"], [("ps", swb)])
                        self.recip(rzl[:, 0, :], ps[:, swb, :], [("ps", swb)], [("ebr", 0)])
                        self.tt("dve", oT[0:64, hp, cs_], Ut[0:64, cs_], rzl[0:64, 0, :], ALU.mult,
                                rd + [("ebr", 0)], [("oT", hp, 0)])
                        self.tt("pool", oT[64:128, hp, cs_], Zt[64:128, cs_], rzl[64:128, 0, :], ALU.mult,
                                rd + [("ebr", 0)], [("oT", hp, 1)])
                self.S.barrier()
            with ExitStack() as es:
                wo = self.sb(es, "wo", [128, 8, D], BF16)
                scr = self.sb(es, "scr", [128, 2, 512], F32)
                dwo = [self.dsem("wo0"), self.dsem("wo1")]
                wov = wo_d[b].rearrange("(kc p) f -> p kc f", p=128)
                for i in range(2):
                    self.dma("pool", dwo[i], wo[:, i * 4:(i + 1) * 4, :], wov[:, i * 4:(i + 1) * 4, :], (), ["wo"])
                for t in range(NT):
                    par = t % 2
                    b0 = 2 * par
                    for h in range(2):
                        for kc in range(8):
                            self.mm(ps[:, b0 + h, :], oT[:, kc, t * 128:(t + 1) * 128], wo[:, kc, h * 512:(h + 1) * 512],
                                    kc == 0, kc == 7, ["wo"], [("ps", b0 + h)])
                    self.epilogue(t, b0, lambda h: scr[:, h, :], lambda h: ("scr", h), par, last)
                self.S.barrier()


def _host_consts(inp):
    b_mod = np.asarray(inp["b_mod"], np.float32).reshape(2, 3, 3, D)
    ss = b_mod[:, :, 0:2, :].reshape(2, 3, 16, 128)
    bm_fm = np.ascontiguousarray(ss.transpose(0, 1, 3, 2))
    bm_gate = np.ascontiguousarray(b_mod[:, :, 2, :])
    pre = np.asarray(inp["norm_pre"], np.float32).reshape(2, 3, 8, 128)
    pre_fm = np.ascontiguousarray(pre.transpose(0, 1, 3, 2))
    pos = np.arange(SQ, dtype=np.float32)
    freqs = (np.float32(10000.0) ** (-(np.arange(16, dtype=np.float32) / np.float32(16)))).astype(np.float32)
    ang = (pos[:, None] * freqs[None, :]).astype(np.float32)
    cos = np.cos(ang).astype(np.float32)
    sin = np.sin(ang).astype(np.float32)
    cc = np.concatenate([cos, cos], axis=1).reshape(NT, 128, 32).transpose(1, 0, 2)
    ssn = np.concatenate([-sin, sin], axis=1).reshape(NT, 128, 32).transpose(1, 0, 2)
    tri = (np.arange(128)[:, None] <= np.arange(128)[None, :]).astype(np.float32)
    rb = np.asarray(inp["rel_bias"], np.float32)
    kk = np.arange(128)[:, None]
    qq = np.arange(128)[None, :]
    eb = np.full((3, 8, 128, 2, 2, 128), -30000.0, np.float32)
    for g, dl in enumerate((1, 4, 16)):
        for kb in range(2):
            rel = 128 + qq - (kb * 128 + kk)
            valid = (rel >= 0) & (rel <= 128)
            dist = np.maximum(rel, 0) * dl
            dd = np.maximum(dist, 1).astype(np.float32)
            large = 16 + (np.log(dd / np.float32(16)) / np.float32(math.log(2048 / 16))
                          * np.float32(16)).astype(np.int32)
            large = np.minimum(large, 31)
            bucket = np.where(dist < 16, dist, large)
            for hp in range(8):
                for e in range(2):
                    col = g * 16 + hp * 2 + e
                    eb[g, hp, :, e, kb, :] = np.where(valid, rb[bucket, col], np.float32(-30000.0))
    npost = np.asarray(inp["norm_post"], np.float32).reshape(2, 3, 8, 128)
    modc = np.zeros((128, 6, 48), np.float32)
    for l in range(2):
        for j in range(3):
            si = l * 3 + j
            for part in range(3):
                modc[:, si, part * 8:(part + 1) * 8] = b_mod[l, j, part].reshape(8, 128).T
            modc[:, si, 24:32] = pre[l, j].T
            modc[:, si, 32:40] = npost[l, j].T
    swap_a = np.zeros((128, 128), np.float32)
    swap_b = np.zeros((128, 128), np.float32)
    for m in range(64):
        swap_a[m + 64, m] = 1.0
        swap_b[m, m + 64] = 1.0
    return {
        "swap_a": swap_a, "swap_b": swap_b,
        "mod_consts": modc,
        "rope_cc": np.ascontiguousarray(cc), "rope_ss": np.ascontiguousarray(ssn),
        "tri_mask": tri.astype(ml_dtypes.bfloat16),
        "dil_eb": np.ascontiguousarray(eb.reshape(3, 8, 128, 512)),
        "bm_fm": bm_fm, "bm_gate": bm_gate, "pre_fm": pre_fm,
        "post_g": np.ascontiguousarray(np.asarray(inp["norm_post"], np.float32)),
        "ident": np.eye(128, dtype=np.float32).astype(ml_dtypes.bfloat16),
    }


_PROG_CACHE = {}
TRACE = False


def run_sublayers(inp, xs, sublayers, dbg=None):
    key = (tuple(sublayers), dbg)
    if key not in _PROG_CACHE:
        p = Prog()
        nc = p.build(list(sublayers), dbg=dbg)
        _PROG_CACHE[key] = (p, nc)
    p, nc = _PROG_CACHE[key]
    consts = _host_consts(inp)
    c = np.asarray(inp["c"], np.float32)
    in_maps = []
    for b in range(N_CORES):
        m = {}
        for name in p.inputs:
            if name == "x":
                m[name] = np.ascontiguousarray(xs[b])
            elif name == "cT":
                m[name] = np.ascontiguousarray(c[b].reshape(8, 128).T)
            elif name in consts:
                m[name] = consts[name]
            else:
                m[name] = np.ascontiguousarray(np.asarray(inp[name], np.float32))
        in_maps.append(m)
    if TRACE:
        res = run_bass_kernel_spmd(nc, in_maps, core_ids=list(range(N_CORES)), trace=True)
        print("EXEC_NS", res.exec_time_ns)
    else:
        res = run_bass_kernel_spmd(nc, in_maps, core_ids=list(range(N_CORES)))
    return np.stack([np.asarray(r["y"]) for r in res.results], axis=0)


ALL_SUBLAYERS = [(0, 0), (0, 1), (0, 2), (1, 0), (1, 1), (1, 2)]


def kernel(**inputs):
    xs = np.asarray(inputs["x"], np.float32)
    out = run_sublayers(inputs, xs, ALL_SUBLAYERS)
    return out.astype(np.float32)
```

```python
import math
from contextlib import ExitStack

import numpy as np
import ml_dtypes

import concourse.bass as bass
import concourse.mybir as mybir
from concourse.bass_utils import run_bass_kernel_spmd

F32 = mybir.dt.float32
BF16 = mybir.dt.bfloat16
AF = mybir.ActivationFunctionType
ALU = mybir.AluOpType

ENGS = ("pe", "act", "dve", "pool", "sp")

D = 1024
SQ = 2048
NT = 16
DFF = 2816
NFC = 22
EPS = 1e-6
N_CORES = 8


class DSem:
    def __init__(self, handle, name):
        self.h = handle
        self.name = name
        self.count = 0


class Op:
    __slots__ = ("eng", "fn", "waits", "inc", "cs", "pos", "dsem")


class Sched:
    def __init__(self):
        self.streams = {e: [] for e in ENGS}
        self.ncomp = {e: 0 for e in ENGS}
        self.last_w = {}
        self.readers = {}
        self.waited = {e: {} for e in ENGS}
        self.comp_ops = {e: [] for e in ENGS}
        self.dsems = []
        self.pending = {e: None for e in ENGS}

    def _deps(self, eng, reads, writes):
        need = {}

        def add(d):
            if d is None:
                return
            cs, pos = d
            if need.get(cs, 0) < pos:
                need[cs] = pos

        if self.pending[eng] is not None:
            for cs, pos in self.pending[eng].items():
                add((cs, pos))
            self.pending[eng] = None
        for t in reads:
            add(self.last_w.get(t))
        for t in writes:
            add(self.last_w.get(t))
            for r in self.readers.get(t, ()):
                add(r)
        waits = []
        wd = self.waited[eng]
        for cs, pos in need.items():
            if cs == "pe" and eng == "pe":
                continue
            if wd.get(cs, 0) >= pos:
                continue
            wd[cs] = pos
            waits.append((cs, pos))
            if isinstance(cs, str):
                self.comp_ops[cs][pos - 1].inc = True
        return waits

    def _commit(self, me, reads, writes):
        for t in reads:
            self.readers.setdefault(t, []).append(me)
        for t in writes:
            self.last_w[t] = me
            self.readers[t] = []

    def op(self, eng, fn, reads=(), writes=()):
        o = Op()
        o.eng = eng
        o.fn = fn
        o.waits = self._deps(eng, reads, writes)
        o.inc = False
        o.dsem = None
        self.ncomp[eng] += 1
        o.cs = eng
        o.pos = self.ncomp[eng]
        self.comp_ops[eng].append(o)
        self.streams[eng].append(o)
        self._commit((o.cs, o.pos), reads, writes)
        return o

    def dma(self, eng, dsem, fn, reads=(), writes=()):
        o = Op()
        o.eng = eng
        o.fn = fn
        o.waits = self._deps(eng, reads, writes)
        if dsem.count > 0 and self.waited[eng].get(dsem, 0) < dsem.count:
            self.waited[eng][dsem] = dsem.count
            o.waits.append((dsem, dsem.count))
        o.inc = True
        o.dsem = dsem
        dsem.count += 1
        o.cs = dsem
        o.pos = dsem.count
        self.streams[eng].append(o)
        self._commit((o.cs, o.pos), reads, writes)
        return o

    def barrier(self):
        front = {}
        for e in ENGS:
            if self.ncomp[e] > 0:
                front[e] = self.ncomp[e]
        for d in self.dsems:
            if d.count > 0:
                front[d] = d.count
        for e in ENGS:
            p = dict(front)
            if self.pending[e] is not None:
                for k, v in self.pending[e].items():
                    if p.get(k, 0) < v:
                        p[k] = v
            self.pending[e] = p

    def emit(self, block, sems, final_dsems):
        val = {}
        for e in ENGS:
            c = 0
            for o in self.comp_ops[e]:
                if o.inc:
                    c += 1
                    val[(e, o.pos)] = c

        def wait(engobj, cs, pos):
            if isinstance(cs, str):
                engobj.wait_ge(sems[cs], val[(cs, pos)])
            else:
                engobj.wait_ge(cs.h, 16 * pos)

        def run(e, engobj):
            for o in self.streams[e]:
                for cs, pos in o.waits:
                    wait(engobj, cs, pos)
                ins = o.fn(engobj)
                if o.dsem is not None:
                    ins.then_inc(o.dsem.h, 16)
                elif o.inc:
                    ins.then_inc(sems[e], 1)
            if e == "sp":
                for d in final_dsems:
                    engobj.wait_ge(d.h, 16 * d.count)

        @block.tensor
        def _(eng):
            run("pe", eng)

        @block.scalar
        def _(eng):
            run("act", eng)

        @block.vector
        def _(eng):
            run("dve", eng)

        @block.gpsimd
        def _(eng):
            run("pool", eng)

        @block.sync
        def _(eng):
            run("sp", eng)


class Prog:
    def __init__(self):
        self.nc = bass.Bass("TRN2", target_bir_lowering=False)
        self.S = Sched()
        self.inputs = {}
        self.es = ExitStack()
        self.nds = 0
        self.ds_pool = {}
        self.ds_taken = None

    def din(self, name, shape, dtype=F32):
        if name not in self.inputs:
            self.inputs[name] = self.nc.dram_tensor(name, list(shape), dtype, kind="ExternalInput").ap()
        return self.inputs[name]

    def sb(self, es, name, shape, dtype):
        self.nsb = getattr(self, "nsb", 0) + 1
        return es.enter_context(self.nc.sbuf_tensor(f"{name}_{self.nsb}", list(shape), dtype))

    def dsem(self, name, q="pool"):
        pool = self.ds_pool.setdefault(q, [])
        if pool:
            d = pool.pop()
        else:
            self.nds += 1
            d = DSem(self.es.enter_context(self.nc.semaphore(f"dsem{self.nds}")), name)
            d.q = q
            self.S.dsems.append(d)
        if self.ds_taken is not None:
            self.ds_taken.append(d)
        return d

    def phase_begin(self):
        self.ds_taken = []

    def phase_end(self):
        self.S.barrier()
        for d in self.ds_taken:
            self.ds_pool[d.q].append(d)
        self.ds_taken = None

    def mm(self, out, lhsT, rhs, start, stop, r, w):
        self.S.op("pe", lambda e: e.matmul(out=out, lhsT=lhsT, rhs=rhs, start=start, stop=stop), r, w)

    def tr(self, out, in_, r, w):
        ident = self.ident[:]
        self.S.op("pe", lambda e: e.transpose(out=out, in_=in_, identity=ident), r, w)

    def act(self, out, in_, func, r, w, bias=None, scale=None, accum=None, eng="act"):
        kw = {}
        if bias is not None:
            kw["bias"] = bias
        if scale is not None:
            kw["scale"] = scale
        if accum is not None:
            kw["accum_out"] = accum
        self.S.op("act", lambda e: e.activation(out=out, in_=in_, func=func, **kw), r, w)

    def ts(self, eng, out, in0, s1, s2, op0, op1, r, w):
        if s2 is None:
            self.S.op(eng, lambda e: e.tensor_scalar(out=out, in0=in0, scalar1=s1, scalar2=None, op0=op0), r, w)
        else:
            self.S.op(eng, lambda e: e.tensor_scalar(out=out, in0=in0, scalar1=s1, scalar2=s2, op0=op0, op1=op1),
                      r, w)

    def tt(self, eng, out, in0, in1, op, r, w):
        self.S.op(eng, lambda e: e.tensor_tensor(out=out, in0=in0, in1=in1, op=op), r, w)

    def stt(self, eng, out, in0, scalar, in1, op0, op1, r, w):
        self.S.op(eng, lambda e: e.scalar_tensor_tensor(out=out, in0=in0, scalar=scalar, in1=in1, op0=op0, op1=op1),
                  r, w)

    def cp(self, eng, out, in_, r, w):
        self.S.op(eng, lambda e: e.tensor_copy(out=out, in_=in_), r, w)

    def recip(self, out, in_, r, w):
        self.S.op("dve", lambda e: e.reciprocal(out=out, in_=in_), r, w)

    def memset(self, eng, ap, v, w):
        self.S.op(eng, lambda e: e.memset(ap, v), (), w)

    def dma(self, eng, dsem, out, in_, r, w):
        assert dsem.q == eng, (dsem.name, dsem.q, eng)
        return self.S.dma(eng, dsem, lambda e: e.dma_start(out=out, in_=in_), r, w)

    def build(self, sublayers, final_store=True, dbg=None):
        nc = self.nc
        es = self.es
        P = self
        self.dbg = dbg
        with es:
            x_d = self.din("x", [SQ, D])
            cT_d = self.din("cT", [128, 8])
            id_d = self.din("ident", [128, 128], BF16)
            self.y_d = nc.dram_tensor("y", [SQ, D], F32, kind="ExternalOutput").ap()

            self.x_sb = self.sb(es, "x_sb", [128, NT, D], F32)
            self.ident = self.sb(es, "ident_sb", [128, 128], BF16)
            self.cs = self.sb(es, "cs", [128, 8], F32)
            self.cs_bf = self.sb(es, "cs_bf", [128, 8], BF16)
            self.cs_bc = self.sb(es, "cs_bc", [128, 8, 128], BF16)
            self.A_fm = self.sb(es, "A_fm", [128, 8], F32)
            self.B_fm = self.sb(es, "B_fm", [128, 8], F32)
            self.C = self.sb(es, "C_bc", [128, D], F32)
            self.epsb = self.sb(es, "epsb", [128, 1], F32)
            self.junk = self.sb(es, "junk", [128, 2, D], BF16)
            self.junk_i = 0
            self.xn = self.sb(es, "xn", [128, 2, D], BF16)
            self.st = self.sb(es, "st", [128, 64], F32)
            self.ps = es.enter_context(nc.psum_tensor("ps", [128, 8, 512], F32))
            self.sems = {e: es.enter_context(nc.semaphore(f"s_{e}")) for e in ENGS}
            self.out_ds = [self.dsem(f"out{i}", "sp") for i in range(4)]
            self.modc = self.sb(es, "modc", [128, 6, 48], F32)
            self.modres = self.sb(es, "modres", [128, 6, 24], F32)
            self.wmr = [self.sb(es, f"wmr{i}", [128, 8, 128], BF16) for i in range(2)]
            self.dwm = [self.dsem(f"wmr{i}") for i in range(2)]
            self.ones_f = self.sb(es, "ones_f", [128, 128], F32)
            self.ident_f = self.sb(es, "ident_f", [128, 128], F32)
            self.wm_i = 0
            self.mod_pending = []
            self.mod_loaded = []
            self.mod_step = 0
            self.mod_emitted = {}
            self.psM = None
            self.xn_i = 0
            self.pst_i = 0

            ld = [self.dsem(f"ldx{i}", "sp") for i in range(4)]
            for t in range(NT):
                self.dma("sp", ld[t % 4], self.x_sb[:, t, :], x_d[t * 128:(t + 1) * 128, :], (), [("x", t)])
            dmisc = self.dsem("misc", "sp")
            self.dma("sp", dmisc, self.ident[:], id_d, (), ["ident"])
            self.dma("sp", dmisc, self.cs[:], cT_d, (), ["cs"])
            self.memset("dve", self.epsb[:], EPS, ["eps"])
            self.memset("dve", self.ones_f[:], 1.0, ["ones_f"])
            self.cp("dve", self.ident_f[:], self.ident[:], ["ident"], ["ident_f"])
            self.dma("sp", dmisc, self.modc[:], self.din("mod_consts", [128, 6, 48]), (), ["modc"])
            self.act(self.cs[:], self.cs[:], AF.Silu, ["cs"], ["cs"])
            self.cp("dve", self.cs_bf[:], self.cs[:], ["cs"], ["cs_bf"])
            self.cp("dve", self.cs_bc[:], self.cs[:].unsqueeze(2).to_broadcast([128, 8, 128]), ["cs"], ["cs_bc"])

            for si, (layer, j) in enumerate(sublayers):
                last = final_store and si == len(sublayers) - 1
                if dbg == "load":
                    break
                if si == 0:
                    self.sub_idx = {sl_: i_ for i_, sl_ in enumerate(sublayers)}
                    self.sublist = list(sublayers)
                    self.mod_pending = [(i_, cg) for i_ in range(len(sublayers)) for cg in range(24)]
                self.phase_begin()
                self.mod_apply(si)
                self.phase_end()
                if dbg == "mod":
                    break
                self.phase_begin()
                if j in (0, 2):
                    self.ffn_phase(layer, j // 2, last, host_mod=(si == 0 and len(sublayers) > 1))
                elif layer % 2 == 0:
                    self.mla_phase(layer // 2, last)
                else:
                    self.dil_phase(layer // 2, last)
                self.phase_end()

            if dbg is not None:
                for t in range(NT):
                    self.dma("sp", self.out_ds[t % 4], self.y_d[t * 128:(t + 1) * 128, :], self.x_sb[:, t, :],
                             [("x", t)], [])
            block = es.enter_context(nc.Block())
            self.S.emit(block, self.sems, self.out_ds)
        return nc

    def mod_unit(self, flush=False):
        w_mod = self.din("w_mod", [2, D, 9 * D])
        ps = self.ps
        loaded = self.mod_loaded
        if self.mod_pending and len(loaded) < 2 and not flush:
            si, cg = self.mod_pending.pop(0)
            layer, j = self.sublist[si]
            wv = w_mod[layer].rearrange("(kc p) f -> p kc f", p=128)
            sl = self.wm_i % 2
            self.wm_i += 1
            c0 = j * 3 * D + cg * 128
            self.dma("pool", self.dwm[sl], self.wmr[sl][:], wv[:, :, c0:c0 + 128], (), [("wmr", sl)])
            loaded.append((si, cg, sl, self.mod_step))
        self.mod_step += 1
        if loaded and (flush or loaded[0][3] < self.mod_step - 1 or not self.mod_pending):
            si, cg, sl, _ = loaded.pop(0)
            layer, j = self.sublist[si]
            bank, col0 = self.psM
            for kc in range(8):
                self.mm(ps[:, bank, col0 + cg:col0 + cg + 1], self.wmr[sl][:, kc, :], self.cs_bf[:, kc:kc + 1],
                        kc == 0, kc == 7, [("wmr", sl), "cs_bf"], [("ps", bank)])
            if cg == 23:
                gsi = layer * 3 + j
                self.cp("dve", self.modres[:, gsi, :], ps[:, bank, col0:col0 + 24], [("ps", bank)],
                        [("modres", gsi)])
                self.mod_emitted[si] = True
        return bool(self.mod_pending or loaded)

    def mod_apply(self, si):
        layer, j = self.sublist[si]
        res_w = 0.5 if j in (0, 2) else 1.0
        ps = self.ps
        if not self.mod_emitted.get(si):
            while self.mod_loaded:
                self.mod_unit(flush=True)
            self.psM = (0, 0)
            while not self.mod_emitted.get(si):
                if self.mod_pending and self.mod_pending[0][0] != si:
                    self.mod_unit(flush=True)
                else:
                    self.mod_unit()
        mr = self.modres
        mc = self.modc
        cf = self.st[:, 56:64]
        si = layer * 3 + j
        self.tt("dve", self.B_fm[:], mr[:, si, 0:8], mc[:, si, 0:8], ALU.add, [("modres", si), "modc"], ["B"])
        self.tt("dve", self.A_fm[:], mr[:, si, 8:16], mc[:, si, 8:16], ALU.add, [("modres", si), "modc"], ["A"])
        self.stt("dve", self.A_fm[:], self.A_fm[:], 1.0, mc[:, si, 24:32], ALU.add, ALU.mult, ["A", "modc"], ["A"])
        self.tt("dve", cf, mr[:, si, 16:24], mc[:, si, 16:24], ALU.add, [("modres", si), "modc"], ["cf"])
        self.stt("dve", cf, cf, float(res_w), mc[:, si, 32:40], ALU.mult, ALU.mult, ["cf", "modc"], ["cf"])
        with ExitStack() as es:
            dg = self.sb(es, "dg", [128, 8, 128], F32)
            for kc in range(8):
                self.ts("dve", dg[:, kc, :], self.ident_f[:], cf[:, kc:kc + 1], None, ALU.mult, None,
                        ["cf", "ident_f"], [("dg", kc)])
                bank = 1 + kc // 4
                self.mm(ps[:, bank, (kc % 4) * 128:(kc % 4 + 1) * 128], self.ones_f[:], dg[:, kc, :], True, True,
                        [("dg", kc), "ones_f"], [("ps", bank)])
            for h in range(2):
                self.cp("dve", self.C[:, h * 512:(h + 1) * 512], ps[:, 1 + h, :], [("ps", 1 + h)], ["C"])
            self.S.barrier()

    def mod_phase(self, layer, j):
        P = self
        w_mod = self.din("w_mod", [2, D, 9 * D])
        bm_fm = self.din("bm_fm", [2, 3, 128, 16])
        bm_gate = self.din("bm_gate", [2, 3, D])
        pre_fm = self.din("pre_fm", [2, 3, 128, 8])
        post_g = self.din("post_g", [2, 3, D])
        res_w = 0.5 if j in (0, 2) else 1.0
        wv = w_mod[layer].rearrange("(kc p) f -> p kc f", p=128)
        ps = self.ps
        with ExitStack() as es:
            wm = [self.sb(es, f"wm{i}", [128, 8, 256], BF16) for i in range(4)]
            pg = self.sb(es, "pg", [128, D], F32)
            bmf = self.sb(es, "bmf", [128, 16], F32)
            pref = self.sb(es, "pref", [128, 8], F32)
            dw = [self.dsem(f"wm{i}") for i in range(4)]
            dv = self.dsem("modv", "sp")
            self.dma("sp", dv, bmf[:], bm_fm[layer, j], (), ["bmf"])
            self.dma("sp", dv, pref[:], pre_fm[layer, j], (), ["pref"])
            self.dma("sp", dv, self.C[:], bm_gate[layer, j].partition_broadcast(128), (), ["C"])
            self.dma("sp", dv, pg[:], post_g[layer, j].partition_broadcast(128), (), ["pg"])
            def ld_wm(ch):
                c0 = j * 3 * D + ch * 256
                self.dma("pool", dw[ch % 4], wm[ch % 4][:], wv[:, :, c0:c0 + 256], (), [("wm", ch % 4)])

            for ch in range(3):
                ld_wm(ch)
            for ch in range(12):
                sl = ch % 4
                if ch + 3 < 12:
                    ld_wm(ch + 3)
                if ch < 8:
                    for sub in range(2):
                        cg = ch * 2 + sub
                        for kc in range(8):
                            self.mm(ps[:, 0, cg:cg + 1], wm[sl][:, kc, sub * 128:(sub + 1) * 128],
                                    self.cs_bf[:, kc:kc + 1], kc == 0, kc == 7,
                                    [("wm", sl), "cs_bf"], [("ps", 0)])
                else:
                    gc = ch - 8
                    bank = 1 + gc // 2
                    for kc in range(8):
                        self.mm(ps[:, bank, (gc % 2) * 256:(gc % 2) * 256 + 256], self.cs_bc[:, kc, :],
                                wm[sl][:, kc, :], kc == 0, kc == 7,
                                [("wm", sl), "cs_bc"], [("ps", bank)])
            self.tt("dve", self.B_fm[:], ps[:, 0, 0:8], bmf[:, 0:8], ALU.add, [("ps", 0), "bmf"], ["B"])
            self.tt("dve", self.A_fm[:], ps[:, 0, 8:16], bmf[:, 8:16], ALU.add, [("ps", 0), "bmf"], ["A"])
            self.stt("dve", self.A_fm[:], self.A_fm[:], 1.0, pref[:], ALU.add, ALU.mult, ["A", "pref"], ["A"])
            for h in range(2):
                self.tt("dve", self.C[:, h * 512:(h + 1) * 512], ps[:, 1 + h, :], self.C[:, h * 512:(h + 1) * 512],
                        ALU.add, [("ps", 1 + h), "C"], ["C"])
            self.stt("dve", self.C[:], self.C[:], float(res_w), pg[:], ALU.mult, ALU.mult, ["C", "pg"], ["C"])
            self.S.barrier()

    def pre_stats(self, tiles, col0=0):
        n = len(tiles)
        st = self.st
        for i, t in enumerate(tiles):
            js = self.junk_i % 2
            self.junk_i += 1
            self.act(self.junk[:, js, :], self.x_sb[:, t, :], AF.Square, [("x", t)],
                     [("st", col0 + i), ("junk", js)], scale=1.0 / 32.0, accum=st[:, col0 + i:col0 + i + 1])
        rd = [("st", col0 + i) for i in range(n)]
        wr = [("st", 32 + col0 + i) for i in range(n)]
        self.act(st[:, col0:col0 + n], st[:, col0:col0 + n], AF.Sqrt, rd + ["eps"], rd, bias=self.epsb[:, 0:1])
        self.recip(st[:, 32 + col0:32 + col0 + n], st[:, col0:col0 + n], rd, wr)

    def pre_tile(self, t, rcol, dst, dst_tok, banks=(6, 7)):
        if self.dbg == "pre_a":
            return
        sl = self.xn_i % 2
        self.xn_i += 1
        xn = self.xn[:, sl, :]
        self.ts("dve", xn, self.x_sb[:, t, :], self.st[:, 32 + rcol:33 + rcol], None, ALU.mult, None,
                [("x", t), ("st", 32 + rcol)], [("xn", sl)])
        if self.dbg == "pre_b":
            return
        bank = banks[self.pst_i % 2]
        self.pst_i += 1
        psb = self.ps[:, bank, :].bitcast(BF16).rearrange("p (k t) -> p k t", k=8)
        for kc in range(8):
            self.tr(psb[:, kc, :], xn[:, kc * 128:(kc + 1) * 128], [("xn", sl), "ident"], [("ps", bank)])
        if self.dbg == "pre_c":
            return
        for kc in range(8):
            if self.dbg == "pre_d" and kc % 2 == 1:
                continue
            if self.dbg == "pre_e" and kc % 2 == 0:
                continue
            if bank % 2 == 0:
                self.act(dst(kc), psb[:, kc, :], AF.Identity, [("ps", bank), "A", "B"], [dst_tok(kc)],
                         scale=self.A_fm[:, kc:kc + 1], bias=self.B_fm[:, kc:kc + 1])
            else:
                self.ts("dve", dst(kc), psb[:, kc, :], self.A_fm[:, kc:kc + 1], self.B_fm[:, kc:kc + 1],
                        ALU.mult, ALU.add, [("ps", bank), "A", "B"], [dst_tok(kc)])

    def epilogue(self, t, b0, scr, scr_tok, par, last):
        ps = self.ps
        st = self.st
        c = 16 + par * 2
        yv = ps[:, b0:b0 + 2, :]
        js = self.junk_i % 2
        self.junk_i += 1
        self.act(self.junk[:, js, :].rearrange("p (a b) -> p a b", a=2), yv, AF.Square,
                 [("ps", b0), ("ps", b0 + 1)], [("st", c), ("junk", js)], scale=1.0 / 32.0, accum=st[:, c:c + 1])
        self.act(st[:, c:c + 1], st[:, c:c + 1], AF.Sqrt, [("st", c), "eps"], [("st", c)], bias=self.epsb[:, 0:1])
        self.recip(st[:, c + 1:c + 2], st[:, c:c + 1], [("st", c)], [("st", c + 1)])
        for h in range(2):
            self.stt("dve", scr(h), ps[:, b0 + h, :], st[:, c + 1:c + 2], self.C[:, h * 512:(h + 1) * 512],
                     ALU.mult, ALU.mult, [("ps", b0 + h), ("st", c + 1), "C"], [scr_tok(h)])
            xs = self.x_sb[:, t, h * 512:(h + 1) * 512]
            self.tt("pool", xs, xs, scr(h), ALU.add, [("x", t), scr_tok(h)], [("x", t)])
        if last:
            self.dma("sp", self.out_ds[t % 4], self.y_d[t * 128:(t + 1) * 128, :], self.x_sb[:, t, :],
                     [("x", t)], [])

    def ffn_phase(self, layer, f, last, host_mod=False):
        wg_d = self.din("ffn_w_gate", [2, 2, D, DFF])
        wu_d = self.din("ffn_w_up", [2, 2, D, DFF])
        wd_d = self.din("ffn_w_down", [2, 2, DFF, D])
        wgv = wg_d[layer, f].rearrange("(kc p) f -> p kc f", p=128)
        wuv = wu_d[layer, f].rearrange("(kc p) f -> p kc f", p=128)
        wdv = wd_d[layer, f].rearrange("(fc p) d -> p fc d", p=128)
        ps = self.ps
        NR = 3
        with ExitStack() as es:
            hnT = self.sb(es, "hnT", [128, 8, 1024], BF16)
            hT = self.sb(es, "hT", [128, NFC, 1024], BF16)
            wd = self.sb(es, "wd", [128, NFC, D], BF16)
            wg = [self.sb(es, f"wg{i}", [128, 8, 128], BF16) for i in range(NR)]
            wu = [self.sb(es, f"wu{i}", [128, 8, 128], BF16) for i in range(NR)]
            sg = self.sb(es, "sg", [128, 2, 512], F32)
            dg = [self.dsem(f"wg{i}") for i in range(NR)]
            du = [self.dsem(f"wu{i}") for i in range(NR)]
            dd = [self.dsem(f"wd{i}") for i in range(4)]
            gi = [0]

            def load_chunk(fc):
                sl = fc % NR
                self.dma("pool", dg[sl], wg[sl][:], wgv[:, :, fc * 128:(fc + 1) * 128], (), [("wg", sl)])
                self.dma("pool", du[sl], wu[sl][:], wuv[:, :, fc * 128:(fc + 1) * 128], (), [("wu", sl)])

            def prologue(tb, banks):
                tiles = [tb * 8 + i for i in range(8)]
                self.pre_stats(tiles)
                for tl, t in enumerate(tiles):
                    self.pre_tile(t, tl, lambda kc, tl=tl: hnT[:, kc, tl * 128:(tl + 1) * 128],
                                  lambda kc, tl=tl: ("hnT", tl, kc), banks)

            def gateup(tb):
                for fc in range(NFC):
                    if fc + NR - 1 < NFC:
                        load_chunk(fc + NR - 1)
                    self.dma("pool", dd[fc % 4], wd[:, fc, :], wdv[:, fc, :], (), [("wd", fc)])
                    sl = fc % NR
                    for nt in range(2):
                        par = gi[0] % 2
                        gi[0] += 1
                        for kc in range(8):
                            rd_h = [("hnT", nt * 4 + q, kc) for q in range(4)]
                            self.mm(ps[:, par, :], wg[sl][:, kc, :], hnT[:, kc, nt * 512:(nt + 1) * 512],
                                    kc == 0, kc == 7, [("wg", sl)] + rd_h, [("ps", par)])
                        for kc in range(8):
                            rd_h = [("hnT", nt * 4 + q, kc) for q in range(4)]
                            self.mm(ps[:, 2 + par, :], wu[sl][:, kc, :], hnT[:, kc, nt * 512:(nt + 1) * 512],
                                    kc == 0, kc == 7, [("wu", sl)] + rd_h, [("ps", 2 + par)])
                        self.act(sg[:, par, :], ps[:, par, :], AF.Silu, [("ps", par)], [("sg", par)])
                        self.tt("dve", hT[:, fc, nt * 512:(nt + 1) * 512], sg[:, par, :], ps[:, 2 + par, :],
                                ALU.mult, [("sg", par), ("ps", 2 + par)], [("hT", fc, nt)])

            def down(tb):
                tiles = [tb * 8 + i for i in range(8)]
                for tl, t in enumerate(tiles):
                    par = tl % 2
                    b0 = 4 + 2 * par
                    for h in range(2):
                        for fc in range(NFC):
                            self.mm(ps[:, b0 + h, :], hT[:, fc, tl * 128:(tl + 1) * 128],
                                    wd[:, fc, h * 512:(h + 1) * 512], fc == 0, fc == NFC - 1,
                                    [("hT", fc, tl // 4), ("wd", fc)], [("ps", b0 + h)])
                            if host_mod and (h, fc) in ((0, 10), (0, 21), (1, 10)):
                                self.psM = (2, 0)
                                self.mod_unit()
                    self.epilogue(t, b0, lambda h: sg[:, h, :], lambda h: ("sg", h), par, last)
                    if host_mod and tl == 7:
                        while self.mod_loaded:
                            self.mod_unit(flush=True)

            for fc in range(NR - 1):
                load_chunk(fc)
            prologue(0, (6, 7))
            gateup(0)
            for fc in range(NR - 1):
                load_chunk(fc)
            prologue(1, (0, 1))
            down(0)
            gateup(1)
            down(1)

    def post_tile(self, t, o_tile, wo, oT, scr, par, last):
        ps = self.ps
        bank = 6 + (self.pst_i % 2)
        self.pst_i += 1
        sl = self.pst_i % 2
        psb = ps[:, bank, :].bitcast(BF16).rearrange("p (k t) -> p k t", k=8)
        for kc in range(8):
            self.tr(psb[:, kc, :], o_tile(kc), ["o_all"], [("ps", bank)])
        if bank == 6:
            self.act(oT[:, sl, :, :], psb, AF.Identity, [("ps", bank)], [("oT", sl)])
        else:
            self.cp("dve", oT[:, sl, :, :], psb, [("ps", bank)], [("oT", sl)])
        b0 = 2 * par
        for h in range(2):
            for kc in range(8):
                self.mm(ps[:, b0 + h, :], oT[:, sl, kc, :], wo[:, kc, h * 512:(h + 1) * 512], kc == 0, kc == 7,
                        [("oT", sl), "wo"], [("ps", b0 + h)])
        self.epilogue(t, b0, lambda h: scr[:, h, :], lambda h: ("scr", h), par, last)

    def mla_phase(self, a, last):
        H = 16
        w_in_d = self.din("mla_w_in", [1, D, 672])
        qn_d = self.din("mla_q_norm", [1, 384])
        wq_d = self.din("mla_w_q_up", [1, 384, 1536])
        kvn_d = self.din("mla_kv_norm", [1, 256])
        wkv_d = self.din("mla_w_kv_up", [1, 256, 2048])
        wo_d = self.din("mla_w_o", [1, D, D])
        cc_d = self.din("rope_cc", [128, NT, 32])
        ss_d = self.din("rope_ss", [128, NT, 32])
        mask_d = self.din("tri_mask", [128, 128], BF16)
        ps = self.ps
        st = self.st
        scale = 96.0 ** -0.5
        with ExitStack() as es0:
            cqnT = self.sb(es0, "cqnT", [128, 3, SQ], BF16)
            ckvnT = self.sb(es0, "ckvnT", [128, 2, SQ], BF16)
            krT = self.sb(es0, "krT", [128, SQ], BF16)
            o_sb = self.sb(es0, "o_sb", [128, NT, D], BF16)
            cc = self.sb(es0, "cc", [128, NT, 32], F32)
            ssn = self.sb(es0, "ssn", [128, NT, 32], F32)
            mask = self.sb(es0, "mask", [128, 128], BF16)
            dm = self.dsem("mla_c", "sp")
            self.dma("sp", dm, cc[:], cc_d, (), ["cc"])
            self.dma("sp", dm, ssn[:], ss_d, (), ["ssn"])
            self.dma("sp", dm, mask[:], mask_d, (), ["mask"])
            with ExitStack() as es:
                win = self.sb(es, "win", [128, 8, 672], BF16)
                hnt = self.sb(es, "hnt", [128, 2, 8, 128], BF16)
                latn = self.sb(es, "latn", [128, 2, 768], BF16)
                qg = self.sb(es, "qg", [128, 384], F32)
                kvg = self.sb(es, "kvg", [128, 256], F32)
                rt = self.sb(es, "rt", [128, 2, 64], F32)
                dwi = [self.dsem("win0"), self.dsem("win1")]
                self.dma("pool", dwi[0], win[:, 0:4, :], w_in_d[a].rearrange("(kc p) f -> p kc f", p=128)[:, 0:4, :],
                         (), [("win", 0)])
                self.dma("pool", dwi[1], win[:, 4:8, :], w_in_d[a].rearrange("(kc p) f -> p kc f", p=128)[:, 4:8, :],
                         (), [("win", 1)])
                self.dma("sp", dm, qg[:], qn_d[a].partition_broadcast(128), (), ["qg"])
                self.dma("sp", dm, kvg[:], kvn_d[a].partition_broadcast(128), (), ["kvg"])
                self.memset("dve", latn[:], 0.0, [("latn", 0), ("latn", 1)])
                self.pre_stats(list(range(NT)))
                def stA(t):
                    sl = t % 2
                    self.pre_tile(t, t, lambda kc, sl=sl: hnt[:, sl, kc, :], lambda kc, sl=sl: ("hnt", sl, kc))

                def stB(t):
                    sl = t % 2
                    bA, bB = (0, 1) if sl == 0 else (2, 3)
                    for kc in range(8):
                        self.mm(ps[:, bA, 0:384], hnt[:, sl, kc, :], win[:, kc, 0:384], kc == 0, kc == 7,
                                [("hnt", sl, kc), ("win", kc // 4)], [("ps", bA)])
                    for kc in range(8):
                        self.mm(ps[:, bB, 0:288], hnt[:, sl, kc, :], win[:, kc, 384:672], kc == 0, kc == 7,
                                [("hnt", sl, kc), ("win", kc // 4)], [("ps", bB)])

                def stC(t):
                    sl = t % 2
                    bA, bB = (0, 1) if sl == 0 else (2, 3)
                    c = 20 + sl * 4
                    js = self.junk_i % 2
                    self.junk_i += 1
                    self.act(self.junk[:, js, 0:384], ps[:, bA, 0:384], AF.Square, [("ps", bA)],
                             [("st", c), ("junk", js)], scale=384.0 ** -0.5, accum=st[:, c:c + 1])
                    js = self.junk_i % 2
                    self.junk_i += 1
                    self.act(self.junk[:, js, 0:256], ps[:, bB, 0:256], AF.Square, [("ps", bB)],
                             [("st", c + 1), ("junk", js)], scale=1.0 / 16.0, accum=st[:, c + 1:c + 2])
                    self.act(st[:, c:c + 2], st[:, c:c + 2], AF.Sqrt, [("st", c), ("st", c + 1), "eps"],
                             [("st", c), ("st", c + 1)], bias=self.epsb[:, 0:1])
                    self.recip(st[:, c + 2:c + 4], st[:, c:c + 2], [("st", c), ("st", c + 1)], [("st", c + 2)])
                    self.stt("dve", latn[:, sl, 0:384], ps[:, bA, 0:384], st[:, c + 2:c + 3], qg[:],
                             ALU.mult, ALU.mult, [("ps", bA), ("st", c + 2), "qg"], [("latn", sl)])
                    self.stt("dve", latn[:, sl, 384:640], ps[:, bB, 0:256], st[:, c + 3:c + 4], kvg[:],
                             ALU.mult, ALU.mult, [("ps", bB), ("st", c + 2), "kvg"], [("latn", sl)])
                    kr = ps[:, bB, 256:288]
                    self.tt("dve", rt[:, sl, 0:32], kr, cc[:, t, :], ALU.mult,
                            [("ps", bB), ("st", c + 2), "cc"], [("rt", sl)])
                    self.tt("dve", rt[:, sl, 32:48], ps[:, bB, 272:288], ssn[:, t, 0:16], ALU.mult,
                            [("ps", bB), "ssn"], [("rt", sl)])
                    self.tt("dve", rt[:, sl, 48:64], ps[:, bB, 256:272], ssn[:, t, 16:32], ALU.mult,
                            [("ps", bB), "ssn"], [("rt", sl)])
                    self.tt("dve", latn[:, sl, 704:736], rt[:, sl, 0:32], rt[:, sl, 32:64], ALU.add,
                            [("rt", sl)], [("latn", sl)])
                    bank = 4 + sl
                    psb = ps[:, bank, :].bitcast(BF16).rearrange("p (k t) -> p k t", k=8)
                    for k6 in range(6):
                        self.tr(psb[:, k6, :], latn[:, sl, k6 * 128:(k6 + 1) * 128], [("latn", sl), "ident"],
                                [("ps", bank)])
                    tc = slice(t * 128, (t + 1) * 128)
                    self.act(cqnT[:, :, tc], psb[:, 0:3, :], AF.Identity, [("ps", bank)], [("cqnT", t)])
                    self.act(ckvnT[:, :, tc], psb[:, 3:5, :], AF.Identity, [("ps", bank)], [("ckvnT", t)])
                    self.act(krT[64:96, tc], psb[64:96, 5, :], AF.Identity, [("ps", bank)], [("krT", t)])

                for step in range(NT + 2):
                    if step < NT:
                        stA(step)
                    if 0 <= step - 1 < NT:
                        stB(step - 1)
                    if 0 <= step - 2 < NT:
                        stC(step - 2)
                self.S.barrier()
            with ExitStack() as es:
                qT = self.sb(es, "qT", [128, 4, SQ], BF16)
                kT = self.sb(es, "kT", [128, 4, SQ], BF16)
                V = self.sb(es, "V", [128, NT, 4, 65], BF16)
                wq = self.sb(es, "wq", [128, 3, 384], BF16)
                wkk = self.sb(es, "wkk", [128, 2, 4, 64], BF16)
                wkv = self.sb(es, "wkv", [128, 2, 4, 64], BF16)
                qb = self.sb(es, "qb", [128, 2, 4, 96], BF16)
                rq = self.sb(es, "rq", [128, 2, 4, 64], F32)
                pT = self.sb(es, "pT", [128, 5, 512], BF16)
                rz = self.sb(es, "rz", [128, 2, 4], F32)
                dq = self.dsem("wq")
                dk = self.dsem("wkk")
                dvv = self.dsem("wkv")
                self.memset("pool", V[:], 1.0, [("V", kt) for kt in range(NT)])
                wqv = wq_d[a].rearrange("(kc p) f -> p kc f", p=128)
                wkvv = wkv_d[a].rearrange("(kc p) (h c) -> p kc h c", p=128, c=128)
                for h4 in range(4):
                    self.cp("pool" if h4 % 2 == 0 else "dve", kT[64:96, h4, :], krT[64:96, :], [], [("kTr", h4)])
                for hg in range(4):
                    self.dma("pool", dq, wq[:], wqv[:, :, hg * 384:(hg + 1) * 384], (), ["wq"])
                    for kc in range(2):
                        self.dma("pool", dk, wkk[:, kc], wkvv[:, kc, hg * 4:(hg + 1) * 4, 0:64], (), ["wkk"])
                        self.dma("pool", dvv, wkv[:, kc], wkvv[:, kc, hg * 4:(hg + 1) * 4, 64:128], (), ["wkv"])
                    for h4 in range(4):
                        for cch in range(4):
                            bank = 2 + (cch % 2)
                            for kc in range(2):
                                self.mm(ps[0:64, bank, :], wkk[:, kc, h4, :], ckvnT[:, kc, cch * 512:(cch + 1) * 512],
                                        kc == 0, kc == 1, ["wkk"], [("ps", bank)])
                            self.act(kT[0:64, h4, cch * 512:(cch + 1) * 512], ps[0:64, bank, :], AF.Identity,
                                     [("ps", bank)], [("kTn", h4, cch)])
                    for kt in range(NT):
                        bank = 6 + (kt % 2)
                        for kc in range(2):
                            self.mm(ps[:, bank, 0:256], ckvnT[:, kc, kt * 128:(kt + 1) * 128],
                                    wkv[:, kc, :, :].rearrange("p h c -> p (h c)"), kc == 0, kc == 1,
                                    ["wkv"], [("ps", bank)])
                        self.act(V[:, kt, :, 0:64], ps[:, bank, 0:256].rearrange("p (h c) -> p h c", c=64),
                                 AF.Identity, [("ps", bank)], [("V", kt)])
                    def q1(t):
                        sl = t % 2
                        bank = sl
                        for kc in range(3):
                            self.mm(ps[:, bank, 0:384], cqnT[:, kc, t * 128:(t + 1) * 128], wq[:, kc, :],
                                    kc == 0, kc == 2, ["wq"], [("ps", bank)])

                    def q2(t):
                        sl = t % 2
                        bank = sl
                        pq = ps[:, bank, 0:384].rearrange("p (h c) -> p h c", c=96)
                        ccb = cc[:, t, :].unsqueeze(1).to_broadcast([128, 4, 32])
                        s0 = ssn[:, t, 0:16].unsqueeze(1).to_broadcast([128, 4, 16])
                        s1 = ssn[:, t, 16:32].unsqueeze(1).to_broadcast([128, 4, 16])
                        self.cp("dve", qb[:, sl, :, 0:64], pq[:, :, 0:64], [("ps", bank)], [("qb", sl)])
                        self.tt("dve", rq[:, sl, :, 0:32], pq[:, :, 64:96], ccb, ALU.mult, [("ps", bank), "cc"],
                                [("rq", sl)])
                        self.tt("dve", rq[:, sl, :, 32:48], pq[:, :, 80:96], s0, ALU.mult, [("ps", bank), "ssn"],
                                [("rq", sl)])
                        self.tt("dve", rq[:, sl, :, 48:64], pq[:, :, 64:80], s1, ALU.mult, [("ps", bank), "ssn"],
                                [("rq", sl)])
                        self.tt("dve", qb[:, sl, :, 64:96], rq[:, sl, :, 0:32], rq[:, sl, :, 32:64], ALU.add,
                                [("rq", sl)], [("qb", sl)])
                        bk = 4 + sl
                        psb = ps[:, bk, :].bitcast(BF16).rearrange("p (k t) -> p k t", k=8)
                        for h4 in range(4):
                            self.tr(psb[0:96, h4, :], qb[:, sl, h4, :], [("qb", sl), "ident"], [("ps", bk)])
                        if sl == 0:
                            self.act(qT[0:96, :, t * 128:(t + 1) * 128], psb[0:96, 0:4, :], AF.Identity,
                                     [("ps", bk)], [("qT", t)])
                        else:
                            self.cp("dve", qT[0:96, :, t * 128:(t + 1) * 128], psb[0:96, 0:4, :],
                                    [("ps", bk)], [("qT", t)])

                    for step in range(NT + 1):
                        if step < NT:
                            q1(step)
                        if step >= 1:
                            q2(step - 1)
                    items = [(qc, h4, kt) for qc in range(4) for h4 in range(4) for kt in range(qc * 4 + 4)]
                    LA = 4
                    SBK = (0, 1, 2, 6, 7)
                    slots = {}

                    def score_item(i):
                        qc, h4, kt = items[i]
                        jl0 = max(0, kt - qc * 4)
                        n0 = jl0 * 128
                        sl = i % 5
                        slots[i] = sl
                        bk = SBK[sl]
                        self.mm(ps[:, bk, n0:512], kT[0:96, h4, kt * 128:(kt + 1) * 128],
                                qT[0:96, h4, qc * 512 + n0:(qc + 1) * 512], True, True,
                                [("kTn", h4, kt // 4), ("kTr", h4)] + [("qT", qc * 4 + j) for j in range(jl0, 4)],
                                [("ps", bk)])
                        self.act(pT[:, sl, n0:512], ps[:, bk, n0:512], AF.Exp, [("ps", bk)], [("pT", sl)],
                                 scale=scale)
                        if kt >= qc * 4:
                            self.tt("dve", pT[:, sl, n0:n0 + 128], pT[:, sl, n0:n0 + 128], mask[:],
                                    ALU.mult, [("pT", sl), "mask"], [("pT", sl)])

                    def pv_item(i):
                        qc, h4, kt = items[i]
                        h = hg * 4 + h4
                        nkt = qc * 4 + 4
                        jl0 = max(0, kt - qc * 4)
                        sl = slots[i]
                        gidx = qc * 4 + h4
                        ob = 3 + (gidx % 2)
                        orz = gidx % 2
                        for jl in range(jl0, 4):
                            self.mm(ps[:, ob, jl * 65:(jl + 1) * 65], pT[:, sl, jl * 128:(jl + 1) * 128],
                                    V[:, kt, h4, :], kt == 0 and jl == 0, kt == nkt - 1 and jl == 3,
                                    [("pT", sl), ("V", kt)], [("ps", ob)])
                        if kt == nkt - 1:
                            po = ps[:, ob, 0:260].rearrange("p (j c) -> p j c", c=65)
                            self.recip(rz[:, orz, :].unsqueeze(2), po[:, :, 64:65], [("ps", ob)], [("rz", orz)])
                            self.tt("dve", o_sb[:, qc * 4:(qc + 1) * 4, h * 64:(h + 1) * 64], po[:, :, 0:64],
                                    rz[:, orz, :].unsqueeze(2).to_broadcast([128, 4, 64]), ALU.mult,
                                    [("ps", ob), ("rz", orz)], [("o", qc, h)])

                    for i in range(min(LA, len(items))):
                        score_item(i)
                    self.psM = (5, 256)
                    for i in range(len(items)):
                        if i + LA < len(items):
                            score_item(i + LA)
                        pv_item(i)
                        if i % 6 == 5:
                            self.mod_unit()
                    if hg == 3:
                        while self.mod_loaded:
                            self.mod_unit(flush=True)
                self.S.barrier()
            with ExitStack() as es:
                wo = self.sb(es, "wo", [128, 8, D], BF16)
                oT = self.sb(es, "oT", [128, 2, 8, 128], BF16)
                scr = self.sb(es, "scr", [128, 2, 512], F32)
                dwo = [self.dsem("wo0"), self.dsem("wo1")]
                wov = wo_d[a].rearrange("(kc p) f -> p kc f", p=128)
                for i in range(2):
                    self.dma("pool", dwo[i], wo[:, i * 4:(i + 1) * 4, :], wov[:, i * 4:(i + 1) * 4, :], (), ["wo"])
                for t in range(NT):
                    self.post_tile(t, lambda kc, t=t: o_sb[:, t, kc * 128:(kc + 1) * 128], wo, oT, scr, t % 2, last)
                self.S.barrier()

    def dil_phase(self, b, last):
        DIL = (1, 4, 16)
        w_in_d = self.din("dil_w_in", [1, D, 9 * D])
        wo_d = self.din("dil_w_o", [1, D, D])
        eb_d = self.din("dil_eb", [3, 8, 128, 512])
        ps = self.ps
        wv_all = w_in_d[b].rearrange("(kc p) f -> p kc f", p=128)

        def tok(T, rows, g, r, n):
            dl = DIL[g]
            s0 = r + dl * n * 128
            return T[rows, s0:s0 + dl * 127 + 1:dl]

        with ExitStack() as es0:
            oT = self.sb(es0, "oT", [128, 8, SQ], BF16)
            with ExitStack() as es:
                hnT = self.sb(es, "hnT", [128, 8, SQ], BF16)
                Ut = self.sb(es, "Ut", [128, SQ], F32)
                Zt = self.sb(es, "Zt", [128, SQ], F32)
                qz = [self.sb(es, f"qz{e}", [128, SQ], BF16) for e in range(2)]
                kT = self.sb(es, "kT", [128, SQ], BF16)
                V0 = self.sb(es, "V0", [128, 16, 128], BF16)
                V1 = self.sb(es, "V1", [128, 16, 128], BF16)
                ones0 = self.sb(es, "ones0", [128, 128], BF16)
                ones1 = self.sb(es, "ones1", [128, 128], BF16)
                wr = [[self.sb(es, f"dw{i}_{s}", [128, 8, 128], BF16) for s in range(3)] for i in range(2)]
                eb = self.sb(es, "eb", [128, 2, 512], BF16)
                ebr = self.sb(es, "ebr", [128, 2, 512], F32)
                pT = self.sb(es, "pT", [128, 4, 512], BF16)
                dwq = [[self.dsem(f"dw{i}{s}") for s in range(3)] for i in range(2)]
                deb = [self.dsem("eb0", "sp"), self.dsem("eb1", "sp")]
                self.memset("pool", qz[0][:], 0.0, ["qz"])
                self.memset("pool", qz[1][:], 0.0, ["qz"])
                self.memset("dve", V0[:], 0.0, ["V0"])
                self.memset("dve", V1[:], 0.0, ["V1"])
                self.memset("dve", ones0[:], 0.0, ["ones"])
                self.memset("dve", ones1[:], 0.0, ["ones"])
                self.memset("dve", ones0[:, 0:64], 1.0, ["ones"])
                self.memset("dve", ones1[:, 64:128], 1.0, ["ones"])
                self.pre_stats(list(range(NT)))
                for t in range(NT):
                    self.pre_tile(t, t, lambda kc, t=t: hnT[:, kc, t * 128:(t + 1) * 128],
                                  lambda kc, t=t: ("hnT", t, kc))
                it = 0
                sbi = 0
                ubi = 0

                def ld_w(hp_, g_, wi_):
                    for s_ in range(3):
                        c0 = (g_ * 3 + s_) * D + hp_ * 128
                        self.dma("pool", dwq[wi_][s_], wr[wi_][s_][:], wv_all[:, :, c0:c0 + 128], (),
                                 [("dw", wi_, s_)])
                    self.dma("sp", deb[wi_], ebr[:, wi_, :], eb_d[g_, hp_], (), [("ebr", wi_)])
                    self.act(eb[:, wi_, :], ebr[:, wi_, :], AF.Exp, [("ebr", wi_)], [("eb", wi_)])

                n_it = 24
                dbg = ""
                for hp in range(8):
                    for g in range(3):
                        dl = DIL[g]
                        nb = 16 // dl
                        wi = it % 2
                        it += 1
                        if it == 1:
                            ld_w(hp, g, wi)
                        nxt = hp * 3 + g + 1
                        if nxt < n_it:
                            ld_w(nxt // 3, nxt % 3, 1 - wi)
                        for s in range(2):
                            for cch in range(4):
                                bank = cch % 2
                                for kc in range(8):
                                    self.mm(ps[:, bank, :], wr[wi][s][:, kc, :], hnT[:, kc, cch * 512:(cch + 1) * 512],
                                            kc == 0, kc == 7, [("dw", wi, s)] + [("hnT", cch * 4 + q, kc) for q in range(4)],
                                            [("ps", bank)])
                                cs_ = slice(cch * 512, (cch + 1) * 512)
                                if s == 0:
                                    self.act(qz[0][0:64, cs_], ps[0:64, bank, :], AF.Identity,
                                             [("ps", bank), "qz"], [("qk", 0, cch)], scale=0.125)
                                    self.act(qz[1][64:128, cs_], ps[64:128, bank, :], AF.Identity,
                                             [("ps", bank), "qz"], [("qk", 0, cch)], scale=0.125)
                                else:
                                    self.act(kT[:, cs_], ps[:, bank, :], AF.Identity,
                                             [("ps", bank)], [("qk", 1, cch)])
                        for b4 in range(0 if dbg.startswith("dil_a") else 4):
                            bank = 2 + (b4 % 2)
                            for bb in range(4):
                                bi = b4 * 4 + bb
                                r, n = bi // nb, bi % nb
                                for kc in range(8):
                                    self.mm(ps[:, bank, bb * 128:(bb + 1) * 128], tok(hnT[:, kc, :], slice(0, 128), g, r, n),
                                            wr[wi][2][:, kc, :], kc == 0, kc == 7,
                                            [("dw", wi, 2)] + [("hnT", q, kc) for q in range(NT)], [("ps", bank)])
                            pv = ps[:, bank, :].rearrange("p (b c) -> p b c", c=128)
                            self.act(V0[:, b4 * 4:(b4 + 1) * 4, 0:64], pv[:, :, 0:64], AF.Identity, [("ps", bank)],
                                     [("V", b4)])
                            self.act(V1[:, b4 * 4:(b4 + 1) * 4, 64:128], pv[:, :, 64:128], AF.Identity, [("ps", bank)],
                                     [("V", b4)])

                        SB = (4, 5, 0, 1)

                        def scores(bi):
                            nonlocal sbi
                            r, n = bi // nb, bi % nb
                            sl = sbi % 4
                            sbi += 1
                            bank = SB[sl]
                            kb0 = 0 if n > 0 else 1
                            allr = slice(0, 128)
                            for e in range(2):
                                for kb in range(kb0, 2):
                                    c0 = (e * 2 + kb) * 128
                                    self.mm(ps[:, bank, c0:c0 + 128], tok(kT, allr, g, r, n - 1 + kb),
                                            tok(qz[e], allr, g, r, n), True, True,
                                            [("qk", 0, q) for q in range(4)] + [("qk", 1, q) for q in range(4)],
                                            [("ps", bank)])
                            if kb0 == 0:
                                self.act(pT[:, sl, :], ps[:, bank, :], AF.Exp, [("ps", bank)], [("pT", sl)])
                                self.tt("dve", pT[:, sl, :], pT[:, sl, :], eb[:, wi, :], ALU.mult,
                                        [("pT", sl), ("eb", wi)], [("pT", sl)])
                            else:
                                v4 = lambda a: a.rearrange("p (e k q) -> p e k q", e=2, k=2)[:, :, 1, :]
                                self.act(v4(pT[:, sl, :]), v4(ps[:, bank, :]), AF.Exp, [("ps", bank)], [("pT", sl)])
                                self.tt("dve", v4(pT[:, sl, :]), v4(pT[:, sl, :]), v4(eb[:, wi, :]), ALU.mult,
                                        [("pT", sl), ("eb", wi)], [("pT", sl)])
                            return sl, kb0

                        def pv_block(bi, sl, kb0, ub, zb):
                            n = bi % nb
                            bb = bi % 4
                            seq = [(e, kb) for e in range(2) for kb in range(kb0, 2)]
                            for i, (e, kb) in enumerate(seq):
                                c0 = (e * 2 + kb) * 128
                                Vs = V0 if e == 0 else V1
                                self.mm(ps[:, ub, bb * 128:(bb + 1) * 128], Vs[:, bi - 1 + kb, :],
                                        pT[:, sl, c0:c0 + 128], i == 0, i == len(seq) - 1,
                                        [("pT", sl), ("V", (bi - 1 + kb) // 4)], [("ps", ub)])
                            for i, (e, kb) in enumerate(seq):
                                c0 = (e * 2 + kb) * 128
                                on = ones0 if e == 0 else ones1
                                self.mm(ps[:, zb, bb * 128:(bb + 1) * 128], on[:], pT[:, sl, c0:c0 + 128],
                                        i == 0, i == len(seq) - 1, [("pT", sl), "ones"], [("ps", zb)])

                        def dst_view(T, b4):
                            if g == 0:
                                return T[:, b4 * 512:(b4 + 1) * 512]
                            if g == 1:
                                return T[:, b4:b4 + 4 * 511 + 1:4]
                            return T[:, :].rearrange("p (m r) -> p r m", r=16)[:, b4 * 4:(b4 + 1) * 4, :]

                        if dbg.startswith("dil_a") or dbg.startswith("dil_b"):
                            continue
                        LA = 3
                        pend = [scores(i) for i in range(LA)]
                        for bi in range(16):
                            b4 = bi // 4
                            if bi % 4 == 0:
                                ub = 6 + (ubi % 2)
                                zb = 2 + (ubi % 2)
                                ubi += 1
                            cur = pend.pop(0)
                            if bi + LA < 16:
                                pend.append(scores(bi + LA))
                            if dbg.startswith("dil_c"):
                                continue
                            pv_block(bi, cur[0], cur[1], ub, zb)
                            if dbg.startswith("dil_d"):
                                continue
                            if bi % 4 == 3:
                                pu = ps[:, ub, :]
                                pz = ps[:, zb, :]
                                if g == 2:
                                    pu = pu.rearrange("p (r m) -> p r m", r=4)
                                    pz = pz.rearrange("p (r m) -> p r m", r=4)
                                du = dst_view(Ut, b4)
                                dz = dst_view(Zt, b4)
                                if g == 0:
                                    self.cp("dve", du, pu, [("ps", ub)], [("Ut", b4)])
                                    self.cp("dve", dz, pz, [("ps", zb)], [("Zt", b4)])
                                else:
                                    self.tt("dve", du, pu, du, ALU.add, [("ps", ub)] + [("Ut", q) for q in range(4)],
                                            [("Ut", q) for q in range(4)])
                                    self.tt("dve", dz, pz, dz, ALU.add, [("ps", zb)] + [("Zt", q) for q in range(4)],
                                            [("Zt", q) for q in range(4)])
                    for q in range(4):
                        cs_ = slice(q * 512, (q + 1) * 512)
                        self.recip(Zt[:, cs_], Zt[:, cs_], [("Zt", k) for k in range(4)], [("Zt", k) for k in range(4)])
                        self.tt("dve", oT[:, hp, cs_], Ut[:, cs_], Zt[:, cs_], ALU.mult,
                                [("Ut", k) for k in range(4)] + [("Zt", k) for k in range(4)], [("oT", hp)])
                self.S.barrier()
            with ExitStack() as es:
                wo = self.sb(es, "wo", [128, 8, D], BF16)
                scr = self.sb(es, "scr", [128, 2, 512], F32)
                dwo = [self.dsem("wo0"), self.dsem("wo1")]
                wov = wo_d[b].rearrange("(kc p) f -> p kc f", p=128)
                for i in range(2):
                    self.dma("pool", dwo[i], wo[:, i * 4:(i + 1) * 4, :], wov[:, i * 4:(i + 1) * 4, :], (), ["wo"])
                for t in range(NT):
                    par = t % 2
                    b0 = 2 * par
                    for h in range(2):
                        for kc in range(8):
                            self.mm(ps[:, b0 + h, :], oT[:, kc, t * 128:(t + 1) * 128], wo[:, kc, h * 512:(h + 1) * 512],
                                    kc == 0, kc == 7, ["wo"], [("ps", b0 + h)])
                    self.epilogue(t, b0, lambda h: scr[:, h, :], lambda h: ("scr", h), par, last)
                self.S.barrier()


def _host_consts(inp):
    b_mod = np.asarray(inp["b_mod"], np.float32).reshape(2, 3, 3, D)
    ss = b_mod[:, :, 0:2, :].reshape(2, 3, 16, 128)
    bm_fm = np.ascontiguousarray(ss.transpose(0, 1, 3, 2))
    bm_gate = np.ascontiguousarray(b_mod[:, :, 2, :])
    pre = np.asarray(inp["norm_pre"], np.float32).reshape(2, 3, 8, 128)
    pre_fm = np.ascontiguousarray(pre.transpose(0, 1, 3, 2))
    pos = np.arange(SQ, dtype=np.float32)
    freqs = (np.float32(10000.0) ** (-(np.arange(16, dtype=np.float32) / np.float32(16)))).astype(np.float32)
    ang = (pos[:, None] * freqs[None, :]).astype(np.float32)
    cos = np.cos(ang).astype(np.float32)
    sin = np.sin(ang).astype(np.float32)
    cc = np.concatenate([cos, cos], axis=1).reshape(NT, 128, 32).transpose(1, 0, 2)
    ssn = np.concatenate([-sin, sin], axis=1).reshape(NT, 128, 32).transpose(1, 0, 2)
    tri = (np.arange(128)[:, None] <= np.arange(128)[None, :]).astype(np.float32)
    rb = np.asarray(inp["rel_bias"], np.float32)
    kk = np.arange(128)[:, None]
    qq = np.arange(128)[None, :]
    eb = np.full((3, 8, 128, 2, 2, 128), -30000.0, np.float32)
    for g, dl in enumerate((1, 4, 16)):
        for kb in range(2):
            rel = 128 + qq - (kb * 128 + kk)
            valid = (rel >= 0) & (rel <= 128)
            dist = np.maximum(rel, 0) * dl
            dd = np.maximum(dist, 1).astype(np.float32)
            large = 16 + (np.log(dd / np.float32(16)) / np.float32(math.log(2048 / 16))
                          * np.float32(16)).astype(np.int32)
            large = np.minimum(large, 31)
            bucket = np.where(dist < 16, dist, large)
            for hp in range(8):
                for e in range(2):
                    col = g * 16 + hp * 2 + e
                    eb[g, hp, :, e, kb, :] = np.where(valid, rb[bucket, col], np.float32(-30000.0))
    npost = np.asarray(inp["norm_post"], np.float32).reshape(2, 3, 8, 128)
    modc = np.zeros((128, 6, 48), np.float32)
    for l in range(2):
        for j in range(3):
            si = l * 3 + j
            for part in range(3):
                modc[:, si, part * 8:(part + 1) * 8] = b_mod[l, j, part].reshape(8, 128).T
            modc[:, si, 24:32] = pre[l, j].T
            modc[:, si, 32:40] = npost[l, j].T
    return {
        "mod_consts": modc,
        "rope_cc": np.ascontiguousarray(cc), "rope_ss": np.ascontiguousarray(ssn),
        "tri_mask": tri.astype(ml_dtypes.bfloat16),
        "dil_eb": np.ascontiguousarray(eb.reshape(3, 8, 128, 512)),
        "bm_fm": bm_fm, "bm_gate": bm_gate, "pre_fm": pre_fm,
        "post_g": np.ascontiguousarray(np.asarray(inp["norm_post"], np.float32)),
        "ident": np.eye(128, dtype=np.float32).astype(ml_dtypes.bfloat16),
    }


_PROG_CACHE = {}
TRACE = False


def run_sublayers(inp, xs, sublayers, dbg=None):
    key = (tuple(sublayers), dbg)
    if key not in _PROG_CACHE:
        p = Prog()
        nc = p.build(list(sublayers), dbg=dbg)
        _PROG_CACHE[key] = (p, nc)
    p, nc = _PROG_CACHE[key]
    consts = _host_consts(inp)
    c = np.asarray(inp["c"], np.float32)
    in_maps = []
    for b in range(N_CORES):
        m = {}
        for name in p.inputs:
            if name == "x":
                m[name] = np.ascontiguousarray(xs[b])
            elif name == "cT":
                m[name] = np.ascontiguousarray(c[b].reshape(8, 128).T)
            elif name in consts:
                m[name] = consts[name]
            else:
                m[name] = np.ascontiguousarray(np.asarray(inp[name], np.float32))
        in_maps.append(m)
    if TRACE:
        res = run_bass_kernel_spmd(nc, in_maps, core_ids=list(range(N_CORES)), trace=True)
        print("EXEC_NS", res.exec_time_ns)
    else:
        res = run_bass_kernel_spmd(nc, in_maps, core_ids=list(range(N_CORES)))
    return np.stack([np.asarray(r["y"]) for r in res.results], axis=0)


ALL_SUBLAYERS = [(0, 0), (0, 1), (0, 2), (1, 0), (1, 1), (1, 2)]


def kernel(**inputs):
    xs = np.asarray(inputs["x"], np.float32)
    out = run_sublayers(inputs, xs, ALL_SUBLAYERS)
    return out.astype(np.float32)
```

```python
import math
from contextlib import ExitStack

import numpy as np
import ml_dtypes

import concourse.bass as bass
import concourse.mybir as mybir
from concourse.bass_utils import run_bass_kernel_spmd

F32 = mybir.dt.float32
BF16 = mybir.dt.bfloat16
AF = mybir.ActivationFunctionType
ALU = mybir.AluOpType

ENGS = ("pe", "act", "dve", "pool", "sp")

D = 1024
SQ = 2048
NT = 16
DFF = 2816
NFC = 22
EPS = 1e-6
N_CORES = 8


class DSem:
    def __init__(self, handle, name):
        self.h = handle
        self.name = name
        self.count = 0


class Op:
    __slots__ = ("eng", "fn", "waits", "inc", "cs", "pos", "dsem")


class Sched:
    def __init__(self):
        self.streams = {e: [] for e in ENGS}
        self.ncomp = {e: 0 for e in ENGS}
        self.last_w = {}
        self.readers = {}
        self.waited = {e: {} for e in ENGS}
        self.comp_ops = {e: [] for e in ENGS}
        self.dsems = []
        self.pending = {e: None for e in ENGS}

    def _deps(self, eng, reads, writes):
        need = {}

        def add(d):
            if d is None:
                return
            cs, pos = d
            if need.get(cs, 0) < pos:
                need[cs] = pos

        if self.pending[eng] is not None:
            for cs, pos in self.pending[eng].items():
                add((cs, pos))
            self.pending[eng] = None
        for t in reads:
            add(self.last_w.get(t))
        for t in writes:
            add(self.last_w.get(t))
            for r in self.readers.get(t, ()):
                add(r)
        waits = []
        wd = self.waited[eng]
        for cs, pos in need.items():
            if cs == "pe" and eng == "pe":
                continue
            if wd.get(cs, 0) >= pos:
                continue
            wd[cs] = pos
            waits.append((cs, pos))
            if isinstance(cs, str):
                self.comp_ops[cs][pos - 1].inc = True
        return waits

    def _commit(self, me, reads, writes):
        for t in reads:
            self.readers.setdefault(t, []).append(me)
        for t in writes:
            self.last_w[t] = me
            self.readers[t] = []

    def op(self, eng, fn, reads=(), writes=()):
        o = Op()
        o.eng = eng
        o.fn = fn
        o.waits = self._deps(eng, reads, writes)
        o.inc = False
        o.dsem = None
        self.ncomp[eng] += 1
        o.cs = eng
        o.pos = self.ncomp[eng]
        self.comp_ops[eng].append(o)
        self.streams[eng].append(o)
        self._commit((o.cs, o.pos), reads, writes)
        return o

    def dma(self, eng, dsem, fn, reads=(), writes=()):
        o = Op()
        o.eng = eng
        o.fn = fn
        o.waits = self._deps(eng, reads, writes)
        if dsem.count > 0 and self.waited[eng].get(dsem, 0) < dsem.count:
            self.waited[eng][dsem] = dsem.count
            o.waits.append((dsem, dsem.count))
        o.inc = True
        o.dsem = dsem
        dsem.count += 1
        o.cs = dsem
        o.pos = dsem.count
        self.streams[eng].append(o)
        self._commit((o.cs, o.pos), reads, writes)
        return o

    def barrier(self):
        front = {}
        for e in ENGS:
            if self.ncomp[e] > 0:
                front[e] = self.ncomp[e]
        for d in self.dsems:
            if d.count > 0:
                front[d] = d.count
        for e in ENGS:
            p = dict(front)
            if self.pending[e] is not None:
                for k, v in self.pending[e].items():
                    if p.get(k, 0) < v:
                        p[k] = v
            self.pending[e] = p

    def emit(self, block, sems, final_dsems):
        val = {}
        for e in ENGS:
            c = 0
            for o in self.comp_ops[e]:
                if o.inc:
                    c += 1
                    val[(e, o.pos)] = c

        def wait(engobj, cs, pos):
            if isinstance(cs, str):
                engobj.wait_ge(sems[cs], val[(cs, pos)])
            else:
                engobj.wait_ge(cs.h, 16 * pos)

        def run(e, engobj):
            for o in self.streams[e]:
                for cs, pos in o.waits:
                    wait(engobj, cs, pos)
                ins = o.fn(engobj)
                if o.dsem is not None:
                    ins.then_inc(o.dsem.h, 16)
                elif o.inc:
                    ins.then_inc(sems[e], 1)
            if e == "sp":
                for d in final_dsems:
                    engobj.wait_ge(d.h, 16 * d.count)

        @block.tensor
        def _(eng):
            run("pe", eng)

        @block.scalar
        def _(eng):
            run("act", eng)

        @block.vector
        def _(eng):
            run("dve", eng)

        @block.gpsimd
        def _(eng):
            run("pool", eng)

        @block.sync
        def _(eng):
            run("sp", eng)


class Prog:
    def __init__(self):
        self.nc = bass.Bass("TRN2", target_bir_lowering=False)
        self.S = Sched()
        self.inputs = {}
        self.es = ExitStack()
        self.nds = 0
        self.ds_pool = {}
        self.ds_taken = None

    def din(self, name, shape, dtype=F32):
        if name not in self.inputs:
            self.inputs[name] = self.nc.dram_tensor(name, list(shape), dtype, kind="ExternalInput").ap()
        return self.inputs[name]

    def sb(self, es, name, shape, dtype):
        self.nsb = getattr(self, "nsb", 0) + 1
        return es.enter_context(self.nc.sbuf_tensor(f"{name}_{self.nsb}", list(shape), dtype))

    def dsem(self, name, q="pool"):
        pool = self.ds_pool.setdefault(q, [])
        if pool:
            d = pool.pop()
        else:
            self.nds += 1
            d = DSem(self.es.enter_context(self.nc.semaphore(f"dsem{self.nds}")), name)
            d.q = q
            self.S.dsems.append(d)
        if self.ds_taken is not None:
            self.ds_taken.append(d)
        return d

    def phase_begin(self):
        self.ds_taken = []

    def phase_end(self):
        self.S.barrier()
        for d in self.ds_taken:
            self.ds_pool[d.q].append(d)
        self.ds_taken = None

    def mm(self, out, lhsT, rhs, start, stop, r, w):
        self.S.op("pe", lambda e: e.matmul(out=out, lhsT=lhsT, rhs=rhs, start=start, stop=stop), r, w)

    def tr(self, out, in_, r, w):
        ident = self.ident[:]
        self.S.op("pe", lambda e: e.transpose(out=out, in_=in_, identity=ident), r, w)

    def act(self, out, in_, func, r, w, bias=None, scale=None, accum=None, eng="act"):
        kw = {}
        if bias is not None:
            kw["bias"] = bias
        if scale is not None:
            kw["scale"] = scale
        if accum is not None:
            kw["accum_out"] = accum
        self.S.op("act", lambda e: e.activation(out=out, in_=in_, func=func, **kw), r, w)

    def ts(self, eng, out, in0, s1, s2, op0, op1, r, w):
        if s2 is None:
            self.S.op(eng, lambda e: e.tensor_scalar(out=out, in0=in0, scalar1=s1, scalar2=None, op0=op0), r, w)
        else:
            self.S.op(eng, lambda e: e.tensor_scalar(out=out, in0=in0, scalar1=s1, scalar2=s2, op0=op0, op1=op1),
                      r, w)

    def tt(self, eng, out, in0, in1, op, r, w):
        self.S.op(eng, lambda e: e.tensor_tensor(out=out, in0=in0, in1=in1, op=op), r, w)

    def stt(self, eng, out, in0, scalar, in1, op0, op1, r, w):
        self.S.op(eng, lambda e: e.scalar_tensor_tensor(out=out, in0=in0, scalar=scalar, in1=in1, op0=op0, op1=op1),
                  r, w)

    def cp(self, eng, out, in_, r, w):
        self.S.op(eng, lambda e: e.tensor_copy(out=out, in_=in_), r, w)

    def recip(self, out, in_, r, w):
        self.S.op("dve", lambda e: e.reciprocal(out=out, in_=in_), r, w)

    def memset(self, eng, ap, v, w):
        self.S.op(eng, lambda e: e.memset(ap, v), (), w)

    def dma(self, eng, dsem, out, in_, r, w):
        assert dsem.q == eng, (dsem.name, dsem.q, eng)
        return self.S.dma(eng, dsem, lambda e: e.dma_start(out=out, in_=in_), r, w)

    def build(self, sublayers, final_store=True, dbg=None):
        nc = self.nc
        es = self.es
        P = self
        self.dbg = dbg
        with es:
            x_d = self.din("x", [SQ, D])
            cT_d = self.din("cT", [128, 8])
            id_d = self.din("ident", [128, 128], BF16)
            self.y_d = nc.dram_tensor("y", [SQ, D], F32, kind="ExternalOutput").ap()

            self.x_sb = self.sb(es, "x_sb", [128, NT, D], F32)
            self.ident = self.sb(es, "ident_sb", [128, 128], BF16)
            self.cs = self.sb(es, "cs", [128, 8], F32)
            self.cs_bf = self.sb(es, "cs_bf", [128, 8], BF16)
            self.cs_bc = self.sb(es, "cs_bc", [128, 8, 128], BF16)
            self.A_fm = self.sb(es, "A_fm", [128, 8], F32)
            self.B_fm = self.sb(es, "B_fm", [128, 8], F32)
            self.C = self.sb(es, "C_bc", [128, D], F32)
            self.epsb = self.sb(es, "epsb", [128, 1], F32)
            self.junk = self.sb(es, "junk", [128, 2, D], BF16)
            self.junk_i = 0
            self.xn = self.sb(es, "xn", [128, 2, D], BF16)
            self.st = self.sb(es, "st", [128, 64], F32)
            self.ps = es.enter_context(nc.psum_tensor("ps", [128, 8, 512], F32))
            self.sems = {e: es.enter_context(nc.semaphore(f"s_{e}")) for e in ENGS}
            self.out_ds = [self.dsem(f"out{i}", "sp") for i in range(4)]
            self.modc = self.sb(es, "modc", [128, 6, 48], F32)
            self.modres = self.sb(es, "modres", [128, 6, 24], F32)
            self.wmr = [self.sb(es, f"wmr{i}", [128, 8, 128], BF16) for i in range(2)]
            self.dwm = [self.dsem(f"wmr{i}") for i in range(2)]
            self.ones_f = self.sb(es, "ones_f", [128, 128], F32)
            self.ident_f = self.sb(es, "ident_f", [128, 128], F32)
            self.wm_i = 0
            self.mod_pending = []
            self.mod_loaded = []
            self.mod_step = 0
            self.mod_emitted = {}
            self.psM = None
            self.xn_i = 0
            self.pst_i = 0

            ld = [self.dsem(f"ldx{i}", "sp") for i in range(4)]
            for t in range(NT):
                self.dma("sp", ld[t % 4], self.x_sb[:, t, :], x_d[t * 128:(t + 1) * 128, :], (), [("x", t)])
            dmisc = self.dsem("misc", "sp")
            self.dma("sp", dmisc, self.ident[:], id_d, (), ["ident"])
            self.dma("sp", dmisc, self.cs[:], cT_d, (), ["cs"])
            self.memset("dve", self.epsb[:], EPS, ["eps"])
            self.memset("dve", self.ones_f[:], 1.0, ["ones_f"])
            self.cp("dve", self.ident_f[:], self.ident[:], ["ident"], ["ident_f"])
            self.dma("sp", dmisc, self.modc[:], self.din("mod_consts", [128, 6, 48]), (), ["modc"])
            self.act(self.cs[:], self.cs[:], AF.Silu, ["cs"], ["cs"])
            self.cp("dve", self.cs_bf[:], self.cs[:], ["cs"], ["cs_bf"])
            self.cp("dve", self.cs_bc[:], self.cs[:].unsqueeze(2).to_broadcast([128, 8, 128]), ["cs"], ["cs_bc"])

            for si, (layer, j) in enumerate(sublayers):
                last = final_store and si == len(sublayers) - 1
                if dbg == "load":
                    break
                if si == 0:
                    self.sub_idx = {sl_: i_ for i_, sl_ in enumerate(sublayers)}
                    self.sublist = list(sublayers)
                    self.mod_pending = [(i_, cg) for i_ in range(len(sublayers)) for cg in range(24)]
                self.phase_begin()
                self.mod_apply(si)
                self.phase_end()
                if dbg == "mod":
                    break
                self.phase_begin()
                if j in (0, 2):
                    self.ffn_phase(layer, j // 2, last)
                elif layer % 2 == 0:
                    self.mla_phase(layer // 2, last)
                else:
                    self.dil_phase(layer // 2, last)
                self.phase_end()

            if dbg is not None:
                for t in range(NT):
                    self.dma("sp", self.out_ds[t % 4], self.y_d[t * 128:(t + 1) * 128, :], self.x_sb[:, t, :],
                             [("x", t)], [])
            block = es.enter_context(nc.Block())
            self.S.emit(block, self.sems, self.out_ds)
        return nc

    def mod_unit(self, flush=False):
        w_mod = self.din("w_mod", [2, D, 9 * D])
        ps = self.ps
        loaded = self.mod_loaded
        if self.mod_pending and len(loaded) < 2 and not flush:
            si, cg = self.mod_pending.pop(0)
            layer, j = self.sublist[si]
            wv = w_mod[layer].rearrange("(kc p) f -> p kc f", p=128)
            sl = self.wm_i % 2
            self.wm_i += 1
            c0 = j * 3 * D + cg * 128
            self.dma("pool", self.dwm[sl], self.wmr[sl][:], wv[:, :, c0:c0 + 128], (), [("wmr", sl)])
            loaded.append((si, cg, sl, self.mod_step))
        self.mod_step += 1
        if loaded and (flush or loaded[0][3] < self.mod_step - 1 or not self.mod_pending):
            si, cg, sl, _ = loaded.pop(0)
            layer, j = self.sublist[si]
            bank, col0 = self.psM
            for kc in range(8):
                self.mm(ps[:, bank, col0 + cg:col0 + cg + 1], self.wmr[sl][:, kc, :], self.cs_bf[:, kc:kc + 1],
                        kc == 0, kc == 7, [("wmr", sl), "cs_bf"], [("ps", bank)])
            if cg == 23:
                gsi = layer * 3 + j
                self.cp("dve", self.modres[:, gsi, :], ps[:, bank, col0:col0 + 24], [("ps", bank)],
                        [("modres", gsi)])
                self.mod_emitted[si] = True
        return bool(self.mod_pending or loaded)

    def mod_apply(self, si):
        layer, j = self.sublist[si]
        res_w = 0.5 if j in (0, 2) else 1.0
        ps = self.ps
        if not self.mod_emitted.get(si):
            while self.mod_loaded:
                self.mod_unit(flush=True)
            self.psM = (0, 0)
            while not self.mod_emitted.get(si):
                if self.mod_pending and self.mod_pending[0][0] != si:
                    self.mod_unit(flush=True)
                else:
                    self.mod_unit()
        mr = self.modres
        mc = self.modc
        cf = self.st[:, 56:64]
        si = layer * 3 + j
        self.tt("dve", self.B_fm[:], mr[:, si, 0:8], mc[:, si, 0:8], ALU.add, [("modres", si), "modc"], ["B"])
        self.tt("dve", self.A_fm[:], mr[:, si, 8:16], mc[:, si, 8:16], ALU.add, [("modres", si), "modc"], ["A"])
        self.stt("dve", self.A_fm[:], self.A_fm[:], 1.0, mc[:, si, 24:32], ALU.add, ALU.mult, ["A", "modc"], ["A"])
        self.tt("dve", cf, mr[:, si, 16:24], mc[:, si, 16:24], ALU.add, [("modres", si), "modc"], ["cf"])
        self.stt("dve", cf, cf, float(res_w), mc[:, si, 32:40], ALU.mult, ALU.mult, ["cf", "modc"], ["cf"])
        with ExitStack() as es:
            dg = self.sb(es, "dg", [128, 8, 128], F32)
            for kc in range(8):
                self.ts("dve", dg[:, kc, :], self.ident_f[:], cf[:, kc:kc + 1], None, ALU.mult, None,
                        ["cf", "ident_f"], [("dg", kc)])
                bank = 1 + kc // 4
                self.mm(ps[:, bank, (kc % 4) * 128:(kc % 4 + 1) * 128], self.ones_f[:], dg[:, kc, :], True, True,
                        [("dg", kc), "ones_f"], [("ps", bank)])
            for h in range(2):
                self.cp("dve", self.C[:, h * 512:(h + 1) * 512], ps[:, 1 + h, :], [("ps", 1 + h)], ["C"])
            self.S.barrier()

    def mod_phase(self, layer, j):
        P = self
        w_mod = self.din("w_mod", [2, D, 9 * D])
        bm_fm = self.din("bm_fm", [2, 3, 128, 16])
        bm_gate = self.din("bm_gate", [2, 3, D])
        pre_fm = self.din("pre_fm", [2, 3, 128, 8])
        post_g = self.din("post_g", [2, 3, D])
        res_w = 0.5 if j in (0, 2) else 1.0
        wv = w_mod[layer].rearrange("(kc p) f -> p kc f", p=128)
        ps = self.ps
        with ExitStack() as es:
            wm = [self.sb(es, f"wm{i}", [128, 8, 256], BF16) for i in range(4)]
            pg = self.sb(es, "pg", [128, D], F32)
            bmf = self.sb(es, "bmf", [128, 16], F32)
            pref = self.sb(es, "pref", [128, 8], F32)
            dw = [self.dsem(f"wm{i}") for i in range(4)]
            dv = self.dsem("modv", "sp")
            self.dma("sp", dv, bmf[:], bm_fm[layer, j], (), ["bmf"])
            self.dma("sp", dv, pref[:], pre_fm[layer, j], (), ["pref"])
            self.dma("sp", dv, self.C[:], bm_gate[layer, j].partition_broadcast(128), (), ["C"])
            self.dma("sp", dv, pg[:], post_g[layer, j].partition_broadcast(128), (), ["pg"])
            def ld_wm(ch):
                c0 = j * 3 * D + ch * 256
                self.dma("pool", dw[ch % 4], wm[ch % 4][:], wv[:, :, c0:c0 + 256], (), [("wm", ch % 4)])

            for ch in range(3):
                ld_wm(ch)
            for ch in range(12):
                sl = ch % 4
                if ch + 3 < 12:
                    ld_wm(ch + 3)
                if ch < 8:
                    for sub in range(2):
                        cg = ch * 2 + sub
                        for kc in range(8):
                            self.mm(ps[:, 0, cg:cg + 1], wm[sl][:, kc, sub * 128:(sub + 1) * 128],
                                    self.cs_bf[:, kc:kc + 1], kc == 0, kc == 7,
                                    [("wm", sl), "cs_bf"], [("ps", 0)])
                else:
                    gc = ch - 8
                    bank = 1 + gc // 2
                    for kc in range(8):
                        self.mm(ps[:, bank, (gc % 2) * 256:(gc % 2) * 256 + 256], self.cs_bc[:, kc, :],
                                wm[sl][:, kc, :], kc == 0, kc == 7,
                                [("wm", sl), "cs_bc"], [("ps", bank)])
            self.tt("dve", self.B_fm[:], ps[:, 0, 0:8], bmf[:, 0:8], ALU.add, [("ps", 0), "bmf"], ["B"])
            self.tt("dve", self.A_fm[:], ps[:, 0, 8:16], bmf[:, 8:16], ALU.add, [("ps", 0), "bmf"], ["A"])
            self.stt("dve", self.A_fm[:], self.A_fm[:], 1.0, pref[:], ALU.add, ALU.mult, ["A", "pref"], ["A"])
            for h in range(2):
                self.tt("dve", self.C[:, h * 512:(h + 1) * 512], ps[:, 1 + h, :], self.C[:, h * 512:(h + 1) * 512],
                        ALU.add, [("ps", 1 + h), "C"], ["C"])
            self.stt("dve", self.C[:], self.C[:], float(res_w), pg[:], ALU.mult, ALU.mult, ["C", "pg"], ["C"])
            self.S.barrier()

    def pre_stats(self, tiles, col0=0):
        n = len(tiles)
        st = self.st
        for i, t in enumerate(tiles):
            js = self.junk_i % 2
            self.junk_i += 1
            self.act(self.junk[:, js, :], self.x_sb[:, t, :], AF.Square, [("x", t)],
                     [("st", col0 + i), ("junk", js)], scale=1.0 / 32.0, accum=st[:, col0 + i:col0 + i + 1])
        rd = [("st", col0 + i) for i in range(n)]
        wr = [("st", 32 + col0 + i) for i in range(n)]
        self.act(st[:, col0:col0 + n], st[:, col0:col0 + n], AF.Sqrt, rd + ["eps"], rd, bias=self.epsb[:, 0:1])
        self.recip(st[:, 32 + col0:32 + col0 + n], st[:, col0:col0 + n], rd, wr)

    def pre_tile(self, t, rcol, dst, dst_tok, banks=(6, 7)):
        if self.dbg == "pre_a":
            return
        sl = self.xn_i % 2
        self.xn_i += 1
        xn = self.xn[:, sl, :]
        self.ts("dve", xn, self.x_sb[:, t, :], self.st[:, 32 + rcol:33 + rcol], None, ALU.mult, None,
                [("x", t), ("st", 32 + rcol)], [("xn", sl)])
        if self.dbg == "pre_b":
            return
        bank = banks[self.pst_i % 2]
        self.pst_i += 1
        psb = self.ps[:, bank, :].bitcast(BF16).rearrange("p (k t) -> p k t", k=8)
        for kc in range(8):
            self.tr(psb[:, kc, :], xn[:, kc * 128:(kc + 1) * 128], [("xn", sl), "ident"], [("ps", bank)])
        if self.dbg == "pre_c":
            return
        for kc in range(8):
            if self.dbg == "pre_d" and kc % 2 == 1:
                continue
            if self.dbg == "pre_e" and kc % 2 == 0:
                continue
            if bank % 2 == 0:
                self.act(dst(kc), psb[:, kc, :], AF.Identity, [("ps", bank), "A", "B"], [dst_tok(kc)],
                         scale=self.A_fm[:, kc:kc + 1], bias=self.B_fm[:, kc:kc + 1])
            else:
                self.ts("dve", dst(kc), psb[:, kc, :], self.A_fm[:, kc:kc + 1], self.B_fm[:, kc:kc + 1],
                        ALU.mult, ALU.add, [("ps", bank), "A", "B"], [dst_tok(kc)])

    def epilogue(self, t, b0, scr, scr_tok, par, last):
        ps = self.ps
        st = self.st
        c = 16 + par * 2
        yv = ps[:, b0:b0 + 2, :]
        js = self.junk_i % 2
        self.junk_i += 1
        self.act(self.junk[:, js, :].rearrange("p (a b) -> p a b", a=2), yv, AF.Square,
                 [("ps", b0), ("ps", b0 + 1)], [("st", c), ("junk", js)], scale=1.0 / 32.0, accum=st[:, c:c + 1])
        self.act(st[:, c:c + 1], st[:, c:c + 1], AF.Sqrt, [("st", c), "eps"], [("st", c)], bias=self.epsb[:, 0:1])
        self.recip(st[:, c + 1:c + 2], st[:, c:c + 1], [("st", c)], [("st", c + 1)])
        for h in range(2):
            self.stt("dve", scr(h), ps[:, b0 + h, :], st[:, c + 1:c + 2], self.C[:, h * 512:(h + 1) * 512],
                     ALU.mult, ALU.mult, [("ps", b0 + h), ("st", c + 1), "C"], [scr_tok(h)])
            xs = self.x_sb[:, t, h * 512:(h + 1) * 512]
            self.tt("pool", xs, xs, scr(h), ALU.add, [("x", t), scr_tok(h)], [("x", t)])
        if last:
            self.dma("sp", self.out_ds[t % 4], self.y_d[t * 128:(t + 1) * 128, :], self.x_sb[:, t, :],
                     [("x", t)], [])

    def ffn_phase(self, layer, f, last):
        wg_d = self.din("ffn_w_gate", [2, 2, D, DFF])
        wu_d = self.din("ffn_w_up", [2, 2, D, DFF])
        wd_d = self.din("ffn_w_down", [2, 2, DFF, D])
        wgv = wg_d[layer, f].rearrange("(kc p) f -> p kc f", p=128)
        wuv = wu_d[layer, f].rearrange("(kc p) f -> p kc f", p=128)
        wdv = wd_d[layer, f].rearrange("(fc p) d -> p fc d", p=128)
        ps = self.ps
        NR = 3
        with ExitStack() as es:
            hnT = self.sb(es, "hnT", [128, 8, 1024], BF16)
            hT = self.sb(es, "hT", [128, NFC, 1024], BF16)
            wd = self.sb(es, "wd", [128, NFC, D], BF16)
            wg = [self.sb(es, f"wg{i}", [128, 8, 128], BF16) for i in range(NR)]
            wu = [self.sb(es, f"wu{i}", [128, 8, 128], BF16) for i in range(NR)]
            sg = self.sb(es, "sg", [128, 2, 512], F32)
            dg = [self.dsem(f"wg{i}") for i in range(NR)]
            du = [self.dsem(f"wu{i}") for i in range(NR)]
            dd = [self.dsem(f"wd{i}") for i in range(4)]
            gi = [0]

            def load_chunk(fc):
                sl = fc % NR
                self.dma("pool", dg[sl], wg[sl][:], wgv[:, :, fc * 128:(fc + 1) * 128], (), [("wg", sl)])
                self.dma("pool", du[sl], wu[sl][:], wuv[:, :, fc * 128:(fc + 1) * 128], (), [("wu", sl)])

            def prologue(tb, banks, tls=range(8), stats=True):
                tiles = [tb * 8 + i for i in range(8)]
                if stats:
                    self.pre_stats(tiles[0:4], col0=0)
                    self.pre_stats(tiles[4:8], col0=4)
                for tl in tls:
                    t = tiles[tl]
                    self.pre_tile(t, tl, lambda kc, tl=tl: hnT[:, kc, tl * 128:(tl + 1) * 128],
                                  lambda kc, tl=tl: ("hnT", tl, kc), banks)

            def gateup(tb, after_first=None):
                for fc in range(NFC):
                    if fc + NR - 1 < NFC:
                        load_chunk(fc + NR - 1)
                    self.dma("pool", dd[fc % 4], wd[:, fc, :], wdv[:, fc, :], (), [("wd", fc)])
                    sl = fc % NR
                    for nt in range(2):
                        par = gi[0] % 2
                        gi[0] += 1
                        for kc in range(8):
                            rd_h = [("hnT", nt * 4 + q, kc) for q in range(4)]
                            self.mm(ps[:, par, :], wg[sl][:, kc, :], hnT[:, kc, nt * 512:(nt + 1) * 512],
                                    kc == 0, kc == 7, [("wg", sl)] + rd_h, [("ps", par)])
                        for kc in range(8):
                            rd_h = [("hnT", nt * 4 + q, kc) for q in range(4)]
                            self.mm(ps[:, 2 + par, :], wu[sl][:, kc, :], hnT[:, kc, nt * 512:(nt + 1) * 512],
                                    kc == 0, kc == 7, [("wu", sl)] + rd_h, [("ps", 2 + par)])
                        self.act(sg[:, par, :], ps[:, par, :], AF.Silu, [("ps", par)], [("sg", par)])
                        self.tt("dve", hT[:, fc, nt * 512:(nt + 1) * 512], sg[:, par, :], ps[:, 2 + par, :],
                                ALU.mult, [("sg", par), ("ps", 2 + par)], [("hT", fc, nt)])
                        if after_first is not None and fc == 0 and nt == 0:
                            after_first()

            def down(tb):
                tiles = [tb * 8 + i for i in range(8)]
                for tl, t in enumerate(tiles):
                    par = tl % 2
                    b0 = 4 + 2 * par
                    for h in range(2):
                        for fc in range(NFC):
                            self.mm(ps[:, b0 + h, :], hT[:, fc, tl * 128:(tl + 1) * 128],
                                    wd[:, fc, h * 512:(h + 1) * 512], fc == 0, fc == NFC - 1,
                                    [("hT", fc, tl // 4), ("wd", fc)], [("ps", b0 + h)])
                    self.epilogue(t, b0, lambda h: sg[:, h, :], lambda h: ("sg", h), par, last)

            for fc in range(NR - 1):
                load_chunk(fc)
            prologue(0, (6, 7), tls=range(0, 4))
            gateup(0, after_first=lambda: prologue(0, (6, 7), tls=range(4, 8), stats=False))
            for fc in range(NR - 1):
                load_chunk(fc)
            prologue(1, (0, 1))
            down(0)
            gateup(1)
            down(1)

    def post_tr(self, t, o_tile, oT):
        ps = self.ps
        bank = 6 + (t % 2)
        sl = t % 2
        psb = ps[:, bank, :].bitcast(BF16).rearrange("p (k t) -> p k t", k=8)
        for kc in range(8):
            self.tr(psb[:, kc, :], o_tile(kc), ["o_all"], [("ps", bank)])
        if bank == 6:
            self.act(oT[:, sl, :, :], psb, AF.Identity, [("ps", bank)], [("oT", sl)])
        else:
            self.cp("dve", oT[:, sl, :, :], psb, [("ps", bank)], [("oT", sl)])

    def post_mm(self, t, wo, oT, scr, last):
        ps = self.ps
        sl = t % 2
        par = t % 2
        b0 = 2 * par
        for h in range(2):
            for kc in range(8):
                self.mm(ps[:, b0 + h, :], oT[:, sl, kc, :], wo[:, kc, h * 512:(h + 1) * 512], kc == 0, kc == 7,
                        [("oT", sl), "wo"], [("ps", b0 + h)])
        self.epilogue(t, b0, lambda h: scr[:, h, :], lambda h: ("scr", h), par, last)

    def mla_phase(self, a, last):
        H = 16
        w_in_d = self.din("mla_w_in", [1, D, 672])
        qn_d = self.din("mla_q_norm", [1, 384])
        wq_d = self.din("mla_w_q_up", [1, 384, 1536])
        kvn_d = self.din("mla_kv_norm", [1, 256])
        wkv_d = self.din("mla_w_kv_up", [1, 256, 2048])
        wo_d = self.din("mla_w_o", [1, D, D])
        cc_d = self.din("rope_cc", [128, NT, 32])
        ss_d = self.din("rope_ss", [128, NT, 32])
        mask_d = self.din("tri_mask", [128, 128], BF16)
        ps = self.ps
        st = self.st
        scale = 96.0 ** -0.5
        with ExitStack() as es0:
            cqnT = self.sb(es0, "cqnT", [128, 3, SQ], BF16)
            ckvnT = self.sb(es0, "ckvnT", [128, 2, SQ], BF16)
            krT = self.sb(es0, "krT", [128, SQ], BF16)
            o_sb = self.sb(es0, "o_sb", [128, NT, D], BF16)
            cc = self.sb(es0, "cc", [128, NT, 32], F32)
            ssn = self.sb(es0, "ssn", [128, NT, 32], F32)
            mask = self.sb(es0, "mask", [128, 128], BF16)
            dm = self.dsem("mla_c", "sp")
            self.dma("sp", dm, cc[:], cc_d, (), ["cc"])
            self.dma("sp", dm, ssn[:], ss_d, (), ["ssn"])
            self.dma("sp", dm, mask[:], mask_d, (), ["mask"])
            with ExitStack() as es:
                win = self.sb(es, "win", [128, 8, 672], BF16)
                hnt = self.sb(es, "hnt", [128, 2, 8, 128], BF16)
                latn = self.sb(es, "latn", [128, 2, 768], BF16)
                qg = self.sb(es, "qg", [128, 384], F32)
                kvg = self.sb(es, "kvg", [128, 256], F32)
                rt = self.sb(es, "rt", [128, 2, 64], F32)
                dwi = [self.dsem("win0"), self.dsem("win1")]
                self.dma("pool", dwi[0], win[:, 0:4, :], w_in_d[a].rearrange("(kc p) f -> p kc f", p=128)[:, 0:4, :],
                         (), [("win", 0)])
                self.dma("pool", dwi[1], win[:, 4:8, :], w_in_d[a].rearrange("(kc p) f -> p kc f", p=128)[:, 4:8, :],
                         (), [("win", 1)])
                self.dma("sp", dm, qg[:], qn_d[a].partition_broadcast(128), (), ["qg"])
                self.dma("sp", dm, kvg[:], kvn_d[a].partition_broadcast(128), (), ["kvg"])
                self.memset("dve", latn[:], 0.0, [("latn", 0), ("latn", 1)])
                self.pre_stats(list(range(NT)))
                def stA(t):
                    sl = t % 2
                    self.pre_tile(t, t, lambda kc, sl=sl: hnt[:, sl, kc, :], lambda kc, sl=sl: ("hnt", sl, kc))

                def stB(t):
                    sl = t % 2
                    bA, bB = (0, 1) if sl == 0 else (2, 3)
                    for kc in range(8):
                        self.mm(ps[:, bA, 0:384], hnt[:, sl, kc, :], win[:, kc, 0:384], kc == 0, kc == 7,
                                [("hnt", sl, kc), ("win", kc // 4)], [("ps", bA)])
                    for kc in range(8):
                        self.mm(ps[:, bB, 0:288], hnt[:, sl, kc, :], win[:, kc, 384:672], kc == 0, kc == 7,
                                [("hnt", sl, kc), ("win", kc // 4)], [("ps", bB)])

                def stC(t):
                    sl = t % 2
                    bA, bB = (0, 1) if sl == 0 else (2, 3)
                    c = 20 + sl * 4
                    js = self.junk_i % 2
                    self.junk_i += 1
                    self.act(self.junk[:, js, 0:384], ps[:, bA, 0:384], AF.Square, [("ps", bA)],
                             [("st", c), ("junk", js)], scale=384.0 ** -0.5, accum=st[:, c:c + 1])
                    js = self.junk_i % 2
                    self.junk_i += 1
                    self.act(self.junk[:, js, 0:256], ps[:, bB, 0:256], AF.Square, [("ps", bB)],
                             [("st", c + 1), ("junk", js)], scale=1.0 / 16.0, accum=st[:, c + 1:c + 2])
                    self.act(st[:, c:c + 2], st[:, c:c + 2], AF.Sqrt, [("st", c), ("st", c + 1), "eps"],
                             [("st", c), ("st", c + 1)], bias=self.epsb[:, 0:1])
                    self.recip(st[:, c + 2:c + 4], st[:, c:c + 2], [("st", c), ("st", c + 1)], [("st", c + 2)])
                    self.stt("dve", latn[:, sl, 0:384], ps[:, bA, 0:384], st[:, c + 2:c + 3], qg[:],
                             ALU.mult, ALU.mult, [("ps", bA), ("st", c + 2), "qg"], [("latn", sl)])
                    self.stt("dve", latn[:, sl, 384:640], ps[:, bB, 0:256], st[:, c + 3:c + 4], kvg[:],
                             ALU.mult, ALU.mult, [("ps", bB), ("st", c + 2), "kvg"], [("latn", sl)])
                    kr = ps[:, bB, 256:288]
                    self.tt("dve", rt[:, sl, 0:32], kr, cc[:, t, :], ALU.mult,
                            [("ps", bB), ("st", c + 2), "cc"], [("rt", sl)])
                    self.tt("dve", rt[:, sl, 32:48], ps[:, bB, 272:288], ssn[:, t, 0:16], ALU.mult,
                            [("ps", bB), "ssn"], [("rt", sl)])
                    self.tt("dve", rt[:, sl, 48:64], ps[:, bB, 256:272], ssn[:, t, 16:32], ALU.mult,
                            [("ps", bB), "ssn"], [("rt", sl)])
                    self.tt("dve", latn[:, sl, 704:736], rt[:, sl, 0:32], rt[:, sl, 32:64], ALU.add,
                            [("rt", sl)], [("latn", sl)])
                def stC2(t):
                    sl = t % 2
                    bank = 4 + sl
                    psb = ps[:, bank, :].bitcast(BF16).rearrange("p (k t) -> p k t", k=8)
                    for k6 in range(6):
                        self.tr(psb[:, k6, :], latn[:, sl, k6 * 128:(k6 + 1) * 128], [("latn", sl), "ident"],
                                [("ps", bank)])
                    tc = slice(t * 128, (t + 1) * 128)
                    self.act(cqnT[:, :, tc], psb[:, 0:3, :], AF.Identity, [("ps", bank)], [("cqnT", t)])
                    self.act(ckvnT[:, :, tc], psb[:, 3:5, :], AF.Identity, [("ps", bank)], [("ckvnT", t)])
                    self.act(krT[64:96, tc], psb[64:96, 5, :], AF.Identity, [("ps", bank)], [("krT", t)])

                for step in range(NT + 3):
                    if step < NT:
                        stA(step)
                    if 0 <= step - 1 < NT:
                        stB(step - 1)
                    if 0 <= step - 2 < NT:
                        stC(step - 2)
                    if 0 <= step - 3 < NT:
                        stC2(step - 3)
                self.S.barrier()
            with ExitStack() as es:
                qT = self.sb(es, "qT", [128, 4, SQ], BF16)
                kT = self.sb(es, "kT", [128, 4, SQ], BF16)
                V = self.sb(es, "V", [128, NT, 4, 65], BF16)
                wq = self.sb(es, "wq", [128, 3, 384], BF16)
                wkk = self.sb(es, "wkk", [128, 2, 4, 64], BF16)
                wkv = self.sb(es, "wkv", [128, 2, 4, 64], BF16)
                qb = self.sb(es, "qb", [128, 2, 4, 96], BF16)
                rq = self.sb(es, "rq", [128, 2, 4, 64], F32)
                pT = self.sb(es, "pT", [128, 5, 512], BF16)
                rz = self.sb(es, "rz", [128, 2, 4], F32)
                dq = self.dsem("wq")
                dk = self.dsem("wkk")
                dvv = self.dsem("wkv")
                self.memset("pool", V[:], 1.0, [("V", kt) for kt in range(NT)])
                wqv = wq_d[a].rearrange("(kc p) f -> p kc f", p=128)
                wkvv = wkv_d[a].rearrange("(kc p) (h c) -> p kc h c", p=128, c=128)
                for h4 in range(4):
                    self.cp("pool" if h4 % 2 == 0 else "dve", kT[64:96, h4, :], krT[64:96, :], [], [("kTr", h4)])
                for hg in range(4):
                    self.dma("pool", dq, wq[:], wqv[:, :, hg * 384:(hg + 1) * 384], (), ["wq"])
                    for kc in range(2):
                        self.dma("pool", dk, wkk[:, kc], wkvv[:, kc, hg * 4:(hg + 1) * 4, 0:64], (), ["wkk"])
                        self.dma("pool", dvv, wkv[:, kc], wkvv[:, kc, hg * 4:(hg + 1) * 4, 64:128], (), ["wkv"])
                    def kstep(i):
                        h4, cch = i // 4, i % 4
                        bank = 2 + (cch % 2)
                        for kc in range(2):
                            self.mm(ps[0:64, bank, :], wkk[:, kc, h4, :], ckvnT[:, kc, cch * 512:(cch + 1) * 512],
                                    kc == 0, kc == 1, ["wkk"], [("ps", bank)])
                        self.act(kT[0:64, h4, cch * 512:(cch + 1) * 512], ps[0:64, bank, :], AF.Identity,
                                 [("ps", bank)], [("kTn", h4, cch)])

                    def vstep(kt):
                        bank = 6 + (kt % 2)
                        for kc in range(2):
                            self.mm(ps[:, bank, 0:256], ckvnT[:, kc, kt * 128:(kt + 1) * 128],
                                    wkv[:, kc, :, :].rearrange("p h c -> p (h c)"), kc == 0, kc == 1,
                                    ["wkv"], [("ps", bank)])
                        self.act(V[:, kt, :, 0:64], ps[:, bank, 0:256].rearrange("p (h c) -> p h c", c=64),
                                 AF.Identity, [("ps", bank)], [("V", kt)])

                    def q1(t):
                        sl = t % 2
                        bank = sl
                        for kc in range(3):
                            self.mm(ps[:, bank, 0:384], cqnT[:, kc, t * 128:(t + 1) * 128], wq[:, kc, :],
                                    kc == 0, kc == 2, ["wq"], [("ps", bank)])

                    def q2(t):
                        sl = t % 2
                        bank = sl
                        pq = ps[:, bank, 0:384].rearrange("p (h c) -> p h c", c=96)
                        ccb = cc[:, t, :].unsqueeze(1).to_broadcast([128, 4, 32])
                        s0 = ssn[:, t, 0:16].unsqueeze(1).to_broadcast([128, 4, 16])
                        s1 = ssn[:, t, 16:32].unsqueeze(1).to_broadcast([128, 4, 16])
                        self.cp("dve", qb[:, sl, :, 0:64], pq[:, :, 0:64], [("ps", bank)], [("qb", sl)])
                        self.tt("dve", rq[:, sl, :, 0:32], pq[:, :, 64:96], ccb, ALU.mult, [("ps", bank), "cc"],
                                [("rq", sl)])
                        self.tt("dve", rq[:, sl, :, 32:48], pq[:, :, 80:96], s0, ALU.mult, [("ps", bank), "ssn"],
                                [("rq", sl)])
                        self.tt("dve", rq[:, sl, :, 48:64], pq[:, :, 64:80], s1, ALU.mult, [("ps", bank), "ssn"],
                                [("rq", sl)])
                        self.tt("dve", qb[:, sl, :, 64:96], rq[:, sl, :, 0:32], rq[:, sl, :, 32:64], ALU.add,
                                [("rq", sl)], [("qb", sl)])
                        bk = 4 + sl
                        psb = ps[:, bk, :].bitcast(BF16).rearrange("p (k t) -> p k t", k=8)
                        for h4 in range(4):
                            self.tr(psb[0:96, h4, :], qb[:, sl, h4, :], [("qb", sl), "ident"], [("ps", bk)])
                        if sl == 0:
                            self.act(qT[0:96, :, t * 128:(t + 1) * 128], psb[0:96, 0:4, :], AF.Identity,
                                     [("ps", bk)], [("qT", t)])
                        else:
                            self.cp("dve", qT[0:96, :, t * 128:(t + 1) * 128], psb[0:96, 0:4, :],
                                    [("ps", bk)], [("qT", t)])

                    for step in range(NT + 1):
                        if step < NT:
                            q1(step)
                            kstep(step)
                            vstep(step)
                        if step >= 1:
                            q2(step - 1)
                    items = [(qc, h4, kt) for qc in range(4) for h4 in range(4) for kt in range(qc * 4 + 4)]
                    LA = 4
                    SBK = (0, 1, 2, 6, 7)
                    slots = {}

                    def score_item(i):
                        qc, h4, kt = items[i]
                        jl0 = max(0, kt - qc * 4)
                        n0 = jl0 * 128
                        sl = i % 5
                        slots[i] = sl
                        bk = SBK[sl]
                        self.mm(ps[:, bk, n0:512], kT[0:96, h4, kt * 128:(kt + 1) * 128],
                                qT[0:96, h4, qc * 512 + n0:(qc + 1) * 512], True, True,
                                [("kTn", h4, kt // 4), ("kTr", h4)] + [("qT", qc * 4 + j) for j in range(jl0, 4)],
                                [("ps", bk)])
                        self.act(pT[:, sl, n0:512], ps[:, bk, n0:512], AF.Exp, [("ps", bk)], [("pT", sl)],
                                 scale=scale)
                        if kt >= qc * 4:
                            self.tt("dve", pT[:, sl, n0:n0 + 128], pT[:, sl, n0:n0 + 128], mask[:],
                                    ALU.mult, [("pT", sl), "mask"], [("pT", sl)])

                    def pv_item(i):
                        qc, h4, kt = items[i]
                        h = hg * 4 + h4
                        nkt = qc * 4 + 4
                        jl0 = max(0, kt - qc * 4)
                        sl = slots[i]
                        gidx = qc * 4 + h4
                        ob = 3 + (gidx % 2)
                        orz = gidx % 2
                        for jl in range(jl0, 4):
                            self.mm(ps[:, ob, jl * 65:(jl + 1) * 65], pT[:, sl, jl * 128:(jl + 1) * 128],
                                    V[:, kt, h4, :], kt == 0 and jl == 0, kt == nkt - 1 and jl == 3,
                                    [("pT", sl), ("V", kt)], [("ps", ob)])
                        if kt == nkt - 1:
                            po = ps[:, ob, 0:260].rearrange("p (j c) -> p j c", c=65)
                            self.recip(rz[:, orz, :].unsqueeze(2), po[:, :, 64:65], [("ps", ob)], [("rz", orz)])
                            self.tt("dve", o_sb[:, qc * 4:(qc + 1) * 4, h * 64:(h + 1) * 64], po[:, :, 0:64],
                                    rz[:, orz, :].unsqueeze(2).to_broadcast([128, 4, 64]), ALU.mult,
                                    [("ps", ob), ("rz", orz)], [("o", qc, h)])

                    for i in range(min(LA, len(items))):
                        score_item(i)
                    self.psM = (5, 256)
                    for i in range(len(items)):
                        if i + LA < len(items):
                            score_item(i + LA)
                        pv_item(i)
                        if i % 6 == 5:
                            self.mod_unit()
                    if hg == 3:
                        while self.mod_loaded:
                            self.mod_unit(flush=True)
                self.S.barrier()
            with ExitStack() as es:
                wo = self.sb(es, "wo", [128, 8, D], BF16)
                oT = self.sb(es, "oT", [128, 2, 8, 128], BF16)
                scr = self.sb(es, "scr", [128, 2, 512], F32)
                dwo = [self.dsem("wo0"), self.dsem("wo1")]
                wov = wo_d[a].rearrange("(kc p) f -> p kc f", p=128)
                for i in range(2):
                    self.dma("pool", dwo[i], wo[:, i * 4:(i + 1) * 4, :], wov[:, i * 4:(i + 1) * 4, :], (), ["wo"])
                for step in range(NT + 1):
                    if step < NT:
                        self.post_tr(step, lambda kc, t=step: o_sb[:, t, kc * 128:(kc + 1) * 128], oT)
                    if step >= 1:
                        self.post_mm(step - 1, wo, oT, scr, last)
                self.S.barrier()

    def dil_phase(self, b, last):
        DIL = (1, 4, 16)
        w_in_d = self.din("dil_w_in", [1, D, 9 * D])
        wo_d = self.din("dil_w_o", [1, D, D])
        eb_d = self.din("dil_eb", [3, 8, 128, 512])
        ps = self.ps
        wv_all = w_in_d[b].rearrange("(kc p) f -> p kc f", p=128)

        def tok(T, rows, g, r, n):
            dl = DIL[g]
            s0 = r + dl * n * 128
            return T[rows, s0:s0 + dl * 127 + 1:dl]

        with ExitStack() as es0:
            oT = self.sb(es0, "oT", [128, 8, SQ], BF16)
            with ExitStack() as es:
                hnT = self.sb(es, "hnT", [128, 8, SQ], BF16)
                Ut = self.sb(es, "Ut", [128, SQ], F32)
                Zt = self.sb(es, "Zt", [128, SQ], F32)
                qz = [self.sb(es, f"qz{e}", [128, SQ], BF16) for e in range(2)]
                kT = self.sb(es, "kT", [128, SQ], BF16)
                V0 = self.sb(es, "V0", [128, 16, 128], BF16)
                V1 = self.sb(es, "V1", [128, 16, 128], BF16)
                ones0 = self.sb(es, "ones0", [128, 128], BF16)
                ones1 = self.sb(es, "ones1", [128, 128], BF16)
                wr = [[self.sb(es, f"dw{i}_{s}", [128, 8, 128], BF16) for s in range(3)] for i in range(2)]
                eb = self.sb(es, "eb", [128, 2, 512], BF16)
                ebr = self.sb(es, "ebr", [128, 2, 512], F32)
                pT = self.sb(es, "pT", [128, 4, 512], BF16)
                dwq = [[self.dsem(f"dw{i}{s}") for s in range(3)] for i in range(2)]
                deb = [self.dsem("eb0", "sp"), self.dsem("eb1", "sp")]
                self.memset("pool", qz[0][:], 0.0, ["qz"])
                self.memset("pool", qz[1][:], 0.0, ["qz"])
                self.memset("dve", V0[:], 0.0, ["V0"])
                self.memset("dve", V1[:], 0.0, ["V1"])
                self.memset("dve", ones0[:], 0.0, ["ones"])
                self.memset("dve", ones1[:], 0.0, ["ones"])
                self.memset("dve", ones0[:, 0:64], 1.0, ["ones"])
                self.memset("dve", ones1[:, 64:128], 1.0, ["ones"])
                self.pre_stats(list(range(NT)))
                for t in range(NT):
                    self.pre_tile(t, t, lambda kc, t=t: hnT[:, kc, t * 128:(t + 1) * 128],
                                  lambda kc, t=t: ("hnT", t, kc))
                it = 0
                sbi = 0
                ubi = 0

                def ld_w(hp_, g_, wi_):
                    for s_ in range(3):
                        c0 = (g_ * 3 + s_) * D + hp_ * 128
                        self.dma("pool", dwq[wi_][s_], wr[wi_][s_][:], wv_all[:, :, c0:c0 + 128], (),
                                 [("dw", wi_, s_)])
                    self.dma("sp", deb[wi_], ebr[:, wi_, :], eb_d[g_, hp_], (), [("ebr", wi_)])
                    self.act(eb[:, wi_, :], ebr[:, wi_, :], AF.Exp, [("ebr", wi_)], [("eb", wi_)])

                n_it = 24
                dbg = ""
                for hp in range(8):
                    for g in range(3):
                        dl = DIL[g]
                        nb = 16 // dl
                        wi = it % 2
                        it += 1
                        if it == 1:
                            ld_w(hp, g, wi)
                        nxt = hp * 3 + g + 1
                        if nxt < n_it:
                            ld_w(nxt // 3, nxt % 3, 1 - wi)
                        for s in range(2):
                            for cch in range(4):
                                bank = cch % 2
                                for kc in range(8):
                                    self.mm(ps[:, bank, :], wr[wi][s][:, kc, :], hnT[:, kc, cch * 512:(cch + 1) * 512],
                                            kc == 0, kc == 7, [("dw", wi, s)] + [("hnT", cch * 4 + q, kc) for q in range(4)],
                                            [("ps", bank)])
                                cs_ = slice(cch * 512, (cch + 1) * 512)
                                if s == 0:
                                    self.act(qz[0][0:64, cs_], ps[0:64, bank, :], AF.Identity,
                                             [("ps", bank), "qz"], [("qk", 0, cch)], scale=0.125)
                                    self.act(qz[1][64:128, cs_], ps[64:128, bank, :], AF.Identity,
                                             [("ps", bank), "qz"], [("qk", 0, cch)], scale=0.125)
                                else:
                                    self.act(kT[:, cs_], ps[:, bank, :], AF.Identity,
                                             [("ps", bank)], [("qk", 1, cch)])
                        for b4 in range(0 if dbg.startswith("dil_a") else 4):
                            bank = 2 + (b4 % 2)
                            for bb in range(4):
                                bi = b4 * 4 + bb
                                r, n = bi // nb, bi % nb
                                for kc in range(8):
                                    self.mm(ps[:, bank, bb * 128:(bb + 1) * 128], tok(hnT[:, kc, :], slice(0, 128), g, r, n),
                                            wr[wi][2][:, kc, :], kc == 0, kc == 7,
                                            [("dw", wi, 2)] + [("hnT", q, kc) for q in range(NT)], [("ps", bank)])
                            pv = ps[:, bank, :].rearrange("p (b c) -> p b c", c=128)
                            self.act(V0[:, b4 * 4:(b4 + 1) * 4, 0:64], pv[:, :, 0:64], AF.Identity, [("ps", bank)],
                                     [("V", b4)])
                            self.act(V1[:, b4 * 4:(b4 + 1) * 4, 64:128], pv[:, :, 64:128], AF.Identity, [("ps", bank)],
                                     [("V", b4)])

                        SB = (4, 5, 0, 1)

                        def scores(bi):
                            nonlocal sbi
                            r, n = bi // nb, bi % nb
                            sl = sbi % 4
                            sbi += 1
                            bank = SB[sl]
                            kb0 = 0 if n > 0 else 1
                            allr = slice(0, 128)
                            for e in range(2):
                                for kb in range(kb0, 2):
                                    c0 = (e * 2 + kb) * 128
                                    self.mm(ps[:, bank, c0:c0 + 128], tok(kT, allr, g, r, n - 1 + kb),
                                            tok(qz[e], allr, g, r, n), True, True,
                                            [("qk", 0, q) for q in range(4)] + [("qk", 1, q) for q in range(4)],
                                            [("ps", bank)])
                            if kb0 == 0:
                                self.act(pT[:, sl, :], ps[:, bank, :], AF.Exp, [("ps", bank)], [("pT", sl)])
                                self.tt("dve", pT[:, sl, :], pT[:, sl, :], eb[:, wi, :], ALU.mult,
                                        [("pT", sl), ("eb", wi)], [("pT", sl)])
                            else:
                                v4 = lambda a: a.rearrange("p (e k q) -> p e k q", e=2, k=2)[:, :, 1, :]
                                self.act(v4(pT[:, sl, :]), v4(ps[:, bank, :]), AF.Exp, [("ps", bank)], [("pT", sl)])
                                self.tt("dve", v4(pT[:, sl, :]), v4(pT[:, sl, :]), v4(eb[:, wi, :]), ALU.mult,
                                        [("pT", sl), ("eb", wi)], [("pT", sl)])
                            return sl, kb0

                        def pv_block(bi, sl, kb0, ub, zb):
                            n = bi % nb
                            bb = bi % 4
                            seq = [(e, kb) for e in range(2) for kb in range(kb0, 2)]
                            for i, (e, kb) in enumerate(seq):
                                c0 = (e * 2 + kb) * 128
                                Vs = V0 if e == 0 else V1
                                self.mm(ps[:, ub, bb * 128:(bb + 1) * 128], Vs[:, bi - 1 + kb, :],
                                        pT[:, sl, c0:c0 + 128], i == 0, i == len(seq) - 1,
                                        [("pT", sl), ("V", (bi - 1 + kb) // 4)], [("ps", ub)])
                            for i, (e, kb) in enumerate(seq):
                                c0 = (e * 2 + kb) * 128
                                on = ones0 if e == 0 else ones1
                                self.mm(ps[:, zb, bb * 128:(bb + 1) * 128], on[:], pT[:, sl, c0:c0 + 128],
                                        i == 0, i == len(seq) - 1, [("pT", sl), "ones"], [("ps", zb)])

                        def dst_view(T, b4):
                            if g == 0:
                                return T[:, b4 * 512:(b4 + 1) * 512]
                            if g == 1:
                                return T[:, b4:b4 + 4 * 511 + 1:4]
                            return T[:, :].rearrange("p (m r) -> p r m", r=16)[:, b4 * 4:(b4 + 1) * 4, :]

                        if dbg.startswith("dil_a") or dbg.startswith("dil_b"):
                            continue
                        LA = 3
                        pend = [scores(i) for i in range(LA)]
                        for bi in range(16):
                            b4 = bi // 4
                            if bi % 4 == 0:
                                ub = 6 + (ubi % 2)
                                zb = 2 + (ubi % 2)
                                ubi += 1
                            cur = pend.pop(0)
                            if bi + LA < 16:
                                pend.append(scores(bi + LA))
                            if dbg.startswith("dil_c"):
                                continue
                            pv_block(bi, cur[0], cur[1], ub, zb)
                            if dbg.startswith("dil_d"):
                                continue
                            if bi % 4 == 3:
                                pu = ps[:, ub, :]
                                pz = ps[:, zb, :]
                                if g == 2:
                                    pu = pu.rearrange("p (r m) -> p r m", r=4)
                                    pz = pz.rearrange("p (r m) -> p r m", r=4)
                                du = dst_view(Ut, b4)
                                dz = dst_view(Zt, b4)
                                if g == 0:
                                    self.cp("dve", du, pu, [("ps", ub)], [("Ut", b4)])
                                    self.cp("dve", dz, pz, [("ps", zb)], [("Zt", b4)])
                                else:
                                    self.tt("dve", du, pu, du, ALU.add, [("ps", ub)] + [("Ut", q) for q in range(4)],
                                            [("Ut", q) for q in range(4)])
                                    self.tt("dve", dz, pz, dz, ALU.add, [("ps", zb)] + [("Zt", q) for q in range(4)],
                                            [("Zt", q) for q in range(4)])
                    for q in range(4):
                        cs_ = slice(q * 512, (q + 1) * 512)
                        self.recip(Zt[:, cs_], Zt[:, cs_], [("Zt", k) for k in range(4)], [("Zt", k) for k in range(4)])
                        self.tt("dve", oT[:, hp, cs_], Ut[:, cs_], Zt[:, cs_], ALU.mult,
                                [("Ut", k) for k in range(4)] + [("Zt", k) for k in range(4)], [("oT", hp)])
                self.S.barrier()
            with ExitStack() as es:
                wo = self.sb(es, "wo", [128, 8, D], BF16)
                scr = self.sb(es, "scr", [128, 2, 512], F32)
                dwo = [self.dsem("wo0"), self.dsem("wo1")]
                wov = wo_d[b].rearrange("(kc p) f -> p kc f", p=128)
                for i in range(2):
                    self.dma("pool", dwo[i], wo[:, i * 4:(i + 1) * 4, :], wov[:, i * 4:(i + 1) * 4, :], (), ["wo"])
                for t in range(NT):
                    par = t % 2
                    b0 = 2 * par
                    for h in range(2):
                        for kc in range(8):
                            self.mm(ps[:, b0 + h, :], oT[:, kc, t * 128:(t + 1) * 128], wo[:, kc, h * 512:(h + 1) * 512],
                                    kc == 0, kc == 7, ["wo"], [("ps", b0 + h)])
                    self.epilogue(t, b0, lambda h: scr[:, h, :], lambda h: ("scr", h), par, last)
                self.S.barrier()


def _host_consts(inp):
    b_mod = np.asarray(inp["b_mod"], np.float32).reshape(2, 3, 3, D)
    ss = b_mod[:, :, 0:2, :].reshape(2, 3, 16, 128)
    bm_fm = np.ascontiguousarray(ss.transpose(0, 1, 3, 2))
    bm_gate = np.ascontiguousarray(b_mod[:, :, 2, :])
    pre = np.asarray(inp["norm_pre"], np.float32).reshape(2, 3, 8, 128)
    pre_fm = np.ascontiguousarray(pre.transpose(0, 1, 3, 2))
    pos = np.arange(SQ, dtype=np.float32)
    freqs = (np.float32(10000.0) ** (-(np.arange(16, dtype=np.float32) / np.float32(16)))).astype(np.float32)
    ang = (pos[:, None] * freqs[None, :]).astype(np.float32)
    cos = np.cos(ang).astype(np.float32)
    sin = np.sin(ang).astype(np.float32)
    cc = np.concatenate([cos, cos], axis=1).reshape(NT, 128, 32).transpose(1, 0, 2)
    ssn = np.concatenate([-sin, sin], axis=1).reshape(NT, 128, 32).transpose(1, 0, 2)
    tri = (np.arange(128)[:, None] <= np.arange(128)[None, :]).astype(np.float32)
    rb = np.asarray(inp["rel_bias"], np.float32)
    kk = np.arange(128)[:, None]
    qq = np.arange(128)[None, :]
    eb = np.full((3, 8, 128, 2, 2, 128), -30000.0, np.float32)
    for g, dl in enumerate((1, 4, 16)):
        for kb in range(2):
            rel = 128 + qq - (kb * 128 + kk)
            valid = (rel >= 0) & (rel <= 128)
            dist = np.maximum(rel, 0) * dl
            dd = np.maximum(dist, 1).astype(np.float32)
            large = 16 + (np.log(dd / np.float32(16)) / np.float32(math.log(2048 / 16))
                          * np.float32(16)).astype(np.int32)
            large = np.minimum(large, 31)
            bucket = np.where(dist < 16, dist, large)
            for hp in range(8):
                for e in range(2):
                    col = g * 16 + hp * 2 + e
                    eb[g, hp, :, e, kb, :] = np.where(valid, rb[bucket, col], np.float32(-30000.0))
    npost = np.asarray(inp["norm_post"], np.float32).reshape(2, 3, 8, 128)
    modc = np.zeros((128, 6, 48), np.float32)
    for l in range(2):
        for j in range(3):
            si = l * 3 + j
            for part in range(3):
                modc[:, si, part * 8:(part + 1) * 8] = b_mod[l, j, part].reshape(8, 128).T
            modc[:, si, 24:32] = pre[l, j].T
            modc[:, si, 32:40] = npost[l, j].T
    return {
        "mod_consts": modc,
        "rope_cc": np.ascontiguousarray(cc), "rope_ss": np.ascontiguousarray(ssn),
        "tri_mask": tri.astype(ml_dtypes.bfloat16),
        "dil_eb": np.ascontiguousarray(eb.reshape(3, 8, 128, 512)),
        "bm_fm": bm_fm, "bm_gate": bm_gate, "pre_fm": pre_fm,
        "post_g": np.ascontiguousarray(np.asarray(inp["norm_post"], np.float32)),
        "ident": np.eye(128, dtype=np.float32).astype(ml_dtypes.bfloat16),
    }


_PROG_CACHE = {}
TRACE = False


def run_sublayers(inp, xs, sublayers, dbg=None):
    key = (tuple(sublayers), dbg)
    if key not in _PROG_CACHE:
        p = Prog()
        nc = p.build(list(sublayers), dbg=dbg)
        _PROG_CACHE[key] = (p, nc)
    p, nc = _PROG_CACHE[key]
    consts = _host_consts(inp)
    c = np.asarray(inp["c"], np.float32)
    in_maps = []
    for b in range(N_CORES):
        m = {}
        for name in p.inputs:
            if name == "x":
                m[name] = np.ascontiguousarray(xs[b])
            elif name == "cT":
                m[name] = np.ascontiguousarray(c[b].reshape(8, 128).T)
            elif name in consts:
                m[name] = consts[name]
            else:
                m[name] = np.ascontiguousarray(np.asarray(inp[name], np.float32))
        in_maps.append(m)
    if TRACE:
        res = run_bass_kernel_spmd(nc, in_maps, core_ids=list(range(N_CORES)), trace=True)
        print("EXEC_NS", res.exec_time_ns)
    else:
        res = run_bass_kernel_spmd(nc, in_maps, core_ids=list(range(N_CORES)))
    return np.stack([np.asarray(r["y"]) for r in res.results], axis=0)


ALL_SUBLAYERS = [(0, 0), (0, 1), (0, 2), (1, 0), (1, 1), (1, 2)]


def kernel(**inputs):
    xs = np.asarray(inputs["x"], np.float32)
    out = run_sublayers(inputs, xs, ALL_SUBLAYERS)
    return out.astype(np.float32)
```

```python
import math
from contextlib import ExitStack

import numpy as np
import ml_dtypes

import concourse.bass as bass
import concourse.mybir as mybir
from concourse.bass_utils import run_bass_kernel_spmd

F32 = mybir.dt.float32
BF16 = mybir.dt.bfloat16
AF = mybir.ActivationFunctionType
ALU = mybir.AluOpType

ENGS = ("pe", "act", "dve", "pool", "sp")

D = 1024
SQ = 2048
NT = 16
DFF = 2816
NFC = 22
EPS = 1e-6
N_CORES = 8


class DSem:
    def __init__(self, handle, name):
        self.h = handle
        self.name = name
        self.count = 0


class Op:
    __slots__ = ("eng", "fn", "waits", "inc", "cs", "pos", "dsem")


class Sched:
    def __init__(self):
        self.streams = {e: [] for e in ENGS}
        self.ncomp = {e: 0 for e in ENGS}
        self.last_w = {}
        self.readers = {}
        self.waited = {e: {} for e in ENGS}
        self.comp_ops = {e: [] for e in ENGS}
        self.dsems = []
        self.pending = {e: None for e in ENGS}

    def _deps(self, eng, reads, writes):
        need = {}

        def add(d):
            if d is None:
                return
            cs, pos = d
            if need.get(cs, 0) < pos:
                need[cs] = pos

        if self.pending[eng] is not None:
            for cs, pos in self.pending[eng].items():
                add((cs, pos))
            self.pending[eng] = None
        for t in reads:
            add(self.last_w.get(t))
        for t in writes:
            add(self.last_w.get(t))
            for r in self.readers.get(t, ()):
                add(r)
        waits = []
        wd = self.waited[eng]
        for cs, pos in need.items():
            if cs == "pe" and eng == "pe":
                continue
            if wd.get(cs, 0) >= pos:
                continue
            wd[cs] = pos
            waits.append((cs, pos))
            if isinstance(cs, str):
                self.comp_ops[cs][pos - 1].inc = True
        return waits

    def _commit(self, me, reads, writes):
        for t in reads:
            self.readers.setdefault(t, []).append(me)
        for t in writes:
            self.last_w[t] = me
            self.readers[t] = []

    def op(self, eng, fn, reads=(), writes=()):
        o = Op()
        o.eng = eng
        o.fn = fn
        o.waits = self._deps(eng, reads, writes)
        o.inc = False
        o.dsem = None
        self.ncomp[eng] += 1
        o.cs = eng
        o.pos = self.ncomp[eng]
        self.comp_ops[eng].append(o)
        self.streams[eng].append(o)
        self._commit((o.cs, o.pos), reads, writes)
        return o

    def dma(self, eng, dsem, fn, reads=(), writes=()):
        o = Op()
        o.eng = eng
        o.fn = fn
        o.waits = self._deps(eng, reads, writes)
        if dsem.count > 0 and self.waited[eng].get(dsem, 0) < dsem.count:
            self.waited[eng][dsem] = dsem.count
            o.waits.append((dsem, dsem.count))
        o.inc = True
        o.dsem = dsem
        dsem.count += 1
        o.cs = dsem
        o.pos = dsem.count
        self.streams[eng].append(o)
        self._commit((o.cs, o.pos), reads, writes)
        return o

    def barrier(self):
        front = {}
        for e in ENGS:
            if self.ncomp[e] > 0:
                front[e] = self.ncomp[e]
        for d in self.dsems:
            if d.count > 0:
                front[d] = d.count
        for e in ENGS:
            p = dict(front)
            if self.pending[e] is not None:
                for k, v in self.pending[e].items():
                    if p.get(k, 0) < v:
                        p[k] = v
            self.pending[e] = p

    def emit(self, block, sems, final_dsems):
        val = {}
        for e in ENGS:
            c = 0
            for o in self.comp_ops[e]:
                if o.inc:
                    c += 1
                    val[(e, o.pos)] = c

        def wait(engobj, cs, pos):
            if isinstance(cs, str):
                engobj.wait_ge(sems[cs], val[(cs, pos)])
            else:
                engobj.wait_ge(cs.h, 16 * pos)

        def run(e, engobj):
            for o in self.streams[e]:
                for cs, pos in o.waits:
                    wait(engobj, cs, pos)
                ins = o.fn(engobj)
                if o.dsem is not None:
                    ins.then_inc(o.dsem.h, 16)
                elif o.inc:
                    ins.then_inc(sems[e], 1)
            if e == "sp":
                for d in final_dsems:
                    engobj.wait_ge(d.h, 16 * d.count)

        @block.tensor
        def _(eng):
            run("pe", eng)

        @block.scalar
        def _(eng):
            run("act", eng)

        @block.vector
        def _(eng):
            run("dve", eng)

        @block.gpsimd
        def _(eng):
            run("pool", eng)

        @block.sync
        def _(eng):
            run("sp", eng)


class Prog:
    def __init__(self):
        self.nc = bass.Bass("TRN2", target_bir_lowering=False)
        self.S = Sched()
        self.inputs = {}
        self.es = ExitStack()
        self.nds = 0
        self.ds_pool = {}
        self.ds_taken = None

    def din(self, name, shape, dtype=F32):
        if name not in self.inputs:
            self.inputs[name] = self.nc.dram_tensor(name, list(shape), dtype, kind="ExternalInput").ap()
        return self.inputs[name]

    def sb(self, es, name, shape, dtype):
        self.nsb = getattr(self, "nsb", 0) + 1
        return es.enter_context(self.nc.sbuf_tensor(f"{name}_{self.nsb}", list(shape), dtype))

    def dsem(self, name, q="pool"):
        pool = self.ds_pool.setdefault(q, [])
        if pool:
            d = pool.pop()
        else:
            self.nds += 1
            d = DSem(self.es.enter_context(self.nc.semaphore(f"dsem{self.nds}")), name)
            d.q = q
            self.S.dsems.append(d)
        if self.ds_taken is not None:
            self.ds_taken.append(d)
        return d

    def phase_begin(self):
        self.ds_taken = []

    def phase_end(self):
        self.S.barrier()
        for d in self.ds_taken:
            self.ds_pool[d.q].append(d)
        self.ds_taken = None

    def mm(self, out, lhsT, rhs, start, stop, r, w):
        self.S.op("pe", lambda e: e.matmul(out=out, lhsT=lhsT, rhs=rhs, start=start, stop=stop), r, w)

    def tr(self, out, in_, r, w):
        ident = self.ident[:]
        self.S.op("pe", lambda e: e.transpose(out=out, in_=in_, identity=ident), r, w)

    def act(self, out, in_, func, r, w, bias=None, scale=None, accum=None, eng="act"):
        kw = {}
        if bias is not None:
            kw["bias"] = bias
        if scale is not None:
            kw["scale"] = scale
        if accum is not None:
            kw["accum_out"] = accum
        self.S.op("act", lambda e: e.activation(out=out, in_=in_, func=func, **kw), r, w)

    def ts(self, eng, out, in0, s1, s2, op0, op1, r, w):
        if s2 is None:
            self.S.op(eng, lambda e: e.tensor_scalar(out=out, in0=in0, scalar1=s1, scalar2=None, op0=op0), r, w)
        else:
            self.S.op(eng, lambda e: e.tensor_scalar(out=out, in0=in0, scalar1=s1, scalar2=s2, op0=op0, op1=op1),
                      r, w)

    def tt(self, eng, out, in0, in1, op, r, w):
        self.S.op(eng, lambda e: e.tensor_tensor(out=out, in0=in0, in1=in1, op=op), r, w)

    def stt(self, eng, out, in0, scalar, in1, op0, op1, r, w):
        self.S.op(eng, lambda e: e.scalar_tensor_tensor(out=out, in0=in0, scalar=scalar, in1=in1, op0=op0, op1=op1),
                  r, w)

    def cp(self, eng, out, in_, r, w):
        self.S.op(eng, lambda e: e.tensor_copy(out=out, in_=in_), r, w)

    def recip(self, out, in_, r, w):
        self.S.op("dve", lambda e: e.reciprocal(out=out, in_=in_), r, w)

    def memset(self, eng, ap, v, w):
        self.S.op(eng, lambda e: e.memset(ap, v), (), w)

    def dma(self, eng, dsem, out, in_, r, w):
        assert dsem.q == eng, (dsem.name, dsem.q, eng)
        return self.S.dma(eng, dsem, lambda e: e.dma_start(out=out, in_=in_), r, w)

    def build(self, sublayers, final_store=True, dbg=None):
        nc = self.nc
        es = self.es
        P = self
        self.dbg = dbg
        with es:
            x_d = self.din("x", [SQ, D])
            cT_d = self.din("cT", [128, 8])
            id_d = self.din("ident", [128, 128], BF16)
            self.y_d = nc.dram_tensor("y", [SQ, D], F32, kind="ExternalOutput").ap()

            self.x_sb = self.sb(es, "x_sb", [128, NT, D], F32)
            self.ident = self.sb(es, "ident_sb", [128, 128], BF16)
            self.cs = self.sb(es, "cs", [128, 8], F32)
            self.cs_bf = self.sb(es, "cs_bf", [128, 8], BF16)
            self.cs_bc = self.sb(es, "cs_bc", [128, 8, 128], BF16)
            self.A_fm = self.sb(es, "A_fm", [128, 8], F32)
            self.B_fm = self.sb(es, "B_fm", [128, 8], F32)
            self.C = self.sb(es, "C_bc", [128, D], F32)
            self.epsb = self.sb(es, "epsb", [128, 1], F32)
            self.junk = self.sb(es, "junk", [128, 2, D], BF16)
            self.junk_i = 0
            self.xn = self.sb(es, "xn", [128, 2, D], BF16)
            self.st = self.sb(es, "st", [128, 64], F32)
            self.ps = es.enter_context(nc.psum_tensor("ps", [128, 8, 512], F32))
            self.sems = {e: es.enter_context(nc.semaphore(f"s_{e}")) for e in ENGS}
            self.out_ds = [self.dsem(f"out{i}", "sp") for i in range(4)]
            self.modc = self.sb(es, "modc", [128, 6, 48], F32)
            self.modres = self.sb(es, "modres", [128, 6, 24], F32)
            self.wmr = [self.sb(es, f"wmr{i}", [128, 8, 128], BF16) for i in range(2)]
            self.dwm = [self.dsem(f"wmr{i}") for i in range(2)]
            self.ones_f = self.sb(es, "ones_f", [128, 128], F32)
            self.ident_f = self.sb(es, "ident_f", [128, 128], F32)
            self.wm_i = 0
            self.mod_pending = []
            self.mod_loaded = []
            self.mod_step = 0
            self.mod_emitted = {}
            self.psM = None
            self.xn_i = 0
            self.pst_i = 0

            ld = [self.dsem(f"ldx{i}", "sp") for i in range(4)]
            for t in range(NT):
                self.dma("sp", ld[t % 4], self.x_sb[:, t, :], x_d[t * 128:(t + 1) * 128, :], (), [("x", t)])
            dmisc = self.dsem("misc", "sp")
            self.dma("sp", dmisc, self.ident[:], id_d, (), ["ident"])
            self.dma("sp", dmisc, self.cs[:], cT_d, (), ["cs"])
            self.memset("dve", self.epsb[:], EPS, ["eps"])
            self.memset("dve", self.ones_f[:], 1.0, ["ones_f"])
            self.cp("dve", self.ident_f[:], self.ident[:], ["ident"], ["ident_f"])
            self.dma("sp", dmisc, self.modc[:], self.din("mod_consts", [128, 6, 48]), (), ["modc"])
            self.act(self.cs[:], self.cs[:], AF.Silu, ["cs"], ["cs"])
            self.cp("dve", self.cs_bf[:], self.cs[:], ["cs"], ["cs_bf"])
            self.cp("dve", self.cs_bc[:], self.cs[:].unsqueeze(2).to_broadcast([128, 8, 128]), ["cs"], ["cs_bc"])

            for si, (layer, j) in enumerate(sublayers):
                last = final_store and si == len(sublayers) - 1
                if dbg == "load":
                    break
                if si == 0:
                    self.sub_idx = {sl_: i_ for i_, sl_ in enumerate(sublayers)}
                    self.sublist = list(sublayers)
                    self.mod_pending = [(i_, cg) for i_ in range(len(sublayers)) for cg in range(24)]
                self.phase_begin()
                self.mod_apply(si)
                self.phase_end()
                if dbg == "mod":
                    break
                self.phase_begin()
                if j in (0, 2):
                    self.ffn_phase(layer, j // 2, last)
                elif layer % 2 == 0:
                    self.mla_phase(layer // 2, last)
                else:
                    self.dil_phase(layer // 2, last)
                self.phase_end()

            if dbg is not None:
                for t in range(NT):
                    self.dma("sp", self.out_ds[t % 4], self.y_d[t * 128:(t + 1) * 128, :], self.x_sb[:, t, :],
                             [("x", t)], [])
            block = es.enter_context(nc.Block())
            self.S.emit(block, self.sems, self.out_ds)
        return nc

    def mod_unit(self, flush=False):
        w_mod = self.din("w_mod", [2, D, 9 * D])
        ps = self.ps
        loaded = self.mod_loaded
        if self.mod_pending and len(loaded) < 2 and not flush:
            si, cg = self.mod_pending.pop(0)
            layer, j = self.sublist[si]
            wv = w_mod[layer].rearrange("(kc p) f -> p kc f", p=128)
            sl = self.wm_i % 2
            self.wm_i += 1
            c0 = j * 3 * D + cg * 128
            self.dma("pool", self.dwm[sl], self.wmr[sl][:], wv[:, :, c0:c0 + 128], (), [("wmr", sl)])
            loaded.append((si, cg, sl, self.mod_step))
        self.mod_step += 1
        if loaded and (flush or loaded[0][3] < self.mod_step - 1 or not self.mod_pending):
            si, cg, sl, _ = loaded.pop(0)
            layer, j = self.sublist[si]
            bank, col0 = self.psM
            for kc in range(8):
                self.mm(ps[:, bank, col0 + cg:col0 + cg + 1], self.wmr[sl][:, kc, :], self.cs_bf[:, kc:kc + 1],
                        kc == 0, kc == 7, [("wmr", sl), "cs_bf"], [("ps", bank)])
            if cg == 23:
                gsi = layer * 3 + j
                self.cp("dve", self.modres[:, gsi, :], ps[:, bank, col0:col0 + 24], [("ps", bank)],
                        [("modres", gsi)])
                self.mod_emitted[si] = True
        return bool(self.mod_pending or loaded)

    def mod_apply(self, si):
        layer, j = self.sublist[si]
        res_w = 0.5 if j in (0, 2) else 1.0
        ps = self.ps
        if not self.mod_emitted.get(si):
            while self.mod_loaded:
                self.mod_unit(flush=True)
            self.psM = (0, 0)
            while not self.mod_emitted.get(si):
                if self.mod_pending and self.mod_pending[0][0] != si:
                    self.mod_unit(flush=True)
                else:
                    self.mod_unit()
        mr = self.modres
        mc = self.modc
        cf = self.st[:, 56:64]
        si = layer * 3 + j
        self.tt("dve", self.B_fm[:], mr[:, si, 0:8], mc[:, si, 0:8], ALU.add, [("modres", si), "modc"], ["B"])
        self.tt("dve", self.A_fm[:], mr[:, si, 8:16], mc[:, si, 8:16], ALU.add, [("modres", si), "modc"], ["A"])
        self.stt("dve", self.A_fm[:], self.A_fm[:], 1.0, mc[:, si, 24:32], ALU.add, ALU.mult, ["A", "modc"], ["A"])
        self.tt("dve", cf, mr[:, si, 16:24], mc[:, si, 16:24], ALU.add, [("modres", si), "modc"], ["cf"])
        self.stt("dve", cf, cf, float(res_w), mc[:, si, 32:40], ALU.mult, ALU.mult, ["cf", "modc"], ["cf"])
        with ExitStack() as es:
            dg = self.sb(es, "dg", [128, 8, 128], F32)
            for kc in range(8):
                self.ts("dve", dg[:, kc, :], self.ident_f[:], cf[:, kc:kc + 1], None, ALU.mult, None,
                        ["cf", "ident_f"], [("dg", kc)])
                bank = 1 + kc // 4
                self.mm(ps[:, bank, (kc % 4) * 128:(kc % 4 + 1) * 128], self.ones_f[:], dg[:, kc, :], True, True,
                        [("dg", kc), "ones_f"], [("ps", bank)])
            for h in range(2):
                self.cp("dve", self.C[:, h * 512:(h + 1) * 512], ps[:, 1 + h, :], [("ps", 1 + h)], ["C"])
            self.S.barrier()

    def mod_phase(self, layer, j):
        P = self
        w_mod = self.din("w_mod", [2, D, 9 * D])
        bm_fm = self.din("bm_fm", [2, 3, 128, 16])
        bm_gate = self.din("bm_gate", [2, 3, D])
        pre_fm = self.din("pre_fm", [2, 3, 128, 8])
        post_g = self.din("post_g", [2, 3, D])
        res_w = 0.5 if j in (0, 2) else 1.0
        wv = w_mod[layer].rearrange("(kc p) f -> p kc f", p=128)
        ps = self.ps
        with ExitStack() as es:
            wm = [self.sb(es, f"wm{i}", [128, 8, 256], BF16) for i in range(4)]
            pg = self.sb(es, "pg", [128, D], F32)
            bmf = self.sb(es, "bmf", [128, 16], F32)
            pref = self.sb(es, "pref", [128, 8], F32)
            dw = [self.dsem(f"wm{i}") for i in range(4)]
            dv = self.dsem("modv", "sp")
            self.dma("sp", dv, bmf[:], bm_fm[layer, j], (), ["bmf"])
            self.dma("sp", dv, pref[:], pre_fm[layer, j], (), ["pref"])
            self.dma("sp", dv, self.C[:], bm_gate[layer, j].partition_broadcast(128), (), ["C"])
            self.dma("sp", dv, pg[:], post_g[layer, j].partition_broadcast(128), (), ["pg"])
            def ld_wm(ch):
                c0 = j * 3 * D + ch * 256
                self.dma("pool", dw[ch % 4], wm[ch % 4][:], wv[:, :, c0:c0 + 256], (), [("wm", ch % 4)])

            for ch in range(3):
                ld_wm(ch)
            for ch in range(12):
                sl = ch % 4
                if ch + 3 < 12:
                    ld_wm(ch + 3)
                if ch < 8:
                    for sub in range(2):
                        cg = ch * 2 + sub
                        for kc in range(8):
                            self.mm(ps[:, 0, cg:cg + 1], wm[sl][:, kc, sub * 128:(sub + 1) * 128],
                                    self.cs_bf[:, kc:kc + 1], kc == 0, kc == 7,
                                    [("wm", sl), "cs_bf"], [("ps", 0)])
                else:
                    gc = ch - 8
                    bank = 1 + gc // 2
                    for kc in range(8):
                        self.mm(ps[:, bank, (gc % 2) * 256:(gc % 2) * 256 + 256], self.cs_bc[:, kc, :],
                                wm[sl][:, kc, :], kc == 0, kc == 7,
                                [("wm", sl), "cs_bc"], [("ps", bank)])
            self.tt("dve", self.B_fm[:], ps[:, 0, 0:8], bmf[:, 0:8], ALU.add, [("ps", 0), "bmf"], ["B"])
            self.tt("dve", self.A_fm[:], ps[:, 0, 8:16], bmf[:, 8:16], ALU.add, [("ps", 0), "bmf"], ["A"])
            self.stt("dve", self.A_fm[:], self.A_fm[:], 1.0, pref[:], ALU.add, ALU.mult, ["A", "pref"], ["A"])
            for h in range(2):
                self.tt("dve", self.C[:, h * 512:(h + 1) * 512], ps[:, 1 + h, :], self.C[:, h * 512:(h + 1) * 512],
                        ALU.add, [("ps", 1 + h), "C"], ["C"])
            self.stt("dve", self.C[:], self.C[:], float(res_w), pg[:], ALU.mult, ALU.mult, ["C", "pg"], ["C"])
            self.S.barrier()

    def pre_stats(self, tiles, col0=0):
        n = len(tiles)
        st = self.st
        for i, t in enumerate(tiles):
            js = self.junk_i % 2
            self.junk_i += 1
            self.act(self.junk[:, js, :], self.x_sb[:, t, :], AF.Square, [("x", t)],
                     [("st", col0 + i), ("junk", js)], scale=1.0 / 32.0, accum=st[:, col0 + i:col0 + i + 1])
        rd = [("st", col0 + i) for i in range(n)]
        wr = [("st", 32 + col0 + i) for i in range(n)]
        self.act(st[:, col0:col0 + n], st[:, col0:col0 + n], AF.Sqrt, rd + ["eps"], rd, bias=self.epsb[:, 0:1])
        self.recip(st[:, 32 + col0:32 + col0 + n], st[:, col0:col0 + n], rd, wr)

    def pre_tile(self, t, rcol, dst, dst_tok, banks=(6, 7)):
        if self.dbg == "pre_a":
            return
        sl = self.xn_i % 2
        self.xn_i += 1
        xn = self.xn[:, sl, :]
        self.ts("dve", xn, self.x_sb[:, t, :], self.st[:, 32 + rcol:33 + rcol], None, ALU.mult, None,
                [("x", t), ("st", 32 + rcol)], [("xn", sl)])
        if self.dbg == "pre_b":
            return
        bank = banks[self.pst_i % 2]
        self.pst_i += 1
        psb = self.ps[:, bank, :].bitcast(BF16).rearrange("p (k t) -> p k t", k=8)
        for kc in range(8):
            self.tr(psb[:, kc, :], xn[:, kc * 128:(kc + 1) * 128], [("xn", sl), "ident"], [("ps", bank)])
        if self.dbg == "pre_c":
            return
        for kc in range(8):
            if self.dbg == "pre_d" and kc % 2 == 1:
                continue
            if self.dbg == "pre_e" and kc % 2 == 0:
                continue
            if bank % 2 == 0:
                self.act(dst(kc), psb[:, kc, :], AF.Identity, [("ps", bank), "A", "B"], [dst_tok(kc)],
                         scale=self.A_fm[:, kc:kc + 1], bias=self.B_fm[:, kc:kc + 1])
            else:
                self.ts("dve", dst(kc), psb[:, kc, :], self.A_fm[:, kc:kc + 1], self.B_fm[:, kc:kc + 1],
                        ALU.mult, ALU.add, [("ps", bank), "A", "B"], [dst_tok(kc)])

    def epilogue(self, t, b0, scr, scr_tok, par, last):
        ps = self.ps
        st = self.st
        c = 16 + par * 2
        yv = ps[:, b0:b0 + 2, :]
        js = self.junk_i % 2
        self.junk_i += 1
        self.act(self.junk[:, js, :].rearrange("p (a b) -> p a b", a=2), yv, AF.Square,
                 [("ps", b0), ("ps", b0 + 1)], [("st", c), ("junk", js)], scale=1.0 / 32.0, accum=st[:, c:c + 1])
        self.act(st[:, c:c + 1], st[:, c:c + 1], AF.Sqrt, [("st", c), "eps"], [("st", c)], bias=self.epsb[:, 0:1])
        self.recip(st[:, c + 1:c + 2], st[:, c:c + 1], [("st", c)], [("st", c + 1)])
        for h in range(2):
            self.stt("dve", scr(h), ps[:, b0 + h, :], st[:, c + 1:c + 2], self.C[:, h * 512:(h + 1) * 512],
                     ALU.mult, ALU.mult, [("ps", b0 + h), ("st", c + 1), "C"], [scr_tok(h)])
            xs = self.x_sb[:, t, h * 512:(h + 1) * 512]
            self.tt("pool", xs, xs, scr(h), ALU.add, [("x", t), scr_tok(h)], [("x", t)])
        if last:
            self.dma("sp", self.out_ds[t % 4], self.y_d[t * 128:(t + 1) * 128, :], self.x_sb[:, t, :],
                     [("x", t)], [])

    def ffn_phase(self, layer, f, last):
        wg_d = self.din("ffn_w_gate", [2, 2, D, DFF])
        wu_d = self.din("ffn_w_up", [2, 2, D, DFF])
        wd_d = self.din("ffn_w_down", [2, 2, DFF, D])
        wgv = wg_d[layer, f].rearrange("(kc p) f -> p kc f", p=128)
        wuv = wu_d[layer, f].rearrange("(kc p) f -> p kc f", p=128)
        wdv = wd_d[layer, f].rearrange("(fc p) d -> p fc d", p=128)
        ps = self.ps
        NR = 3
        with ExitStack() as es:
            hnT = self.sb(es, "hnT", [128, 8, 1024], BF16)
            hT = self.sb(es, "hT", [128, NFC, 1024], BF16)
            wd = self.sb(es, "wd", [128, NFC, D], BF16)
            wg = [self.sb(es, f"wg{i}", [128, 8, 128], BF16) for i in range(NR)]
            wu = [self.sb(es, f"wu{i}", [128, 8, 128], BF16) for i in range(NR)]
            sg = self.sb(es, "sg", [128, 2, 512], F32)
            dg = [self.dsem(f"wg{i}") for i in range(NR)]
            du = [self.dsem(f"wu{i}") for i in range(NR)]
            dd = [self.dsem(f"wd{i}") for i in range(4)]
            gi = [0]

            def load_chunk(fc):
                sl = fc % NR
                self.dma("pool", dg[sl], wg[sl][:], wgv[:, :, fc * 128:(fc + 1) * 128], (), [("wg", sl)])
                self.dma("pool", du[sl], wu[sl][:], wuv[:, :, fc * 128:(fc + 1) * 128], (), [("wu", sl)])

            def prologue(tb, banks, tls=range(8), stats=True):
                tiles = [tb * 8 + i for i in range(8)]
                if stats:
                    self.pre_stats(tiles[0:4], col0=0)
                    self.pre_stats(tiles[4:8], col0=4)
                for tl in tls:
                    t = tiles[tl]
                    self.pre_tile(t, tl, lambda kc, tl=tl: hnT[:, kc, tl * 128:(tl + 1) * 128],
                                  lambda kc, tl=tl: ("hnT", tl, kc), banks)

            def gateup(tb, after_first=None):
                for fc in range(NFC):
                    if fc + NR - 1 < NFC:
                        load_chunk(fc + NR - 1)
                    self.dma("pool", dd[fc % 4], wd[:, fc, :], wdv[:, fc, :], (), [("wd", fc)])
                    sl = fc % NR
                    for nt in range(2):
                        par = gi[0] % 2
                        gi[0] += 1
                        for kc in range(8):
                            rd_h = [("hnT", nt * 4 + q, kc) for q in range(4)]
                            self.mm(ps[:, par, :], wg[sl][:, kc, :], hnT[:, kc, nt * 512:(nt + 1) * 512],
                                    kc == 0, kc == 7, [("wg", sl)] + rd_h, [("ps", par)])
                        for kc in range(8):
                            rd_h = [("hnT", nt * 4 + q, kc) for q in range(4)]
                            self.mm(ps[:, 2 + par, :], wu[sl][:, kc, :], hnT[:, kc, nt * 512:(nt + 1) * 512],
                                    kc == 0, kc == 7, [("wu", sl)] + rd_h, [("ps", 2 + par)])
                        self.act(sg[:, par, :], ps[:, par, :], AF.Silu, [("ps", par)], [("sg", par)])
                        self.tt("dve", hT[:, fc, nt * 512:(nt + 1) * 512], sg[:, par, :], ps[:, 2 + par, :],
                                ALU.mult, [("sg", par), ("ps", 2 + par)], [("hT", fc, nt)])
                        if after_first is not None and fc == 0 and nt == 0:
                            after_first()

            def down(tb):
                tiles = [tb * 8 + i for i in range(8)]
                for tl, t in enumerate(tiles):
                    par = tl % 2
                    b0 = 4 + 2 * par
                    for h in range(2):
                        for fc in range(NFC):
                            self.mm(ps[:, b0 + h, :], hT[:, fc, tl * 128:(tl + 1) * 128],
                                    wd[:, fc, h * 512:(h + 1) * 512], fc == 0, fc == NFC - 1,
                                    [("hT", fc, tl // 4), ("wd", fc)], [("ps", b0 + h)])
                    self.epilogue(t, b0, lambda h: sg[:, h, :], lambda h: ("sg", h), par, last)

            for fc in range(NR - 1):
                load_chunk(fc)
            prologue(0, (6, 7), tls=range(0, 4))
            gateup(0, after_first=lambda: prologue(0, (6, 7), tls=range(4, 8), stats=False))
            for fc in range(NR - 1):
                load_chunk(fc)
            prologue(1, (0, 1))
            down(0)
            gateup(1)
            down(1)

    def post_tr(self, t, o_tile, oT):
        ps = self.ps
        bank = 6 + (t % 2)
        sl = t % 2
        psb = ps[:, bank, :].bitcast(BF16).rearrange("p (k t) -> p k t", k=8)
        for kc in range(8):
            self.tr(psb[:, kc, :], o_tile(kc), ["o_all"], [("ps", bank)])
        if bank == 6:
            self.act(oT[:, sl, :, :], psb, AF.Identity, [("ps", bank)], [("oT", sl)])
        else:
            self.cp("dve", oT[:, sl, :, :], psb, [("ps", bank)], [("oT", sl)])

    def post_mm(self, t, wo, oT, scr, last):
        ps = self.ps
        sl = t % 2
        par = t % 2
        b0 = 2 * par
        for h in range(2):
            for kc in range(8):
                self.mm(ps[:, b0 + h, :], oT[:, sl, kc, :], wo[:, kc, h * 512:(h + 1) * 512], kc == 0, kc == 7,
                        [("oT", sl), "wo"], [("ps", b0 + h)])
        self.epilogue(t, b0, lambda h: scr[:, h, :], lambda h: ("scr", h), par, last)

    def mla_phase(self, a, last):
        H = 16
        w_in_d = self.din("mla_w_in", [1, D, 672])
        qn_d = self.din("mla_q_norm", [1, 384])
        wq_d = self.din("mla_w_q_up", [1, 384, 1536])
        kvn_d = self.din("mla_kv_norm", [1, 256])
        wkv_d = self.din("mla_w_kv_up", [1, 256, 2048])
        wo_d = self.din("mla_w_o", [1, D, D])
        cc_d = self.din("rope_cc", [128, NT, 32])
        ss_d = self.din("rope_ss", [128, NT, 32])
        mask_d = self.din("tri_mask", [128, 128], BF16)
        ps = self.ps
        st = self.st
        scale = 96.0 ** -0.5
        with ExitStack() as es0:
            cqnT = self.sb(es0, "cqnT", [128, 3, SQ], BF16)
            ckvnT = self.sb(es0, "ckvnT", [128, 2, SQ], BF16)
            krT = self.sb(es0, "krT", [128, SQ], BF16)
            o_sb = self.sb(es0, "o_sb", [128, NT, D], BF16)
            cc = self.sb(es0, "cc", [128, NT, 32], F32)
            ssn = self.sb(es0, "ssn", [128, NT, 32], F32)
            mask = self.sb(es0, "mask", [128, 128], BF16)
            dm = self.dsem("mla_c", "sp")
            self.dma("sp", dm, cc[:], cc_d, (), ["cc"])
            self.dma("sp", dm, ssn[:], ss_d, (), ["ssn"])
            self.dma("sp", dm, mask[:], mask_d, (), ["mask"])
            with ExitStack() as es:
                win = self.sb(es, "win", [128, 8, 672], BF16)
                hnt = self.sb(es, "hnt", [128, 2, 8, 128], BF16)
                latn = self.sb(es, "latn", [128, 2, 768], BF16)
                qg = self.sb(es, "qg", [128, 384], F32)
                kvg = self.sb(es, "kvg", [128, 256], F32)
                rt = self.sb(es, "rt", [128, 2, 64], F32)
                dwi = [self.dsem("win0"), self.dsem("win1")]
                self.dma("pool", dwi[0], win[:, 0:4, :], w_in_d[a].rearrange("(kc p) f -> p kc f", p=128)[:, 0:4, :],
                         (), [("win", 0)])
                self.dma("pool", dwi[1], win[:, 4:8, :], w_in_d[a].rearrange("(kc p) f -> p kc f", p=128)[:, 4:8, :],
                         (), [("win", 1)])
                self.dma("sp", dm, qg[:], qn_d[a].partition_broadcast(128), (), ["qg"])
                self.dma("sp", dm, kvg[:], kvn_d[a].partition_broadcast(128), (), ["kvg"])
                self.memset("dve", latn[:], 0.0, [("latn", 0), ("latn", 1)])
                self.pre_stats(list(range(NT)))
                def stA(t):
                    sl = t % 2
                    self.pre_tile(t, t, lambda kc, sl=sl: hnt[:, sl, kc, :], lambda kc, sl=sl: ("hnt", sl, kc))

                def stB(t):
                    sl = t % 2
                    bA, bB = (0, 1) if sl == 0 else (2, 3)
                    for kc in range(8):
                        self.mm(ps[:, bA, 0:384], hnt[:, sl, kc, :], win[:, kc, 0:384], kc == 0, kc == 7,
                                [("hnt", sl, kc), ("win", kc // 4)], [("ps", bA)])
                    for kc in range(8):
                        self.mm(ps[:, bB, 0:288], hnt[:, sl, kc, :], win[:, kc, 384:672], kc == 0, kc == 7,
                                [("hnt", sl, kc), ("win", kc // 4)], [("ps", bB)])

                def stC(t):
                    sl = t % 2
                    bA, bB = (0, 1) if sl == 0 else (2, 3)
                    c = 20 + sl * 4
                    js = self.junk_i % 2
                    self.junk_i += 1
                    self.act(self.junk[:, js, 0:384], ps[:, bA, 0:384], AF.Square, [("ps", bA)],
                             [("st", c), ("junk", js)], scale=384.0 ** -0.5, accum=st[:, c:c + 1])
                    js = self.junk_i % 2
                    self.junk_i += 1
                    self.act(self.junk[:, js, 0:256], ps[:, bB, 0:256], AF.Square, [("ps", bB)],
                             [("st", c + 1), ("junk", js)], scale=1.0 / 16.0, accum=st[:, c + 1:c + 2])
                    self.act(st[:, c:c + 2], st[:, c:c + 2], AF.Sqrt, [("st", c), ("st", c + 1), "eps"],
                             [("st", c), ("st", c + 1)], bias=self.epsb[:, 0:1])
                    self.recip(st[:, c + 2:c + 4], st[:, c:c + 2], [("st", c), ("st", c + 1)], [("st", c + 2)])
                    self.stt("dve", latn[:, sl, 0:384], ps[:, bA, 0:384], st[:, c + 2:c + 3], qg[:],
                             ALU.mult, ALU.mult, [("ps", bA), ("st", c + 2), "qg"], [("latn", sl)])
                    self.stt("dve", latn[:, sl, 384:640], ps[:, bB, 0:256], st[:, c + 3:c + 4], kvg[:],
                             ALU.mult, ALU.mult, [("ps", bB), ("st", c + 2), "kvg"], [("latn", sl)])
                    kr = ps[:, bB, 256:288]
                    self.tt("dve", rt[:, sl, 0:32], kr, cc[:, t, :], ALU.mult,
                            [("ps", bB), ("st", c + 2), "cc"], [("rt", sl)])
                    self.tt("dve", rt[:, sl, 32:48], ps[:, bB, 272:288], ssn[:, t, 0:16], ALU.mult,
                            [("ps", bB), "ssn"], [("rt", sl)])
                    self.tt("dve", rt[:, sl, 48:64], ps[:, bB, 256:272], ssn[:, t, 16:32], ALU.mult,
                            [("ps", bB), "ssn"], [("rt", sl)])
                    self.tt("dve", latn[:, sl, 704:736], rt[:, sl, 0:32], rt[:, sl, 32:64], ALU.add,
                            [("rt", sl)], [("latn", sl)])
                def stC2(t):
                    sl = t % 2
                    bank = 4 + sl
                    psb = ps[:, bank, :].bitcast(BF16).rearrange("p (k t) -> p k t", k=8)
                    for k6 in range(6):
                        self.tr(psb[:, k6, :], latn[:, sl, k6 * 128:(k6 + 1) * 128], [("latn", sl), "ident"],
                                [("ps", bank)])
                    tc = slice(t * 128, (t + 1) * 128)
                    self.act(cqnT[:, :, tc], psb[:, 0:3, :], AF.Identity, [("ps", bank)], [("cqnT", t)])
                    self.act(ckvnT[:, :, tc], psb[:, 3:5, :], AF.Identity, [("ps", bank)], [("ckvnT", t)])
                    self.act(krT[64:96, tc], psb[64:96, 5, :], AF.Identity, [("ps", bank)], [("krT", t)])

                for step in range(NT + 3):
                    if step < NT:
                        stA(step)
                    if 0 <= step - 1 < NT:
                        stB(step - 1)
                    if 0 <= step - 2 < NT:
                        stC(step - 2)
                    if 0 <= step - 3 < NT:
                        stC2(step - 3)
                self.S.barrier()
            with ExitStack() as es:
                qT = self.sb(es, "qT", [128, 4, SQ], BF16)
                kT = self.sb(es, "kT", [128, 4, SQ], BF16)
                V = self.sb(es, "V", [128, NT, 4, 65], BF16)
                wq = self.sb(es, "wq", [128, 3, 384], BF16)
                wkk = self.sb(es, "wkk", [128, 2, 4, 64], BF16)
                wkv = self.sb(es, "wkv", [128, 2, 4, 64], BF16)
                qb = self.sb(es, "qb", [128, 2, 4, 96], BF16)
                rq = self.sb(es, "rq", [128, 2, 4, 64], F32)
                pT = self.sb(es, "pT", [128, 5, 512], BF16)
                rz = self.sb(es, "rz", [128, 2, 4], F32)
                dq = self.dsem("wq")
                dk = self.dsem("wkk")
                dvv = self.dsem("wkv")
                self.memset("pool", V[:], 1.0, [("V", kt) for kt in range(NT)])
                wqv = wq_d[a].rearrange("(kc p) f -> p kc f", p=128)
                wkvv = wkv_d[a].rearrange("(kc p) (h c) -> p kc h c", p=128, c=128)
                for h4 in range(4):
                    self.cp("pool" if h4 % 2 == 0 else "dve", kT[64:96, h4, :], krT[64:96, :], [], [("kTr", h4)])
                for hg in range(4):
                    self.dma("pool", dq, wq[:], wqv[:, :, hg * 384:(hg + 1) * 384], (), ["wq"])
                    for kc in range(2):
                        self.dma("pool", dk, wkk[:, kc], wkvv[:, kc, hg * 4:(hg + 1) * 4, 0:64], (), ["wkk"])
                        self.dma("pool", dvv, wkv[:, kc], wkvv[:, kc, hg * 4:(hg + 1) * 4, 64:128], (), ["wkv"])
                    def kstep(i):
                        h4, cch = i // 4, i % 4
                        bank = 2 + (cch % 2)
                        for kc in range(2):
                            self.mm(ps[0:64, bank, :], wkk[:, kc, h4, :], ckvnT[:, kc, cch * 512:(cch + 1) * 512],
                                    kc == 0, kc == 1, ["wkk"], [("ps", bank)])
                        self.act(kT[0:64, h4, cch * 512:(cch + 1) * 512], ps[0:64, bank, :], AF.Identity,
                                 [("ps", bank)], [("kTn", h4, cch)])

                    def vstep(kt):
                        bank = 6 + (kt % 2)
                        for kc in range(2):
                            self.mm(ps[:, bank, 0:256], ckvnT[:, kc, kt * 128:(kt + 1) * 128],
                                    wkv[:, kc, :, :].rearrange("p h c -> p (h c)"), kc == 0, kc == 1,
                                    ["wkv"], [("ps", bank)])
                        self.act(V[:, kt, :, 0:64], ps[:, bank, 0:256].rearrange("p (h c) -> p h c", c=64),
                                 AF.Identity, [("ps", bank)], [("V", kt)])

                    def q1(t):
                        sl = t % 2
                        bank = sl
                        for kc in range(3):
                            self.mm(ps[:, bank, 0:384], cqnT[:, kc, t * 128:(t + 1) * 128], wq[:, kc, :],
                                    kc == 0, kc == 2, ["wq"], [("ps", bank)])

                    def q2(t):
                        sl = t % 2
                        bank = sl
                        pq = ps[:, bank, 0:384].rearrange("p (h c) -> p h c", c=96)
                        ccb = cc[:, t, :].unsqueeze(1).to_broadcast([128, 4, 32])
                        s0 = ssn[:, t, 0:16].unsqueeze(1).to_broadcast([128, 4, 16])
                        s1 = ssn[:, t, 16:32].unsqueeze(1).to_broadcast([128, 4, 16])
                        self.cp("dve", qb[:, sl, :, 0:64], pq[:, :, 0:64], [("ps", bank)], [("qb", sl)])
                        self.tt("dve", rq[:, sl, :, 0:32], pq[:, :, 64:96], ccb, ALU.mult, [("ps", bank), "cc"],
                                [("rq", sl)])
                        self.tt("dve", rq[:, sl, :, 32:48], pq[:, :, 80:96], s0, ALU.mult, [("ps", bank), "ssn"],
                                [("rq", sl)])
                        self.tt("dve", rq[:, sl, :, 48:64], pq[:, :, 64:80], s1, ALU.mult, [("ps", bank), "ssn"],
                                [("rq", sl)])
                        self.tt("dve", qb[:, sl, :, 64:96], rq[:, sl, :, 0:32], rq[:, sl, :, 32:64], ALU.add,
                                [("rq", sl)], [("qb", sl)])
                        bk = 4 + sl
                        psb = ps[:, bk, :].bitcast(BF16).rearrange("p (k t) -> p k t", k=8)
                        for h4 in range(4):
                            self.tr(psb[0:96, h4, :], qb[:, sl, h4, :], [("qb", sl), "ident"], [("ps", bk)])
                        if sl == 0:
                            self.act(qT[0:96, :, t * 128:(t + 1) * 128], psb[0:96, 0:4, :], AF.Identity,
                                     [("ps", bk)], [("qT", t)])
                        else:
                            self.cp("dve", qT[0:96, :, t * 128:(t + 1) * 128], psb[0:96, 0:4, :],
                                    [("ps", bk)], [("qT", t)])

                    for step in range(NT + 1):
                        if step < NT:
                            q1(step)
                            kstep(step)
                            vstep(step)
                        if step >= 1:
                            q2(step - 1)
                    items = [(qc, h4, kt) for qc in range(4) for h4 in range(4) for kt in range(qc * 4 + 4)]
                    LA = 4
                    SBK = (0, 1, 2, 6, 7)
                    slots = {}

                    def score_item(i):
                        qc, h4, kt = items[i]
                        jl0 = max(0, kt - qc * 4)
                        n0 = jl0 * 128
                        sl = i % 5
                        slots[i] = sl
                        bk = SBK[sl]
                        diag = kt >= qc * 4
                        self.mm(ps[:, bk, n0:512], kT[0:96, h4, kt * 128:(kt + 1) * 128],
                                qT[0:96, h4, qc * 512 + n0:(qc + 1) * 512], True, not diag,
                                [("kTn", h4, kt // 4), ("kTr", h4)] + [("qT", qc * 4 + j) for j in range(jl0, 4)],
                                [("ps", bk)])
                        if diag:
                            self.mm(ps[:, bk, n0:n0 + 128], self.ident[:], mask[:], False, True,
                                    ["ident", "mask"], [("ps", bk)])
                        self.act(pT[:, sl, n0:512], ps[:, bk, n0:512], AF.Exp, [("ps", bk)], [("pT", sl)],
                                 scale=scale)

                    def pv_item(i):
                        qc, h4, kt = items[i]
                        h = hg * 4 + h4
                        nkt = qc * 4 + 4
                        jl0 = max(0, kt - qc * 4)
                        sl = slots[i]
                        gidx = qc * 4 + h4
                        ob = 3 + (gidx % 2)
                        orz = gidx % 2
                        for jl in range(jl0, 4):
                            self.mm(ps[:, ob, jl * 65:(jl + 1) * 65], pT[:, sl, jl * 128:(jl + 1) * 128],
                                    V[:, kt, h4, :], kt == 0 and jl == 0, kt == nkt - 1 and jl == 3,
                                    [("pT", sl), ("V", kt)], [("ps", ob)])
                        if kt == nkt - 1:
                            po = ps[:, ob, 0:260].rearrange("p (j c) -> p j c", c=65)
                            self.recip(rz[:, orz, :].unsqueeze(2), po[:, :, 64:65], [("ps", ob)], [("rz", orz)])
                            self.tt("dve", o_sb[:, qc * 4:(qc + 1) * 4, h * 64:(h + 1) * 64], po[:, :, 0:64],
                                    rz[:, orz, :].unsqueeze(2).to_broadcast([128, 4, 64]), ALU.mult,
                                    [("ps", ob), ("rz", orz)], [("o", qc, h)])

                    for i in range(min(LA, len(items))):
                        score_item(i)
                    self.psM = (5, 256)
                    for i in range(len(items)):
                        if i + LA < len(items):
                            score_item(i + LA)
                        pv_item(i)
                        if i % 6 == 5:
                            self.mod_unit()
                    if hg == 3:
                        while self.mod_loaded:
                            self.mod_unit(flush=True)
                self.S.barrier()
            with ExitStack() as es:
                wo = self.sb(es, "wo", [128, 8, D], BF16)
                oT = self.sb(es, "oT", [128, 2, 8, 128], BF16)
                scr = self.sb(es, "scr", [128, 2, 512], F32)
                dwo = [self.dsem("wo0"), self.dsem("wo1")]
                wov = wo_d[a].rearrange("(kc p) f -> p kc f", p=128)
                for i in range(2):
                    self.dma("pool", dwo[i], wo[:, i * 4:(i + 1) * 4, :], wov[:, i * 4:(i + 1) * 4, :], (), ["wo"])
                for step in range(NT + 1):
                    if step < NT:
                        self.post_tr(step, lambda kc, t=step: o_sb[:, t, kc * 128:(kc + 1) * 128], oT)
                    if step >= 1:
                        self.post_mm(step - 1, wo, oT, scr, last)
                self.S.barrier()

    def dil_phase(self, b, last):
        DIL = (1, 4, 16)
        w_in_d = self.din("dil_w_in", [1, D, 9 * D])
        wo_d = self.din("dil_w_o", [1, D, D])
        eb_d = self.din("dil_eb", [3, 8, 128, 512])
        ps = self.ps
        wv_all = w_in_d[b].rearrange("(kc p) f -> p kc f", p=128)

        def tok(T, rows, g, r, n):
            dl = DIL[g]
            s0 = r + dl * n * 128
            return T[rows, s0:s0 + dl * 127 + 1:dl]

        with ExitStack() as es0:
            oT = self.sb(es0, "oT", [128, 8, SQ], BF16)
            with ExitStack() as es:
                hnT = self.sb(es, "hnT", [128, 8, SQ], BF16)
                Ut = self.sb(es, "Ut", [128, SQ], F32)
                Zt = self.sb(es, "Zt", [128, SQ], F32)
                qz = [self.sb(es, f"qz{e}", [128, SQ], BF16) for e in range(2)]
                kT = self.sb(es, "kT", [128, SQ], BF16)
                V0 = self.sb(es, "V0", [128, 16, 128], BF16)
                V1 = self.sb(es, "V1", [128, 16, 128], BF16)
                ones0 = self.sb(es, "ones0", [128, 128], BF16)
                ones1 = self.sb(es, "ones1", [128, 128], BF16)
                wr = [[self.sb(es, f"dw{i}_{s}", [128, 8, 128], BF16) for s in range(3)] for i in range(2)]
                eb = self.sb(es, "eb", [128, 2, 512], BF16)
                ebr = self.sb(es, "ebr", [128, 2, 512], F32)
                pT = self.sb(es, "pT", [128, 4, 512], BF16)
                dwq = [[self.dsem(f"dw{i}{s}") for s in range(3)] for i in range(2)]
                deb = [self.dsem("eb0", "sp"), self.dsem("eb1", "sp")]
                self.memset("pool", qz[0][:], 0.0, ["qz"])
                self.memset("pool", qz[1][:], 0.0, ["qz"])
                self.memset("dve", V0[:], 0.0, ["V0"])
                self.memset("dve", V1[:], 0.0, ["V1"])
                self.memset("dve", ones0[:], 0.0, ["ones"])
                self.memset("dve", ones1[:], 0.0, ["ones"])
                self.memset("dve", ones0[:, 0:64], 1.0, ["ones"])
                self.memset("dve", ones1[:, 64:128], 1.0, ["ones"])
                self.pre_stats(list(range(NT)))
                for t in range(NT):
                    self.pre_tile(t, t, lambda kc, t=t: hnT[:, kc, t * 128:(t + 1) * 128],
                                  lambda kc, t=t: ("hnT", t, kc))
                it = 0
                sbi = 0
                ubi = 0

                def ld_w(hp_, g_, wi_):
                    for s_ in range(3):
                        c0 = (g_ * 3 + s_) * D + hp_ * 128
                        self.dma("pool", dwq[wi_][s_], wr[wi_][s_][:], wv_all[:, :, c0:c0 + 128], (),
                                 [("dw", wi_, s_)])
                    self.dma("sp", deb[wi_], ebr[:, wi_, :], eb_d[g_, hp_], (), [("ebr", wi_)])
                    self.act(eb[:, wi_, :], ebr[:, wi_, :], AF.Exp, [("ebr", wi_)], [("eb", wi_)])

                n_it = 24
                dbg = ""
                for hp in range(8):
                    for g in range(3):
                        dl = DIL[g]
                        nb = 16 // dl
                        wi = it % 2
                        it += 1
                        if it == 1:
                            ld_w(hp, g, wi)
                        nxt = hp * 3 + g + 1
                        if nxt < n_it:
                            ld_w(nxt // 3, nxt % 3, 1 - wi)
                        for s in range(2):
                            for cch in range(4):
                                bank = cch % 2
                                for kc in range(8):
                                    self.mm(ps[:, bank, :], wr[wi][s][:, kc, :], hnT[:, kc, cch * 512:(cch + 1) * 512],
                                            kc == 0, kc == 7, [("dw", wi, s)] + [("hnT", cch * 4 + q, kc) for q in range(4)],
                                            [("ps", bank)])
                                cs_ = slice(cch * 512, (cch + 1) * 512)
                                if s == 0:
                                    self.act(qz[0][0:64, cs_], ps[0:64, bank, :], AF.Identity,
                                             [("ps", bank), "qz"], [("qk", 0, cch)], scale=0.125)
                                    self.act(qz[1][64:128, cs_], ps[64:128, bank, :], AF.Identity,
                                             [("ps", bank), "qz"], [("qk", 0, cch)], scale=0.125)
                                else:
                                    self.act(kT[:, cs_], ps[:, bank, :], AF.Identity,
                                             [("ps", bank)], [("qk", 1, cch)])
                        for b4 in range(0 if dbg.startswith("dil_a") else 4):
                            bank = 2 + (b4 % 2)
                            for bb in range(4):
                                bi = b4 * 4 + bb
                                r, n = bi // nb, bi % nb
                                for kc in range(8):
                                    self.mm(ps[:, bank, bb * 128:(bb + 1) * 128], tok(hnT[:, kc, :], slice(0, 128), g, r, n),
                                            wr[wi][2][:, kc, :], kc == 0, kc == 7,
                                            [("dw", wi, 2)] + [("hnT", q, kc) for q in range(NT)], [("ps", bank)])
                            pv = ps[:, bank, :].rearrange("p (b c) -> p b c", c=128)
                            self.act(V0[:, b4 * 4:(b4 + 1) * 4, 0:64], pv[:, :, 0:64], AF.Identity, [("ps", bank)],
                                     [("V", b4)])
                            self.act(V1[:, b4 * 4:(b4 + 1) * 4, 64:128], pv[:, :, 64:128], AF.Identity, [("ps", bank)],
                                     [("V", b4)])

                        SB = (4, 5, 0, 1)

                        def scores(bi):
                            nonlocal sbi
                            r, n = bi // nb, bi % nb
                            sl = sbi % 4
                            sbi += 1
                            bank = SB[sl]
                            kb0 = 0 if n > 0 else 1
                            allr = slice(0, 128)
                            for e in range(2):
                                for kb in range(kb0, 2):
                                    c0 = (e * 2 + kb) * 128
                                    self.mm(ps[:, bank, c0:c0 + 128], tok(kT, allr, g, r, n - 1 + kb),
                                            tok(qz[e], allr, g, r, n), True, True,
                                            [("qk", 0, q) for q in range(4)] + [("qk", 1, q) for q in range(4)],
                                            [("ps", bank)])
                            if kb0 == 0:
                                self.act(pT[:, sl, :], ps[:, bank, :], AF.Exp, [("ps", bank)], [("pT", sl)])
                                self.tt("dve", pT[:, sl, :], pT[:, sl, :], eb[:, wi, :], ALU.mult,
                                        [("pT", sl), ("eb", wi)], [("pT", sl)])
                            else:
                                v4 = lambda a: a.rearrange("p (e k q) -> p e k q", e=2, k=2)[:, :, 1, :]
                                self.act(v4(pT[:, sl, :]), v4(ps[:, bank, :]), AF.Exp, [("ps", bank)], [("pT", sl)])
                                self.tt("dve", v4(pT[:, sl, :]), v4(pT[:, sl, :]), v4(eb[:, wi, :]), ALU.mult,
                                        [("pT", sl), ("eb", wi)], [("pT", sl)])
                            return sl, kb0

                        def pv_block(bi, sl, kb0, ub, zb):
                            n = bi % nb
                            bb = bi % 4
                            seq = [(e, kb) for e in range(2) for kb in range(kb0, 2)]
                            for i, (e, kb) in enumerate(seq):
                                c0 = (e * 2 + kb) * 128
                                Vs = V0 if e == 0 else V1
                                self.mm(ps[:, ub, bb * 128:(bb + 1) * 128], Vs[:, bi - 1 + kb, :],
                                        pT[:, sl, c0:c0 + 128], i == 0, i == len(seq) - 1,
                                        [("pT", sl), ("V", (bi - 1 + kb) // 4)], [("ps", ub)])
                            for i, (e, kb) in enumerate(seq):
                                c0 = (e * 2 + kb) * 128
                                on = ones0 if e == 0 else ones1
                                self.mm(ps[:, zb, bb * 128:(bb + 1) * 128], on[:], pT[:, sl, c0:c0 + 128],
                                        i == 0, i == len(seq) - 1, [("pT", sl), "ones"], [("ps", zb)])

                        def dst_view(T, b4):
                            if g == 0:
                                return T[:, b4 * 512:(b4 + 1) * 512]
                            if g == 1:
                                return T[:, b4:b4 + 4 * 511 + 1:4]
                            return T[:, :].rearrange("p (m r) -> p r m", r=16)[:, b4 * 4:(b4 + 1) * 4, :]

                        if dbg.startswith("dil_a") or dbg.startswith("dil_b"):
                            continue
                        LA = 3
                        pend = [scores(i) for i in range(LA)]
                        for bi in range(16):
                            b4 = bi // 4
                            if bi % 4 == 0:
                                ub = 6 + (ubi % 2)
                                zb = 2 + (ubi % 2)
                                ubi += 1
                            cur = pend.pop(0)
                            if bi + LA < 16:
                                pend.append(scores(bi + LA))
                            if dbg.startswith("dil_c"):
                                continue
                            pv_block(bi, cur[0], cur[1], ub, zb)
                            if dbg.startswith("dil_d"):
                                continue
                            if bi % 4 == 3:
                                pu = ps[:, ub, :]
                                pz = ps[:, zb, :]
                                if g == 2:
                                    pu = pu.rearrange("p (r m) -> p r m", r=4)
                                    pz = pz.rearrange("p (r m) -> p r m", r=4)
                                du = dst_view(Ut, b4)
                                dz = dst_view(Zt, b4)
                                if g == 0:
                                    self.cp("dve", du, pu, [("ps", ub)], [("Ut", b4)])
                                    self.cp("dve", dz, pz, [("ps", zb)], [("Zt", b4)])
                                else:
                                    self.tt("dve", du, pu, du, ALU.add, [("ps", ub)] + [("Ut", q) for q in range(4)],
                                            [("Ut", q) for q in range(4)])
                                    self.tt("dve", dz, pz, dz, ALU.add, [("ps", zb)] + [("Zt", q) for q in range(4)],
                                            [("Zt", q) for q in range(4)])
                    for q in range(4):
                        cs_ = slice(q * 512, (q + 1) * 512)
                        self.recip(Zt[:, cs_], Zt[:, cs_], [("Zt", k) for k in range(4)], [("Zt", k) for k in range(4)])
                        self.tt("dve", oT[:, hp, cs_], Ut[:, cs_], Zt[:, cs_], ALU.mult,
                                [("Ut", k) for k in range(4)] + [("Zt", k) for k in range(4)], [("oT", hp)])
                self.S.barrier()
            with ExitStack() as es:
                wo = self.sb(es, "wo", [128, 8, D], BF16)
                scr = self.sb(es, "scr", [128, 2, 512], F32)
                dwo = [self.dsem("wo0"), self.dsem("wo1")]
                wov = wo_d[b].rearrange("(kc p) f -> p kc f", p=128)
                for i in range(2):
                    self.dma("pool", dwo[i], wo[:, i * 4:(i + 1) * 4, :], wov[:, i * 4:(i + 1) * 4, :], (), ["wo"])
                for t in range(NT):
                    par = t % 2
                    b0 = 2 * par
                    for h in range(2):
                        for kc in range(8):
                            self.mm(ps[:, b0 + h, :], oT[:, kc, t * 128:(t + 1) * 128], wo[:, kc, h * 512:(h + 1) * 512],
                                    kc == 0, kc == 7, ["wo"], [("ps", b0 + h)])
                    self.epilogue(t, b0, lambda h: scr[:, h, :], lambda h: ("scr", h), par, last)
                self.S.barrier()


def _host_consts(inp):
    b_mod = np.asarray(inp["b_mod"], np.float32).reshape(2, 3, 3, D)
    ss = b_mod[:, :, 0:2, :].reshape(2, 3, 16, 128)
    bm_fm = np.ascontiguousarray(ss.transpose(0, 1, 3, 2))
    bm_gate = np.ascontiguousarray(b_mod[:, :, 2, :])
    pre = np.asarray(inp["norm_pre"], np.float32).reshape(2, 3, 8, 128)
    pre_fm = np.ascontiguousarray(pre.transpose(0, 1, 3, 2))
    pos = np.arange(SQ, dtype=np.float32)
    freqs = (np.float32(10000.0) ** (-(np.arange(16, dtype=np.float32) / np.float32(16)))).astype(np.float32)
    ang = (pos[:, None] * freqs[None, :]).astype(np.float32)
    cos = np.cos(ang).astype(np.float32)
    sin = np.sin(ang).astype(np.float32)
    cc = np.concatenate([cos, cos], axis=1).reshape(NT, 128, 32).transpose(1, 0, 2)
    ssn = np.concatenate([-sin, sin], axis=1).reshape(NT, 128, 32).transpose(1, 0, 2)
    tri = np.where(np.arange(128)[:, None] <= np.arange(128)[None, :], 0.0, -30000.0).astype(np.float32)
    rb = np.asarray(inp["rel_bias"], np.float32)
    kk = np.arange(128)[:, None]
    qq = np.arange(128)[None, :]
    eb = np.full((3, 8, 128, 2, 2, 128), -30000.0, np.float32)
    for g, dl in enumerate((1, 4, 16)):
        for kb in range(2):
            rel = 128 + qq - (kb * 128 + kk)
            valid = (rel >= 0) & (rel <= 128)
            dist = np.maximum(rel, 0) * dl
            dd = np.maximum(dist, 1).astype(np.float32)
            large = 16 + (np.log(dd / np.float32(16)) / np.float32(math.log(2048 / 16))
                          * np.float32(16)).astype(np.int32)
            large = np.minimum(large, 31)
            bucket = np.where(dist < 16, dist, large)
            for hp in range(8):
                for e in range(2):
                    col = g * 16 + hp * 2 + e
                    eb[g, hp, :, e, kb, :] = np.where(valid, rb[bucket, col], np.float32(-30000.0))
    npost = np.asarray(inp["norm_post"], np.float32).reshape(2, 3, 8, 128)
    modc = np.zeros((128, 6, 48), np.float32)
    for l in range(2):
        for j in range(3):
            si = l * 3 + j
            for part in range(3):
                modc[:, si, part * 8:(part + 1) * 8] = b_mod[l, j, part].reshape(8, 128).T
            modc[:, si, 24:32] = pre[l, j].T
            modc[:, si, 32:40] = npost[l, j].T
    return {
        "mod_consts": modc,
        "rope_cc": np.ascontiguousarray(cc), "rope_ss": np.ascontiguousarray(ssn),
        "tri_mask": tri.astype(ml_dtypes.bfloat16),
        "dil_eb": np.ascontiguousarray(eb.reshape(3, 8, 128, 512)),
        "bm_fm": bm_fm, "bm_gate": bm_gate, "pre_fm": pre_fm,
        "post_g": np.ascontiguousarray(np.asarray(inp["norm_post"], np.float32)),
        "ident": np.eye(128, dtype=np.float32).astype(ml_dtypes.bfloat16),
    }


_PROG_CACHE = {}
TRACE = False


def run_sublayers(inp, xs, sublayers, dbg=None):
    key = (tuple(sublayers), dbg)
    if key not in _PROG_CACHE:
        p = Prog()
        nc = p.build(list(sublayers), dbg=dbg)
        _PROG_CACHE[key] = (p, nc)
    p, nc = _PROG_CACHE[key]
    consts = _host_consts(inp)
    c = np.asarray(inp["c"], np.float32)
    in_maps = []
    for b in range(N_CORES):
        m = {}
        for name in p.inputs:
            if name == "x":
                m[name] = np.ascontiguousarray(xs[b])
            elif name == "cT":
                m[name] = np.ascontiguousarray(c[b].reshape(8, 128).T)
            elif name in consts:
                m[name] = consts[name]
            else:
                m[name] = np.ascontiguousarray(np.asarray(inp[name], np.float32))
        in_maps.append(m)
    if TRACE:
        res = run_bass_kernel_spmd(nc, in_maps, core_ids=list(range(N_CORES)), trace=True)
        print("EXEC_NS", res.exec_time_ns)
    else:
        res = run_bass_kernel_spmd(nc, in_maps, core_ids=list(range(N_CORES)))
    return np.stack([np.asarray(r["y"]) for r in res.results], axis=0)


ALL_SUBLAYERS = [(0, 0), (0, 1), (0, 2), (1, 0), (1, 1), (1, 2)]


def kernel(**inputs):
    xs = np.asarray(inputs["x"], np.float32)
    out = run_sublayers(inputs, xs, ALL_SUBLAYERS)
    return out.astype(np.float32)
```
